# Optimizing a Trainium2 kernel written in Bass

```python
import math
import jax
import jax.numpy as jnp
from jax import lax
import numpy as np

D_MODEL = 1024
BATCH = 16
SEQ = 256
DEPTH = 4
DEC_BATCH = 2
DEC_SEQ = 4096
PAST_LEN = 256

GRID_W = 64
HEAD_DIM = 64
A_HEADS = 6
A_KV_HEADS = 2
B_HEADS = 4
B_Q_RANK = 256
B_KV_RANK = 128
B_NOPE = 64
B_ROPE = 32
B_V = 64
C_HEADS = 6
C_HEAD_DIM = 64
C_INNER = C_HEADS * C_HEAD_DIM
C_GROUPS = 2
C_STATE = 64
C_CONV = 5
C_CHUNK = 128
C_CONV_CH = C_INNER + 2 * C_GROUPS * C_STATE
D_FF = 4 * D_MODEL
A_Q = A_HEADS * HEAD_DIM
A_KV = A_KV_HEADS * HEAD_DIM
IN_SIZES = (A_Q, A_KV, A_KV, B_Q_RANK, B_KV_RANK, B_ROPE, C_INNER, C_CONV_CH, 2 * C_HEADS)
IN_COLS = A_Q + 2 * A_KV + B_Q_RANK + B_KV_RANK + B_ROPE + C_INNER + C_CONV_CH + 2 * C_HEADS
MIX_W = A_Q + B_HEADS * B_V + C_INNER
Q_BLOCK = 128
ROPE_THETA = 10000.0
EPS = 1e-6
F32 = jnp.float32

kernel_name = 'hybrid_dit_prefix_ctx_step'


def rmsnorm(x, g):
    xf = x.astype(F32)
    y = xf * lax.rsqrt(jnp.mean(xf * xf, axis=-1, keepdims=True) + EPS)
    return (y * g.astype(F32)).astype(x.dtype)


def axial_rope_tables(t, rot_dim):
    rows = t // GRID_W
    row = jnp.repeat(jnp.arange(rows), GRID_W).astype(F32)
    col = jnp.tile(jnp.arange(GRID_W), rows).astype(F32)
    half = rot_dim // 2
    inv = 1.0 / (ROPE_THETA ** (jnp.arange(0, half, 2, dtype=F32) / half))
    ang = jnp.concatenate([row[:, None] * inv, col[:, None] * inv], axis=-1)
    return jnp.cos(ang), jnp.sin(ang)


def apply_axial_rope(x, cos, sin):
    half = x.shape[-1] // 2
    q = half // 2
    cos = cos[None, :, None, :].astype(x.dtype)
    sin = sin[None, :, None, :].astype(x.dtype)

    def rot(xa, cs, sn):
        x1, x2 = xa[..., :q], xa[..., q:]
        return jnp.concatenate([x1 * cs - x2 * sn, x1 * sn + x2 * cs], axis=-1)

    return jnp.concatenate([rot(x[..., :half], cos[..., :q], sin[..., :q]),
                            rot(x[..., half:], cos[..., q:], sin[..., q:])], axis=-1)


def block_attention(q, k, v, scale):
    b, t, h, dk = q.shape
    g = k.shape[2]
    rep = h // g
    dv = v.shape[-1]
    nb = t // Q_BLOCK
    qb = q.reshape(b, nb, Q_BLOCK, g, rep, dk).transpose(1, 0, 2, 3, 4, 5)

    def one_block(qblk):
        s = jnp.einsum('bqgrd,bsgd->bgrqs', qblk, k).astype(F32) * scale
        pr = jax.nn.softmax(s, axis=-1).astype(v.dtype)
        return jnp.einsum('bgrqs,bsgd->bqgrd', pr, v)

    o = lax.map(one_block, qb)
    return o.transpose(1, 0, 2, 3, 4, 5).reshape(b, t, h, dv)


def dwconv_centred(u, w, bias):
    rhs = w.T[:, None, :].astype(u.dtype)
    out = lax.conv_general_dilated(u, rhs, window_strides=(1,),
                                   padding=[(C_CONV // 2, C_CONV // 2)],
                                   dimension_numbers=('NWC', 'WIO', 'NWC'),
                                   feature_group_count=u.shape[-1])
    return out + bias.astype(u.dtype)


def ssd_scan(x, dt, a, bm, cm, h0):
    b, t, nh, hp = x.shape
    nc = t // C_CHUNK
    rep = nh // bm.shape[2]
    bm = jnp.repeat(bm, rep, axis=2)
    cm = jnp.repeat(cm, rep, axis=2)
    xc = x.reshape(b, nc, C_CHUNK, nh, hp)
    bc = bm.reshape(b, nc, C_CHUNK, nh, C_STATE)
    cc = cm.reshape(b, nc, C_CHUNK, nh, C_STATE)
    dtc = dt.reshape(b, nc, C_CHUNK, nh)
    acum = jnp.cumsum(dtc * a.astype(F32), axis=2)
    acum_t = acum.transpose(0, 1, 3, 2)
    seg = acum_t[..., :, None] - acum_t[..., None, :]
    causal = jnp.tril(jnp.ones((C_CHUNK, C_CHUNK), dtype=bool))
    decay = jnp.exp(jnp.where(causal, seg, -jnp.inf))
    scores = jnp.einsum('bclhn,bcshn->bchls', cc, bc) * decay * dtc.transpose(0, 1, 3, 2)[..., None, :]
    y_diag = jnp.einsum('bchls,bcshp->bclhp', scores, xc)
    w_state = jnp.exp(acum[:, :, -1:, :] - acum) * dtc
    states = jnp.einsum('bclhn,bclh,bclhp->bchpn', bc, w_state, xc)
    chunk_decay = jnp.exp(acum[:, :, -1, :])

    def step(hc, inp):
        st, dec = inp
        return hc * dec[..., None, None] + st, hc

    h_final, h_start = lax.scan(step, h0.astype(F32),
                                (states.transpose(1, 0, 2, 3, 4), chunk_decay.transpose(1, 0, 2)))
    h_start = h_start.transpose(1, 0, 2, 3, 4)
    y_off = jnp.einsum('bclhn,bchpn->bclhp', cc, h_start) * jnp.exp(acum)[..., None]
    y = (y_diag + y_off).reshape(b, t, nh, hp).astype(x.dtype)
    return y, h_final.astype(x.dtype)


def ssd_mixer(z, xbc, dt_raw, p, h0):
    b, t, _ = z.shape
    xbc = jax.nn.silu(dwconv_centred(xbc, p['ssm_conv_w'], p['ssm_conv_b']))
    xs, bm, cm = jnp.split(xbc, [C_INNER, C_INNER + C_GROUPS * C_STATE], axis=-1)
    xs = xs.reshape(b, t, C_HEADS, C_HEAD_DIM)
    bm = bm.reshape(b, t, C_GROUPS, C_STATE)
    cm = cm.reshape(b, t, C_GROUPS, C_STATE)
    dt = jax.nn.softplus(dt_raw.astype(F32).reshape(b, t, 2, C_HEADS) + p['ssm_dt_bias'].astype(F32))
    a = -jnp.exp(p['ssm_a_log'].astype(F32))
    y_f, h_f = ssd_scan(xs, dt[:, :, 0], a[0], bm, cm, h0[:, 0])
    y_b, h_b = ssd_scan(jnp.flip(xs, 1), jnp.flip(dt[:, :, 1], 1), a[1],
                        jnp.flip(bm, 1), jnp.flip(cm, 1), h0[:, 1])
    y = y_f + jnp.flip(y_b, 1) + xs * p['ssm_d'][:, None].astype(xs.dtype)
    y = rmsnorm(y.reshape(b, t, C_INNER) * jax.nn.silu(z), p['ssm_norm'])
    return y, jnp.stack([h_f, h_b], axis=1)


def token_mixers(h, p, ctx):
    b, t, _ = h.shape
    proj = h @ p['w_in']
    qa, ka, va, qc, kvc, kpe, z, xbc, dt_raw = jnp.split(
        proj, np.cumsum(IN_SIZES)[:-1].tolist(), axis=-1)
    qa = rmsnorm(qa.reshape(b, t, A_HEADS, HEAD_DIM), p['attn_q_norm'])
    ka = rmsnorm(ka.reshape(b, t, A_KV_HEADS, HEAD_DIM), p['attn_k_norm'])
    va = va.reshape(b, t, A_KV_HEADS, HEAD_DIM)
    qb = (rmsnorm(qc, p['mla_q_norm']) @ p['mla_w_qb']).reshape(b, t, B_HEADS, B_NOPE + B_ROPE)
    q_nope, q_pe = qb[..., :B_NOPE], qb[..., B_NOPE:]
    ckv = rmsnorm(kvc, p['mla_kv_norm'])
    if ctx is None:
        k_all, v_all, ckv_all, kpe_all = ka, va, ckv, kpe
        h0 = jnp.zeros((b, 2, C_HEADS, C_HEAD_DIM, C_STATE), h.dtype)
    else:
        ctx_k, ctx_v, ctx_ckv, ctx_kpe, h0 = ctx
        cos_a, sin_a = axial_rope_tables(t, HEAD_DIM)
        cos_b, sin_b = axial_rope_tables(t, B_ROPE)
        qa = apply_axial_rope(qa, cos_a, sin_a)
        k_all = jnp.concatenate([ctx_k, apply_axial_rope(ka, cos_a, sin_a)], axis=1)
        v_all = jnp.concatenate([ctx_v, va], axis=1)
        q_pe = apply_axial_rope(q_pe, cos_b, sin_b)
        ckv_all = jnp.concatenate([ctx_ckv, ckv], axis=1)
        kpe_all = jnp.concatenate(
            [ctx_kpe, apply_axial_rope(kpe[:, :, None, :], cos_b, sin_b)[:, :, 0, :]], axis=1)
    s = k_all.shape[1]
    o_a = block_attention(qa, k_all, v_all, HEAD_DIM ** -0.5).reshape(b, t, A_Q)
    kv = (ckv_all @ p['mla_w_kvb']).reshape(b, s, B_HEADS, B_NOPE + B_V)
    k_b = jnp.concatenate(
        [kv[..., :B_NOPE], jnp.broadcast_to(kpe_all[:, :, None, :], (b, s, B_HEADS, B_ROPE))], axis=-1)
    q_b = jnp.concatenate([q_nope, q_pe], axis=-1)
    o_b = block_attention(q_b, k_b, kv[..., B_NOPE:], (B_NOPE + B_ROPE) ** -0.5).reshape(b, t, B_HEADS * B_V)
    o_c, h_fin = ssd_mixer(z, xbc, dt_raw, p, h0)
    out = jnp.concatenate([o_a, o_b, o_c], axis=-1) @ p['w_out']
    return out, (ka, va, ckv, kpe, h_fin)


def trunk_layer(x, mod, p, ctx):
    shift1, scale1, gate1, shift2, scale2, gate2 = jnp.split(mod[:, None, :].astype(x.dtype), 6, axis=-1)
    h = rmsnorm(x, p['norm_mix_pre']) * (1 + scale1) + shift1
    mix, ctx_tensors = token_mixers(h, p, ctx)
    x = x + gate1 * rmsnorm(mix, p['norm_mix_post'])
    h = rmsnorm(x, p['norm_ffn_pre']) * (1 + scale2) + shift2
    f = jnp.square(jax.nn.relu(h @ p['w_ffn1'])) @ p['w_ffn2']
    x = x + gate2 * rmsnorm(f, p['norm_ffn_post'])
    return x, ctx_tensors


def setup_inputs(seed: int = 0) -> dict:
    key = jax.random.key(seed)
    ks = jax.random.split(key, 32)

    def nrm(k, shape, s):
        return jax.random.normal(k, shape, F32) * s

    def gain(k, shape):
        return 1.0 + 0.02 * jax.random.normal(k, shape, F32)

    dt0 = jnp.exp(jax.random.uniform(ks[24], (DEPTH, 2, C_HEADS), F32, math.log(1e-3), math.log(1e-1)))
    return {
        'x_prompt': nrm(ks[0], (BATCH, SEQ, D_MODEL), 1.0),
        'x_sample': nrm(ks[1], (DEC_BATCH, DEC_SEQ, D_MODEL), 1.0),
        'cache_attn_k': nrm(ks[2], (DEC_BATCH, DEPTH, PAST_LEN, A_KV_HEADS, HEAD_DIM), 1.0),
        'cache_attn_v': nrm(ks[3], (DEC_BATCH, DEPTH, PAST_LEN, A_KV_HEADS, HEAD_DIM), 1.0),
        'cache_mla_ckv': nrm(ks[4], (DEC_BATCH, DEPTH, PAST_LEN, B_KV_RANK), 1.0),
        'cache_mla_kpe': nrm(ks[5], (DEC_BATCH, DEPTH, PAST_LEN, B_ROPE), 1.0),
        'state_ssm': nrm(ks[6], (DEC_BATCH, DEPTH, 2, C_HEADS, C_HEAD_DIM, C_STATE), 0.5),
        'c': nrm(ks[7], (DEC_BATCH, D_MODEL), 1.0),
        'c_ctx': nrm(ks[8], (D_MODEL,), 1.0),
        'norm_mix_pre': gain(ks[9], (DEPTH, D_MODEL)),
        'norm_mix_post': gain(ks[10], (DEPTH, D_MODEL)),
        'norm_ffn_pre': gain(ks[11], (DEPTH, D_MODEL)),
        'norm_ffn_post': gain(ks[12], (DEPTH, D_MODEL)),
        'w_mod': nrm(ks[13], (DEPTH, D_MODEL, 6 * D_MODEL), 0.5 * D_MODEL ** -0.5),
        'b_mod': nrm(ks[14], (DEPTH, 6 * D_MODEL), 0.01),
        'w_in': nrm(ks[15], (DEPTH, D_MODEL, IN_COLS), D_MODEL ** -0.5),
        'attn_q_norm': gain(ks[16], (DEPTH, HEAD_DIM)),
        'attn_k_norm': gain(ks[17], (DEPTH, HEAD_DIM)),
        'mla_q_norm': gain(ks[18], (DEPTH, B_Q_RANK)),
        'mla_w_qb': nrm(ks[19], (DEPTH, B_Q_RANK, B_HEADS * (B_NOPE + B_ROPE)), B_Q_RANK ** -0.5),
        'mla_kv_norm': gain(ks[20], (DEPTH, B_KV_RANK)),
        'mla_w_kvb': nrm(ks[21], (DEPTH, B_KV_RANK, B_HEADS * (B_NOPE + B_V)), B_KV_RANK ** -0.5),
        'ssm_conv_w': nrm(ks[22], (DEPTH, C_CONV_CH, C_CONV), C_CONV ** -0.5),
        'ssm_conv_b': nrm(ks[23], (DEPTH, C_CONV_CH), 0.01),
        'ssm_dt_bias': dt0 + jnp.log(-jnp.expm1(-dt0)),
        'ssm_a_log': jnp.log(jax.random.uniform(ks[25], (DEPTH, 2, C_HEADS), F32, 1.0, 16.0)),
        'ssm_d': 1.0 + 0.1 * jax.random.normal(ks[26], (DEPTH, C_HEADS), F32),
        'ssm_norm': gain(ks[27], (DEPTH, C_INNER)),
        'w_out': nrm(ks[28], (DEPTH, MIX_W, D_MODEL), MIX_W ** -0.5),
        'w_ffn1': nrm(ks[29], (DEPTH, D_MODEL, D_FF), D_MODEL ** -0.5),
        'w_ffn2': nrm(ks[30], (DEPTH, D_FF, D_MODEL), D_FF ** -0.5),
    }


def reference(x_prompt, x_sample, cache_attn_k, cache_attn_v, cache_mla_ckv, cache_mla_kpe, state_ssm,
              c, c_ctx, norm_mix_pre, norm_mix_post, norm_ffn_pre, norm_ffn_post, w_mod, b_mod, w_in,
              attn_q_norm, attn_k_norm, mla_q_norm, mla_w_qb, mla_kv_norm, mla_w_kvb,
              ssm_conv_w, ssm_conv_b, ssm_dt_bias, ssm_a_log, ssm_d, ssm_norm, w_out, w_ffn1, w_ffn2):
    xp = x_prompt
    xs = x_sample
    new_k, new_v, new_ckv, new_kpe, new_ssm = [], [], [], [], []
    for l in range(DEPTH):
        p = {
            'norm_mix_pre': norm_mix_pre[l], 'norm_mix_post': norm_mix_post[l],
            'norm_ffn_pre': norm_ffn_pre[l], 'norm_ffn_post': norm_ffn_post[l],
            'w_in': w_in[l], 'attn_q_norm': attn_q_norm[l], 'attn_k_norm': attn_k_norm[l],
            'mla_q_norm': mla_q_norm[l], 'mla_w_qb': mla_w_qb[l],
            'mla_kv_norm': mla_kv_norm[l], 'mla_w_kvb': mla_w_kvb[l],
            'ssm_conv_w': ssm_conv_w[l], 'ssm_conv_b': ssm_conv_b[l], 'ssm_dt_bias': ssm_dt_bias[l],
            'ssm_a_log': ssm_a_log[l], 'ssm_d': ssm_d[l], 'ssm_norm': ssm_norm[l],
            'w_out': w_out[l], 'w_ffn1': w_ffn1[l], 'w_ffn2': w_ffn2[l],
        }
        mod_ctx = (jax.nn.silu(c_ctx) @ w_mod[l] + b_mod[l])[None, :]
        mod_lat = jax.nn.silu(c) @ w_mod[l] + b_mod[l]
        xp, (k_a, v_a, ckv, kpe, h_ssm) = trunk_layer(xp, mod_ctx, p, None)
        new_k.append(k_a)
        new_v.append(v_a)
        new_ckv.append(ckv)
        new_kpe.append(kpe)
        new_ssm.append(h_ssm)
        xs, _ = trunk_layer(xs, mod_lat, p, (cache_attn_k[:, l], cache_attn_v[:, l], cache_mla_ckv[:, l],
                                             cache_mla_kpe[:, l], state_ssm[:, l]))
    return (xp, xs, jnp.stack(new_k, axis=1), jnp.stack(new_v, axis=1), jnp.stack(new_ckv, axis=1),
            jnp.stack(new_kpe, axis=1), jnp.stack(new_ssm, axis=1))
```

```python
import numpy as np
from contextlib import ExitStack
import concourse.bass as bass
import concourse.mybir as mybir
from concourse.bass_utils import run_bass_kernel_spmd

F32 = mybir.dt.float32
BF16 = mybir.dt.bfloat16
ALU = mybir.AluOpType
AF = mybir.ActivationFunctionType

DEPTH = 4
D = 1024
T = 1536
EPS = 1e-6
NV = 115
NBC = 792
G1W = 4128
G2W = 392
STOP = None
SEM_LIMIT = 2000
NDMA = 12
DEBUG = {}
RUN_DEPTH = DEPTH


class StopBuild(Exception):
    pass


class Sem:
    __slots__ = ("i", "h")

    def __init__(s, i, h):
        s.i = i
        s.h = h


class Tok:
    __slots__ = ("w", "r")

    def __init__(s):
        s.w = None
        s.r = {}


def toks(n):
    return [Tok() for _ in range(n)]


class KB:
    def __init__(s, nc):
        s.nc = nc
        s.es = ExitStack()
        s.E = dict(pe=nc.tensor, act=nc.scalar, dve=nc.vector, pool=nc.gpsimd, sp=nc.sync)
        s.nsem = 0
        s.cur = {}
        s.known = {e: {} for e in s.E}
        s.retired = []
        for e in s.E:
            s.cur[e] = [s.newsem(), 0]
        s.dq = {q: [[s.newsem(), 0] for _ in range(NDMA)] for q in ("sp", "pool")}
        s.dqi = {"sp": 0, "pool": 0}
        s.cc = [s.newsem(), 0]
        s.nins = {e: 0 for e in s.E}

    def newsem(s):
        s.nsem += 1
        return Sem(s.nsem, s.es.enter_context(s.nc.semaphore(f"s{s.nsem}")))

    def _wait(s, eng, deps):
        kn = s.known[eng]
        best = {}
        own = s.cur[eng][0]
        for (sm, v) in deps:
            if eng == "pe" and sm is own:
                continue
            if v > kn.get(sm.i, 0) and v > best.get(sm.i, (None, 0))[1]:
                best[sm.i] = (sm, v)
        for sm, v in best.values():
            s.E[eng].wait_ge(sm.h, v)
            kn[sm.i] = v

    def _deps(s, eng, r, w):
        deps = []
        own = s.cur[eng][0]
        for t in r:
            if t.w:
                deps.append(t.w)
        for t in w:
            if t.w:
                deps.append(t.w)
            for ev in t.r.values():
                if ev[0] is not own or eng != "pe":
                    deps.append(ev)
        return deps

    def _mark(s, ev, r, w):
        for t in r:
            t.r[ev[0].i] = ev
        for t in w:
            t.w = ev
            t.r = {}

    def op(s, eng, fn, r=(), w=()):
        s._wait(eng, s._deps(eng, r, w))
        ins = fn(s.E[eng])
        c = s.cur[eng]
        c[1] += 1
        ins.then_inc(c[0].h, 1)
        s.nins[eng] += 1
        s._mark((c[0], c[1]), r, w)
        if c[1] >= SEM_LIMIT:
            s.retired.append((c[0], c[1]))
            s.cur[eng] = [s.newsem(), 0]

    def dma(s, q, out, in_, r=(), w=()):
        deps = s._deps(q, r, w)
        slot = s.dq[q][s.dqi[q] % NDMA]
        s.dqi[q] += 1
        if slot[1] > 0:
            deps.append((slot[0], slot[1]))
        s._wait(q, deps)
        ins = s.E[q].dma_start(out=out, in_=in_)
        slot[1] += 16
        ins.then_inc(slot[0].h, 16)
        s.nins[q] += 1
        s._mark((slot[0], slot[1]), r, w)

    def allgather(s, gin, gout, r=(), w=(), groups=None):
        s._wait("pool", s._deps("pool", r, w))
        ins = s.nc.gpsimd.collective_compute(
            "AllGather", ALU.bypass, replica_groups=groups or [[0, 1, 2, 3], [4, 5, 6, 7]],
            ins=[gin], outs=[gout])
        s.cc[1] += 1
        ins.then_inc(s.cc[0].h, 1)
        s._mark((s.cc[0], s.cc[1]), r, w)

    def barrier(s, final=False):
        evs = list(s.retired)
        for e in s.E:
            if s.cur[e][1] > 0:
                evs.append((s.cur[e][0], s.cur[e][1]))
        for q in s.dq:
            for sl in s.dq[q]:
                if sl[1] > 0:
                    evs.append((sl[0], sl[1]))
        if final and s.cc[1] > 0:
            evs.append((s.cc[0], s.cc[1]))
        for e in s.E:
            s._wait(e, evs)


def build_program(depth=DEPTH):
    nc = bass.Bass("TRN2", target_bir_lowering=False)
    kb = KB(nc)
    es = kb.es

    def din(name, shape, dt=F32):
        return nc.dram_tensor(name, list(shape), dt, kind="ExternalInput").ap()

    def dout(name, shape, dt=F32):
        return nc.dram_tensor(name, list(shape), dt, kind="ExternalOutput").ap()

    xT_d = din("xT", [D, T])
    cT_d = din("cT", [128, 32])
    vec_d = din("vec", [128, DEPTH * NV])
    bc_d = din("bc", [DEPTH, 128, NBC])
    cst_d = din("cst", [128, 9 * 128])
    rope_d = din("rope", [4, 128, 1024])
    msk_d = din("msk", [128, 18])
    ctxK_d = din("ctxK", [DEPTH, 128, 256])
    ctxV_d = din("ctxV", [DEPTH, 256, 128])
    ctxC_d = din("ctxC", [DEPTH, 128, 256])
    ctxP_d = din("ctxP", [DEPTH, 128, 256])
    h0_d = din("h0", [DEPTH, 2, 128, 192])
    wmod_d = din("w_mod", [depth, D, 1536])
    win_d = din("w_in", [depth, D, 2092])
    wqb_d = din("w_qb", [depth, 256, 384])
    wkvb_d = din("w_kvb", [depth, 128, 512])
    wout_d = din("w_out", [depth, D, D])
    w1_d = din("w_ffn1", [depth, D, 4 * D])
    w2_d = din("w_ffn2", [depth, 4 * D, D])

    yT_d = dout("yT", [D, T])
    nkT_d = dout("nkT", [DEPTH, 128, 512])
    nv_d = dout("nv", [DEPTH, 512, 128])
    nckvT_d = dout("nckvT", [DEPTH, 128, 512])
    nkpeT_d = dout("nkpeT", [DEPTH, 32, 512])
    nssm_d = dout("nssm", [DEPTH, 2, 2, 128, 192])

    g1i = [nc.dram_tensor(f"gin1{k}", [128, w_], F32).ap() for k, w_ in enumerate((2048, 2048, 32))]
    g1o = [nc.dram_tensor(f"gout1{k}", [512, w_], F32).ap() for k, w_ in enumerate((2048, 2048, 32))]

    class _G1:
        def __init__(s, lst, out):
            s.lst = lst
            s.out = out

        def __getitem__(s, key):
            if s.out:
                p, r_, c = key
            else:
                p, c = key
            k = c.start // 2048
            c2 = slice(c.start - 2048 * k, c.stop - 2048 * k)
            if s.out:
                return s.lst[k].rearrange("(r p) c -> p r c", p=128)[p, r_, c2]
            return s.lst[k][p, c2]
    gin1 = _G1(g1i, False)
    gv1 = _G1(g1o, True)
    gmi = nc.dram_tensor("gmi", [128, 192], F32).ap()
    gmo = nc.dram_tensor("gmo", [512, 192], F32).ap()
    gin2 = nc.dram_tensor("gin2", [128, G2W], F32).ap()
    gout2 = nc.dram_tensor("gout2", [512, G2W], F32).ap()
    gin1_t, gout1_t, gin2_t, gout2_t = toks(4)
    gin_t = toks(3)
    gout_t = toks(3)

    uid = [0]

    def sb(stack, name, shape, dt=F32):
        uid[0] += 1
        return stack.enter_context(nc.sbuf_tensor(f"sb{uid[0]}_{name}", list(shape), dt))

    PS = {}
    for pool, n in (("acc", 2), ("S", 3), ("gen", 2)):
        PS[pool] = [[es.enter_context(nc.psum_tensor(f"ps_{pool}{i}", [128, 512], F32)), Tok()] for i in range(n)]
    PS["big"] = PS["acc"] + PS["S"]
    psi = {"acc": 0, "S": 0, "gen": 0, "bf": 0, "big": 0}

    def psum(pool="gen"):
        lst = PS[pool]
        p = lst[psi[pool] % len(lst)]
        psi[pool] += 1
        return p[0], p[1]

    def mm(out, lhsT, rhs, start, stop, r, w):
        kb.op("pe", lambda e: e.matmul(out, lhsT, rhs, start=start, stop=stop), r=r, w=w)

    def act(out, in_, func, r, w, **kw):
        kb.op("act", lambda e: e.activation(out=out, in_=in_, func=func, **kw), r=r, w=w)

    def tt(out, in0, in1, op, r, w):
        kb.op("dve", lambda e: e.tensor_tensor(out=out, in0=in0, in1=in1, op=op), r=r, w=w)

    def stt(out, in0, scalar, in1, op0, op1, r, w):
        kb.op("dve", lambda e: e.scalar_tensor_tensor(out=out, in0=in0, scalar=scalar, in1=in1, op0=op0, op1=op1), r=r, w=w)

    def vcopy(out, in_, r, w):
        kb.op("dve", lambda e: e.tensor_copy(out=out, in_=in_), r=r, w=w)

    def acopy(out, in_, r, w):
        kb.op("act", lambda e: e.activation(out=out, in_=in_, func=AF.Copy), r=r, w=w)

    P = ExitStack()
    es.enter_context(P)
    xT = sb(P, "xT", [128, 8, T])
    x_t = [toks(6) for _ in range(8)]

    def xtk(c, a, n):
        return [x_t[c][b] for b in range(a // 256, (a + n + 255) // 256)]

    CF = sb(P, "CF", [128, 9, 128])
    CB = sb(P, "CB", [128, 3, 128], BF16)
    VEC = sb(P, "VEC", [128, DEPTH, NV])
    MSK = sb(P, "MSK", [128, 18])
    cT = sb(P, "cT", [128, 8, 4])
    MOD = sb(P, "MOD", [128, DEPTH, 48, 2])
    DER = sb(P, "DER", [128, DEPTH, 4, 8, 2])
    cst_t, vec_t, msk_t, c_t, mod_t, der_t = toks(6)
    IDENT, U_, UB, SL, SLB, ONES, PERMA, PERMB, BLK = [CF[:, i, :] for i in range(9)]
    ONESB, BLKB, IDENTB = [CB[:, i, :] for i in range(3)]

    for c in range(8):
        kb.dma("sp", xT[:, c, :], xT_d[c * 128:(c + 1) * 128, :], w=x_t[c])
    kb.dma("sp", CF[:], cst_d.rearrange("p (a b) -> p a b", a=9), w=[cst_t])
    kb.dma("sp", VEC[:], vec_d.rearrange("p (a b) -> p a b", a=DEPTH), w=[vec_t])
    kb.dma("sp", MSK[:], msk_d, w=[msk_t])
    kb.dma("sp", cT[:], cT_d.rearrange("p (a b) -> p a b", a=8), w=[c_t])
    vcopy(CB[:, 0, :], ONES, [cst_t], [cst_t])
    vcopy(CB[:, 1, :], BLK, [cst_t], [cst_t])
    vcopy(CB[:, 2, :], IDENT, [cst_t], [cst_t])
    act(cT[:], cT[:], AF.Silu, [c_t], [c_t])

    with ExitStack() as S0:
        wm = [sb(S0, f"wm{i}", [128, 8, 384]) for i in range(2)]
        wm_t = toks(2)
        GMs = sb(S0, "GMs", [128, 192])
        GM = sb(S0, "GM", [128, 4, 192])
        TM0 = sb(S0, "TM0", [128, 4, 12])
        gms_t, gm_t, gmi_t, gmo_t, tm0_t = toks(5)
        ps, pt = psum("gen")
        for l in range(depth):
            for pc in range(4):
                b = pc % 2
                kb.dma("sp", wm[b][:], wmod_d[l].rearrange("(c p) n -> p c n", p=128)[:, :, pc * 384:(pc + 1) * 384], w=[wm_t[b]])
                for j in range(3):
                    ch = l * 12 + pc * 3 + j
                    for k in range(8):
                        mm(ps[:, ch * 4:ch * 4 + 4], wm[b][:, k, j * 128:(j + 1) * 128], cT[:, k, :], k == 0, k == 7,
                           [wm_t[b], c_t], [pt])
        vcopy(GMs[:, 0:48 * depth], ps[:, 0:48 * depth], [pt], [gms_t])
        kb.dma("sp", gmi[:, 0:48 * depth], GMs[:, 0:48 * depth], r=[gms_t], w=[gmi_t])
        kb.allgather(gmi, gmo, r=[gmi_t], w=[gmo_t])
        kb.dma("sp", GM[:], gmo.rearrange("(r p) c -> p r c", p=128), r=[gmo_t], w=[gm_t])
        for l in range(depth):
            gl = GM[:, :, 48 * l:48 * l + 48].rearrange("p r (j v) -> p r j v", v=4)
            bm = VEC[:, l, 32:80].rearrange("p (r j) -> p r j", r=4)
            mo = MOD[:, l].rearrange("p (r j) g -> p r j g", r=4)
            tt(mo[:, :, :, 0], gl[:, :, :, 0], bm, ALU.add, [gm_t, vec_t], [mod_t])
            kb.op("dve", lambda e: e.tensor_scalar_mul(out=TM0[:], in0=gl[:, :, :, 1], scalar1=MSK[:, 16:17]), r=[gm_t, msk_t], w=[tm0_t])
            stt(TM0[:], gl[:, :, :, 2], MSK[:, 17:18], TM0[:], ALU.mult, ALU.add, [gm_t, msk_t, tm0_t], [tm0_t])
            tt(mo[:, :, :, 1], TM0[:], bm, ALU.add, [tm0_t, vec_t, mod_t], [mod_t])
            for i, (gcol, mch, plus1) in enumerate(((0, 8, True), (8, 16, False), (16, 32, True), (24, 40, False))):
                gb = VEC[:, l, gcol:gcol + 8].unsqueeze(2).broadcast_to([128, 8, 2])
                if plus1:
                    stt(DER[:, l, i], MOD[:, l, mch:mch + 8, :], 1.0, gb, ALU.add, ALU.mult, [mod_t, vec_t], [der_t])
                else:
                    tt(DER[:, l, i], MOD[:, l, mch:mch + 8, :], gb, ALU.mult, [mod_t, vec_t], [der_t])
        kb.barrier()

    OPEN = []

    def ckpt(name):
        if STOP == name:
            raise StopBuild()

    class Rot:
        def __init__(s, stack, name, shape, dt, n):
            s.b = [sb(stack, f"{name}{i}", shape, dt) for i in range(n)]
            s.t = toks(n)
            s.i = 0

        def get(s):
            j = s.i % len(s.b)
            s.i += 1
            return s.b[j], s.t[j]

    def stats(srcs, n, ones_b, inv, sq, rs_out, rs_tok):
        ps, pt = psum("gen")
        for i, (ap, tk) in enumerate(srcs):
            q, qt = sq.get()
            act(q[:, :n], ap, AF.Square, tk, [qt])
            mm(ps[:, :n], ones_b, q[:, :n], i == 0, i == len(srcs) - 1, [qt, cst_t], [pt])
        act(rs_out, ps[:, :n], AF.Ln, [pt], [rs_tok], scale=inv, bias=EPS)
        act(rs_out, rs_out, AF.Exp, [rs_tok], [rs_tok], scale=-0.5)

    def norm_mod(l, di_scale, shift_ch, mg, a0, n, hT, h_tok, sq, tmpf, rsb):
        rs, rst = rsb.get()
        stats([(xT[:, c, a0:a0 + n], xtk(c, a0, n)) for c in range(8)], n, ONESB, 1.0 / D, sq, rs[:, :n], rst)
        for c in range(8):
            tf, tft = tmpf.get()
            stt(tf[:, :n], xT[:, c, a0:a0 + n], DER[:, l, di_scale, c, mg:mg + 1], rs[:, :n], ALU.mult, ALU.mult,
                xtk(c, a0, n) + [rst, der_t], [tft])
            act(hT[:, c, :n], tf[:, :n], AF.Identity, [tft, mod_t], [h_tok], bias=MOD[:, l, shift_ch + c, mg:mg + 1], scale=1.0)

    def resid_add(l, di_gate, mg, a0, n, src, src_tok, sq, tmpf, rsb):
        rs, rst = rsb.get()
        stats([(src[:, c, :n], [src_tok]) for c in range(8)], n, ONESB, 1.0 / D, sq, rs[:, :n], rst)
        for c in range(8):
            tf, tft = tmpf.get()
            stt(tf[:, :n], src[:, c, :n], DER[:, l, di_gate, c, mg:mg + 1], rs[:, :n], ALU.mult, ALU.mult,
                [src_tok, rst, der_t], [tft])
            xs_, tfs_ = xT[:, c, a0:a0 + n], tf[:, :n]
            kb.op("pool", lambda e: e.tensor_tensor(out=xs_, in0=xs_, in1=tfs_, op=ALU.add), r=xtk(c, a0, n) + [tft], w=xtk(c, a0, n))

    def mixer_pass(l, sample):
        t0 = 512 if sample else 0
        TG = 1024 if sample else 512
        ntile = TG // 128
        mg = 1 if sample else 0
        Tp = 1028 if sample else 520
        W = Tp - 4
        NK = 4352 if sample else 512
        NKT = NK // 128

        def acol(i):
            return i * 128 if sample else (i // 2) * 260 + (i % 2) * 128

        ckpt("P0")
        L = ExitStack()
        OPEN.append(L)
        mixT = sb(L, "mixT", [128, 8, TG], BF16)
        mix_t = toks(ntile)
        QaT = sb(L, "QaT", [128, 3, TG], BF16)
        QbT = sb(L, "QbT", [128, 4, TG], BF16)
        q_t = toks(3)
        BCt = sb(L, "BCt", [128, NBC])
        bc_t = Tok()
        kb.dma("sp", BCt[:], bc_d[l], w=[bc_t])
        key_t = Tok()
        KK = {}

        def alloc_keys(stack):
            KK["KaT"] = sb(stack, "KaT", [128, NK], BF16)
            KK["Va"] = sb(stack, "Va", [128, NKT, 2, 65], BF16)
            KK["KbT"] = sb(stack, "KbT", [128, 4, NK], BF16)
            KK["Vb"] = sb(stack, "Vb", [128, NKT, 4, 65], BF16)
            kb.op("dve", lambda e: e.memset(KK["Va"][:, :, :, 64:65], 1.0), w=[key_t])
            kb.op("dve", lambda e: e.memset(KK["Vb"][:, :, :, 64:65], 1.0), w=[key_t])

        if not sample:
            alloc_keys(L)

        M = ExitStack()
        OPEN.append(M)
        xpad = sb(M, "xpad", [128, 5, Tp])
        xpad_t = Tok()
        szTM = sb(M, "szTM", [128, ntile, 384], BF16)
        dtTM = sb(M, "dtTM", [128, ntile, 12])
        dA = sb(M, "dA", [128, ntile, 12])
        sz_t, dt_t = toks(2)

        with ExitStack() as A:
            Win = sb(A, "Win", [128, 8, 2092], BF16)
            Wkp = sb(A, "Wkp", [128, 8, 128], BF16)
            Wqb = sb(A, "Wqb", [128, 2, 384], BF16)
            Wkvb = sb(A, "Wkvb", [128, 512], BF16)
            WkvV = sb(A, "WkvV", [128, 256], BF16)
            Wqa = sb(A, "Wqa", [128, 8, 3, 128], BF16)
            w_t = []

            def wtk():
                w_t.append(Tok())
                return [w_t[-1]]
            wv = win_d[l].rearrange("(c p) n -> p c n", p=128)
            for j in range(3):
                for a_ in range(2):
                    hh = j + 3 * a_
                    kb.dma("pool", Wqa[:, :, j, 64 * a_:64 * a_ + 64], wv[:, :, 64 * hh:64 * hh + 64], w=wtk())
            for h in range(4):
                kb.dma("pool", WkvV[:, 64 * h:64 * h + 64], wkvb_d[l, :, 128 * h + 64:128 * h + 128], w=wtk())
            for c0 in range(0, 8, 2):
                kb.dma("pool", Win[:, c0:c0 + 2, :], wv[:, c0:c0 + 2, :], w=wtk())
            for j in range(4):
                kb.dma("pool", Wkp[:, :, 32 * j:32 * j + 32], wv[:, :, 1024:1056], w=wtk())
            kb.dma("pool", Wqb[:], wqb_d[l].rearrange("(c p) n -> p c n", p=128), w=wtk())
            kb.dma("pool", Wkvb[:], wkvb_d[l], w=wtk())
            hT = sb(A, "hT", [128, 8, 512], BF16)
            h_t = Tok()
            sq = Rot(A, "sq", [128, 512], BF16, 2)
            tmpf = Rot(A, "tmpf", [128, 512], F32, 4)
            rsb = Rot(A, "rsb", [128, 512], F32, 2)
            qcf = sb(A, "qcf", [128, 2, 512])
            qcn = sb(A, "qcn", [128, 2, 512], BF16)
            qc_t, qcn_t = toks(2)
            ckb = sb(A, "ckb", [128, 512], BF16)
            ck_t = Tok()
            if sample:
                rope = sb(A, "rope", [128, 4, 512])
                rope_t = Tok()

            def proj(lhs_fn, nk=8, rhs_fn=None, pool="big", m=128):
                ps, pt = psum(pool)
                for k in range(nk):
                    mm(ps[0:m, :], lhs_fn(k), hT[:, k, :] if rhs_fn is None else rhs_fn(k), k == 0, k == nk - 1,
                       w_t + [h_t] if rhs_fn is None else w_t + [qcn_t], [pt])
                return ps, pt

            def headnorm(ps, pt, gcol):
                rs, rst = rsb.get()
                stats([(ps[:, :], [pt])], 512, BLKB, 1.0 / 64, sq, rs[:, :], rst)
                o, ot = tmpf.get()
                stt(o[:], ps[:, :], VEC[:, l, gcol:gcol + 1], rs[:], ALU.mult, ALU.mult, [pt, rst, vec_t], [ot])
                return o, ot

            def do_rope(src, st, perm, ci, out, r, w):
                ps, pt = psum("gen")
                mm(ps[:, :], perm, src[:], True, True, [st, cst_t], [pt])
                a, at = tmpf.get()
                tt(a[:], src[:], rope[:, ci, :], ALU.mult, [st, rope_t], [at])
                b, bt = tmpf.get()
                tt(b[:], ps[:, :], rope[:, ci + 1, :], ALU.mult, [pt, rope_t], [bt])
                tt(out, a[:], b[:], ALU.add, [at, bt] + list(r), list(w))

            for ti in range(TG // 512):
                g0 = ti * 512
                a0 = t0 + g0
                if sample:
                    kb.dma("sp", rope[:], rope_d[:, :, g0:g0 + 512].rearrange("a p n -> p a n"), w=[rope_t])
                norm_mod(l, 0, 0, mg, a0, 512, hT, h_t, sq, tmpf, rsb)
                if sample and DEBUG.get('as') == 0 and ti == DEBUG.get('as_ti', 0):
                    kb.barrier()
                    raise StopBuild()
                for j in range(3):
                    ps, pt = proj(lambda k: Wqa[:, k, j, :])
                    o, ot = headnorm(ps, pt, 80)
                    if sample:
                        do_rope(o, ot, PERMA, 0, QaT[:, j, g0:g0 + 512], [], [q_t[0]])
                    else:
                        vcopy(QaT[:, j, g0:g0 + 512], o[:], [ot], [q_t[0]])
                if sample and DEBUG.get('as') == 1 and ti == DEBUG.get('as_ti', 0):
                    kb.barrier()
                    raise StopBuild()
                ps, pt = proj(lambda k: Win[:, k, 384:512])
                o, ot = headnorm(ps, pt, 81)
                if sample:
                    o2, o2t = tmpf.get()
                    do_rope(o, ot, PERMA, 0, o2[:], [], [o2t])
                    kb.dma("sp", gin1[:, g0:g0 + 512], o2[:], r=[o2t], w=[gin_t[0]])
                else:
                    kb.dma("sp", nkT_d[l, :, g0:g0 + 512], o[:], r=[ot])
                    vcopy(KK["KaT"][:, g0:g0 + 512], o[:], [ot], [key_t])
                if sample and DEBUG.get('as') == 2 and ti == DEBUG.get('as_ti', 0):
                    kb.barrier()
                    raise StopBuild()
                for i in range(2):
                    ps, pt = proj(lambda k: Win[:, k, 640 + 128 * i:768 + 128 * i])
                    acopy(qcf[:, i, :], ps[:, :], [pt], [qc_t])
                rs, rst = rsb.get()
                stats([(qcf[:, i, :], [qc_t]) for i in range(2)], 512, ONESB, 1.0 / 256, sq, rs[:], rst)
                for i in range(2):
                    stt(qcn[:, i, :], qcf[:, i, :], VEC[:, l, 82 + i:83 + i], rs[:], ALU.mult, ALU.mult, [qc_t, rst, vec_t], [qcn_t])
                for h in range(4):
                    ps, pt = proj(lambda k: Wqb[:, k, 96 * h:96 * h + 96], nk=2, rhs_fn=lambda k: qcn[:, k, :], m=96)
                    if sample:
                        o, ot = tmpf.get()
                        acopy(o[0:96, :], ps[0:96, :], [pt], [ot])
                        vcopy(QbT[0:64, h, g0:g0 + 512], o[0:64, :], [ot], [q_t[1]])
                        kb.op("dve", lambda e: e.memset(o[96:128, :], 0.0), r=[ot], w=[ot])
                        ps3, pt3 = psum("gen")
                        mm(ps3[:, :], PERMB, o[:, :], True, True, [ot, cst_t], [pt3])
                        a_, at_ = tmpf.get()
                        tt(a_[64:96, :], o[64:96, :], rope[64:96, 2, :], ALU.mult, [ot, rope_t], [at_])
                        b_, bt_ = tmpf.get()
                        tt(b_[64:96, :], ps3[64:96, :], rope[64:96, 3, :], ALU.mult, [pt3, rope_t], [bt_])
                        tt(QbT[64:96, h, g0:g0 + 512], a_[64:96, :], b_[64:96, :], ALU.add, [at_, bt_], [q_t[1]])
                    else:
                        acopy(QbT[0:96, h, g0:g0 + 512], ps[0:96, :], [pt], [q_t[1]])
                if sample and DEBUG.get('as') == 3 and ti == DEBUG.get('as_ti', 0):
                    kb.barrier()
                    raise StopBuild()
                ps, pt = proj(lambda k: Win[:, k, 896:1024])
                rs, rst = rsb.get()
                stats([(ps[:, :], [pt])], 512, ONESB, 1.0 / 128, sq, rs[:], rst)
                o, ot = tmpf.get()
                stt(o[:], ps[:, :], VEC[:, l, 84:85], rs[:], ALU.mult, ALU.mult, [pt, rst, vec_t], [ot])
                if sample:
                    kb.dma("sp", gin1[:, 1024 + g0:1024 + g0 + 512], o[:], r=[ot], w=[gin_t[0]])

                else:
                    kb.dma("sp", nckvT_d[l, :, g0:g0 + 512], o[:], r=[ot])
                    vcopy(ckb[:], o[:], [ot], [ck_t])
                    for h in range(4):
                        ps2, pt2 = psum("gen")
                        mm(ps2[0:64, :], Wkvb[:, 128 * h:128 * h + 64], ckb[:], True, True, w_t + [ck_t], [pt2])
                        acopy(KK["KbT"][0:64, h, g0:g0 + 512], ps2[0:64, :], [pt2], [key_t])
                    for kt in range(4):
                        ps2, pt2 = psum("gen")
                        mm(ps2[:, 0:256], ckb[:, kt * 128:(kt + 1) * 128], WkvV[:],
                           True, True, w_t + [ck_t], [pt2])
                        acopy(KK["Vb"][:, kt, :, 0:64], ps2[:, 0:256].rearrange("p (h d) -> p h d", h=4), [pt2], [key_t])
                if sample and DEBUG.get('as') == 4 and ti == DEBUG.get('as_ti', 0):
                    kb.barrier()
                    raise StopBuild()
                ps, pt = proj(lambda k: Wkp[:, k, :])
                o, ot = tmpf.get()
                acopy(o[:], ps[:, :], [pt], [ot])
                if sample:
                    o2, o2t = tmpf.get()
                    do_rope(o, ot, PERMB, 2, o2[:], [], [o2t])
                    kb.dma("sp", gin1[:, 2048 + g0:2048 + g0 + 512], o2[:], r=[o2t], w=[gin_t[1]])
                else:
                    kb.dma("sp", nkpeT_d[l, :, g0:g0 + 512], o[0:32, :], r=[ot])
                    for h in range(4):
                        vcopy(KK["KbT"][64:96, h, g0:g0 + 512], o[64:96, :], [ot], [key_t])
                if sample and DEBUG.get('as') == 5 and ti == DEBUG.get('as_ti', 0):
                    kb.barrier()
                    raise StopBuild()
                for i in range(5):
                    ps, pt = proj(lambda k: Win[:, k, 1440 + 128 * i:1568 + 128 * i])
                    if sample:
                        acopy(xpad[:, i, 2 + g0:2 + g0 + 512], ps[:, :], [pt], [xpad_t])
                    else:
                        acopy(xpad[:, i, :].rearrange("p (s w) -> p s w", s=2)[:, :, 2:258],
                              ps[:, :].rearrange("p (s w) -> p s w", s=2), [pt], [xpad_t])
                if sample and DEBUG.get('as') == 6 and ti == DEBUG.get('as_ti', 0):
                    kb.barrier()
                    raise StopBuild()
                if sample and ti == TG // 512 - 1:
                    HS = sb(A, "HS", [128, 5, 4])
                    hs_t = Tok()
                    vcopy(HS[:, :, 0:2], xpad[:, :, 2:4], [xpad_t], [hs_t])
                    vcopy(HS[:, :, 2:4], xpad[:, :, 1024:1026], [xpad_t], [hs_t])
                    kb.dma("sp", gin1[:, 4096:4116], HS[:].rearrange("p c j -> p (c j)"), r=[hs_t], w=[gin_t[2]])
                    kb.allgather(g1i[2], g1o[2], r=[gin_t[2]], w=[gout_t[2]])
                    kb.allgather(g1i[0], g1o[0], r=[gin_t[0]], w=[gout_t[0]])
                for i4 in range(4):
                    gi = ti * 4 + i4
                    psA, ptA = psum("gen")
                    for k in range(8):
                        mm(psA[:, 0:128], hT[:, k, i4 * 128:(i4 + 1) * 128], Win[:, k, 512:640], k == 0, k == 7, w_t + [h_t], [ptA])
                    for k in range(8):
                        mm(psA[:, 128:140], hT[:, k, i4 * 128:(i4 + 1) * 128], Win[:, k, 2080:2092], k == 0, k == 7, w_t + [h_t], [ptA])
                    psZ, ptZ = psum("gen")
                    for k in range(8):
                        mm(psZ[:, 0:384], hT[:, k, i4 * 128:(i4 + 1) * 128], Win[:, k, 1056:1440], k == 0, k == 7, w_t + [h_t], [ptZ])
                    sk = DEBUG.get("skip2", 0) if (sample and gi == 2) else 0
                    o, ot = tmpf.get()
                    if not sk & 1:
                        vcopy(o[:, 0:128], psA[:, 0:128], [ptA], [ot])
                    if sample:
                        if not (DEBUG.get("skipv") and gi >= 2):
                            kb.dma("sp", gin1[:, 3072 + gi * 128:3072 + (gi + 1) * 128], o[:, 0:128], r=[ot], w=[gin_t[1]])
                    else:
                        kb.dma("sp", nv_d[l, gi * 128:(gi + 1) * 128, :], o[:, 0:128], r=[ot])
                        vcopy(KK["Va"][:, gi, :, 0:64], o[:, 0:128].rearrange("p (g d) -> p g d", g=2), [ot], [key_t])
                    if not sk & 2:
                        tt(dtTM[:, gi, :], psA[:, 128:140], BCt[:, 0:12], ALU.add, [ptA, bc_t], [dt_t])
                    if not sk & 4:
                        act(szTM[:, gi, :], psZ[:, 0:384], AF.Silu, [ptZ], [sz_t])
                    if sample and DEBUG.get('as') == 9 and i4 == DEBUG.get('as_i4', 3):
                        kb.barrier()
                        raise StopBuild()
            if sample and DEBUG.get('as') == 7:
                kb.barrier()
                raise StopBuild()
            nd = ntile * 12
            dtf = dtTM[:].rearrange("p a b -> p (a b)")
            sp1, sp1t = tmpf.get()
            act(sp1[:, :nd], dtf, AF.Abs, [dt_t], [sp1t])
            act(sp1[:, :nd], sp1[:, :nd], AF.Exp, [sp1t], [sp1t], scale=-1.0)
            act(sp1[:, :nd], sp1[:, :nd], AF.Ln, [sp1t], [sp1t], bias=1.0, scale=1.0)
            stt(dtf, dtf, 0.0, sp1[:, :nd], ALU.max, ALU.add, [dt_t, sp1t], [dt_t])
            act(BCt[:, 12:24], BCt[:, 12:24], AF.Exp, [bc_t], [bc_t])
            stt(dA[:], dtTM[:], -1.0, BCt[:, 12:24].unsqueeze(1).broadcast_to([128, ntile, 12]), ALU.mult, ALU.mult,
                [dt_t, bc_t], [dt_t])
            if sample and DEBUG.get('as') == 8:
                kb.barrier()
                raise StopBuild()
            if sample:
                kb.allgather(g1i[1], g1o[1], r=[gin_t[1]], w=[gout_t[1]])
            kb.barrier()
            ckpt("A" + ("s" if sample else "p"))

        def attention():
            KaT, Va, KbT, Vb = KK["KaT"], KK["Va"], KK["KbT"], KK["Vb"]
            with ExitStack() as B:
                Pb = Rot(B, "Pb", [128, 512], BF16, 4)
                oS = Rot(B, "oS", [128, 640], F32, 2)
                rc = Rot(B, "rc", [128, 16], F32, 2)
                for qt in range(ntile):
                    q0 = qt * 128
                    if sample:
                        kts = list(range(NKT))
                    else:
                        kts = [2 * (qt // 2), 2 * (qt // 2) + 1]
                    groups = [kts[i:i + 4] for i in range(0, len(kts), 4)]
                    ob, obt = oS.get()
                    for mixer in range(2):
                        if DEBUG.get("skip_mixer") == mixer:
                            continue
                        nh = 6 if mixer == 0 else 4
                        psO, ptO = psum("acc")
                        items = [(h, grp) for h in range(nh) for grp in groups]

                        def emit_S(h, grp):
                            psS, ptS = psum("S")
                            for i, kt in enumerate(grp):
                                kc = slice(kt * 128, (kt + 1) * 128)
                                if mixer == 0:
                                    g = h // 3
                                    j = h % 3
                                    mm(psS[:, i * 128:(i + 1) * 128], KaT[64 * g:64 * g + 64, kc], QaT[64 * g:64 * g + 64, j, q0:q0 + 128],
                                       True, True, [key_t, q_t[0]], [ptS])
                                else:
                                    mm(psS[:, i * 128:(i + 1) * 128], KbT[0:96, h, kc], QbT[0:96, h, q0:q0 + 128],
                                       True, True, [key_t, q_t[1]], [ptS])
                            n = len(grp) * 128
                            pb, pbt = Pb.get()
                            act(pb[:, :n], psS[:, :n], AF.Exp, [ptS], [pbt], scale=(0.125 if mixer == 0 else 96.0 ** -0.5))
                            return pb, pbt

                        def emit_PV(h, grp, pb, pbt):
                            for i, kt in enumerate(grp):
                                v = Va[:, kt, h // 3, :] if mixer == 0 else Vb[:, kt, h, :]
                                mm(psO[:, 65 * h:65 * h + 65], pb[:, i * 128:(i + 1) * 128], v,
                                   kt == kts[0], kt == kts[-1], [pbt, key_t], [ptO])

                        DEPTH_P = 2
                        pend = [emit_S(*items[k]) for k in range(min(DEPTH_P, len(items)))]
                        for ii, (h, grp) in enumerate(items):
                            cur = pend.pop(0)
                            if ii + DEPTH_P < len(items):
                                pend.append(emit_S(*items[ii + DEPTH_P]))
                            emit_PV(h, grp, *cur)
                        r_, rt = rc.get()
                        pv = psO[:, 0:65 * nh].rearrange("p (h d) -> p h d", h=nh)
                        kb.op("dve", lambda e: e.reciprocal(out=r_[:, 0:nh], in_=pv[:, :, 64]), r=[ptO], w=[rt])
                        off = 0 if mixer == 0 else 384
                        tt(ob[:, off:off + 64 * nh].rearrange("p (h d) -> p h d", h=nh), pv[:, :, 0:64],
                           r_[:, 0:nh].unsqueeze(2).broadcast_to([128, nh, 64]), ALU.mult, [ptO, rt], [obt])
                    psA_, ptA_ = psum("gen")
                    psB_, ptB_ = psum("gen")
                    for c5 in range(5):
                        dst, dt_ = (psA_[:, c5 * 128:(c5 + 1) * 128], ptA_) if c5 < 4 else (psB_[:, 0:128], ptB_)
                        kb.op("pe", lambda e: e.transpose(out=dst, in_=ob[:, c5 * 128:(c5 + 1) * 128], identity=IDENT),
                              r=[obt, cst_t], w=[dt_])
                    vcopy(mixT[:, 0:4, q0:q0 + 128], psA_[:, 0:512].rearrange("p (c n) -> p c n", c=4), [ptA_], [mix_t[qt]])
                    vcopy(mixT[:, 4, q0:q0 + 128], psB_[:, 0:128], [ptB_], [mix_t[qt]])
                kb.barrier()
                ckpt("B" + ("s" if sample else "p"))

        if not sample:
            attention()

        with ExitStack() as C:
            xTM = sb(C, "xTM", [128, ntile, 384])
            BTM = sb(C, "BTM", [128, ntile, 128], BF16)
            BT = sb(C, "BT", [128, W], BF16)
            CT = sb(C, "CT", [128, W], BF16)
            St = sb(C, "St", [128, ntile, 2, 192])
            Hbf = sb(C, "Hbf", [128, ntile, 2, 192], BF16)
            EX = sb(C, "EX", [128, ntile, 36])
            CDm = sb(C, "CDm", [128, ntile, 2, 3])
            bt_t, cd_t = toks(2)
            xtm_t = toks(ntile)
            btm_t = toks(ntile)
            ex_t = toks(ntile)
            st_t = [toks(2) for _ in range(ntile)]
            hb_t = [toks(2) for _ in range(ntile)]
            tmpf = Rot(C, "tmpc", [128, 768], F32, 3)
            xwr = Rot(C, "xwr", [128, 384], BF16, 2)
            sml = Rot(C, "sml", [128, 64], F32, 4)
            hw = Rot(C, "hw", [128, 192], F32, 4)
            with ExitStack() as C1:
                acc = sb(C1, "acc", [128, W])
                xc = sb(C1, "xc", [128, W])
                acc_t, xc_t = toks(2)
                if sample:
                    HL = sb(C1, "HL", [128, 4, 20])
                    hl_t = Tok()
                    kb.dma("sp", HL[:], gv1[:, :, 4096:4116], r=[gout_t[2]], w=[hl_t])
                    HLv = HL[:].rearrange("p r (c j) -> p r c j", j=4)
                    for side in range(2):
                        dst = xpad[:, :, 0:2] if side == 0 else xpad[:, :, 1026:1028]
                        for j in range(4):
                            src = HLv[:, j, :, 2:4] if side == 0 else HLv[:, j, :, 0:2]
                            mcol = MSK[:, 8 + 4 * side + j:9 + 4 * side + j]
                            if j == 0:
                                kb.op("dve", lambda e: e.tensor_scalar_mul(out=dst, in0=src, scalar1=mcol), r=[hl_t, msk_t], w=[xpad_t])
                            else:
                                stt(dst, src, mcol, dst, ALU.mult, ALU.add, [hl_t, msk_t, xpad_t], [xpad_t])
                else:
                    for (a, b) in ((0, 2), (258, 262), (518, 520)):
                        kb.op("dve", lambda e: e.memset(xpad[:, :, a:b], 0.0), w=[xpad_t])
                for c in range(5):
                    cw = lambda j: VEC[:, l, 85 + c * 5 + j:86 + c * 5 + j]
                    kb.op("dve", lambda e: e.tensor_scalar_mul(out=acc[:], in0=xpad[:, c, 0:W], scalar1=cw(0)), r=[xpad_t, vec_t], w=[acc_t])
                    for j in range(1, 5):
                        stt(acc[:], xpad[:, c, j:j + W], cw(j), acc[:], ALU.mult, ALU.add, [xpad_t, vec_t, acc_t], [acc_t])
                    act(xc[:], acc[:], AF.Silu, [acc_t, vec_t], [xc_t], bias=VEC[:, l, 110 + c:111 + c], scale=1.0)
                    if c < 4:
                        for i in range(ntile):
                            ps, pt = psum("gen")
                            kb.op("pe", lambda e: e.transpose(out=ps[:, 0:128], in_=xc[:, acol(i):acol(i) + 128], identity=IDENT),
                                  r=[xc_t, cst_t], w=[pt])
                            if c < 3:
                                acopy(xTM[:, i, c * 128:(c + 1) * 128], ps[:, 0:128], [pt], [xtm_t[i]])
                            else:
                                acopy(BTM[:, i, :], ps[:, 0:128], [pt], [btm_t[i]])
                    if c == 3:
                        vcopy(BT[:], xc[:], [xc_t], [bt_t])
                    if c == 4:
                        vcopy(CT[:], xc[:], [xc_t], [bt_t])
            kb.barrier()
            ckpt("C1")
            for i in range(ntile):
                psM, ptM = psum("gen")
                for (c0, lhs, d0, n) in ((0, U_, 0, 6), (6, UB, 6, 6), (12, SL, 0, 6), (18, SLB, 6, 6), (24, ONES, 0, 12)):
                    mm(psM[:, c0:c0 + n], lhs, dA[:, i, d0:d0 + n], True, True, [dt_t, cst_t], [ptM])
                act(EX[:, i, :], psM[:, 0:36], AF.Exp, [ptM], [ex_t[i]])
                wdt, wdtt = sml.get()
                tt(wdt[:, 0:12], EX[:, i, 12:24], dtTM[:, i, :], ALU.mult, [ex_t[i], dt_t], [wdtt])
                for d in range(2):
                    xwb, xwt = xwr.get()
                    tt(xwb[:].rearrange("p (h d) -> p h d", h=6), xTM[:, i, :].rearrange("p (h d) -> p h d", h=6),
                       wdt[:, 6 * d:6 * d + 6].unsqueeze(2).broadcast_to([128, 6, 64]), ALU.mult, [xtm_t[i], wdtt], [xwt])
                    psT, ptT = psum("gen")
                    for g in range(2):
                        mm(psT[64 * g:64 * g + 64, 0:192], BTM[:, i, 64 * g:64 * g + 64], xwb[:, 192 * g:192 * g + 192], True, True,
                           [btm_t[i], xwt], [ptT])
                    acopy(St[:, i, d, :], psT[:, 0:192], [ptT], [st_t[i][d]])
            for d in range(2):
                for g in range(2):
                    vcopy(CDm[64 * g:64 * g + 64, :, d, :], EX[64 * g:64 * g + 64, :, 24 + 6 * d + 3 * g:27 + 6 * d + 3 * g], ex_t, [cd_t])

            ckpt("C2")

            def step(out, h, i, d, r_extra=(), w_extra=()):
                tt(out.rearrange("p (h d) -> p h d", h=3), h.rearrange("p (h d) -> p h d", h=3),
                   CDm[:, i, d, :].unsqueeze(2).broadcast_to([128, 3, 64]), ALU.mult, [cd_t] + list(r_extra), list(w_extra))
                tt(out, out, St[:, i, d, :], ALU.add, [st_t[i][d]] + list(w_extra), list(w_extra))

            def scan(h_init, hit, i_list, d, final=None):
                h, ht = h_init, hit
                for n_, i in enumerate(i_list):
                    vcopy(Hbf[:, i, d, :], h, [ht], [hb_t[i][d]])
                    if n_ == len(i_list) - 1 and final is None:
                        break
                    hn, hnt = hw.get()
                    step(hn[:], h, i, d, [ht], [hnt])
                    h, ht = hn[:], hnt
                return h, ht

            zero = sb(C, "zero", [128, 192])
            zt = Tok()
            kb.op("dve", lambda e: e.memset(zero[:], 0.0), w=[zt])
            if not sample:
                for sq_ in range(2):
                    for d in range(2):
                        il = [2 * sq_, 2 * sq_ + 1] if d == 0 else [2 * sq_ + 1, 2 * sq_]
                        h, ht = scan(zero[:], zt, il, d, final=True)
                        kb.dma("sp", nssm_d[l, sq_, d], h, r=[ht])
            else:
                G2 = sb(C, "G2", [128, G2W])
                g2_t = Tok()
                for d in range(2):
                    il = list(range(ntile)) if d == 0 else list(range(ntile - 1, -1, -1))
                    h, ht = hw.get()
                    vcopy(h[:], St[:, il[0], d, :], [st_t[il[0]][d]], [ht])
                    for i in il[1:]:
                        hn, hnt = hw.get()
                        step(hn[:], h[:], i, d, [ht], [hnt])
                        h, ht = hn, hnt
                    vcopy(G2[:, 192 * d:192 * d + 192], h[:], [ht], [g2_t])
                    vcopy(G2[:, 384 + 3 * d:387 + 3 * d], CDm[:, 0, d, :], [cd_t], [g2_t])
                    for i in range(1, ntile):
                        tt(G2[:, 384 + 3 * d:387 + 3 * d], G2[:, 384 + 3 * d:387 + 3 * d], CDm[:, i, d, :], ALU.mult, [cd_t, g2_t], [g2_t])
                kb.dma("sp", gin2, G2[:], r=[g2_t], w=[gin2_t])
                kb.allgather(gin2, gout2, r=[gin2_t], w=[gout2_t])
                GS = sb(C, "GS", [128, 4, G2W])
                H0 = sb(C, "H0", [128, 2, 192])
                gs_t, h0_t = toks(2)
                kb.dma("sp", GS[:], gout2.rearrange("(r p) c -> p r c", p=128), r=[gout2_t], w=[gs_t])
                kb.dma("sp", H0[:], h0_d[l].rearrange("d p c -> p d c"), w=[h0_t])
                for d in range(2):
                    h, ht = hw.get()
                    vcopy(h[:], H0[:, d, :], [h0_t], [ht])
                    for j in (range(4) if d == 0 else range(3, -1, -1)):
                        t1, t1t = hw.get()
                        tt(t1[:].rearrange("p (h d) -> p h d", h=3), h[:].rearrange("p (h d) -> p h d", h=3),
                           GS[:, j, 384 + 3 * d:387 + 3 * d].unsqueeze(2).broadcast_to([128, 3, 64]), ALU.mult, [ht, gs_t], [t1t])
                        tt(t1[:], t1[:], GS[:, j, 192 * d:192 * d + 192], ALU.add, [t1t, gs_t], [t1t])
                        tt(t1[:], t1[:], h[:], ALU.subtract, [t1t, ht], [t1t])
                        hn, hnt = hw.get()
                        stt(hn[:], t1[:], MSK[:, 4 * d + j:4 * d + j + 1], h[:], ALU.mult, ALU.add, [t1t, ht, msk_t], [hnt])
                        h, ht = hn, hnt
                    il = list(range(ntile)) if d == 0 else list(range(ntile - 1, -1, -1))
                    scan(h[:], ht, il, d)

            ckpt("C3")
            ysb = Rot(C, "ysb", [128, 384], F32, 2)
            ynb = Rot(C, "ynb", [128, 384], F32, 2)
            gmb = Rot(C, "gmb", [128, 256], F32, 4)
            scb = Rot(C, "scb", [128, 768], BF16, 2)
            xdb = Rot(C, "xdb", [128, 384], BF16, 2)
            for i in range(ntile):
                cs = slice(acol(i), acol(i) + 128)
                psGs = [psum("gen"), psum("S")]
                for g in range(2):
                    mm(psGs[g][0][:, 0:128], BT[64 * g:64 * g + 64, cs], CT[64 * g:64 * g + 64, cs], True, True, [bt_t], [psGs[g][1]])
                gm = []
                for d in range(2):
                    m_, mt = gmb.get()
                    for g in range(2):
                        tt(m_[:, 128 * g:128 * g + 128], psGs[g][0][:, 0:128], (U_ if d == 0 else UB), ALU.mult, [psGs[g][1], cst_t], [mt])
                    gm.append((m_, mt))
                y, yt = ysb.get()
                for d in range(2):
                    R, Rt = tmpf.get()
                    for h in range(6):
                        kb.op("dve", lambda e: e.tensor_scalar_mul(out=R[:, 128 * h:128 * h + 128], in0=(U_ if d == 0 else UB),
                                                                   scalar1=dA[:, i, 6 * d + h:6 * d + h + 1]), r=[dt_t, cst_t], w=[Rt])
                    Ee, Et = tmpf.get()
                    for hf in range(2):
                        psE, ptE = psum("S")
                        mm(psE[:, 0:384], SL if d == 0 else SLB, R[:, 384 * hf:384 * hf + 384], True, True, [Rt, cst_t], [ptE])
                        act(Ee[:, 384 * hf:384 * hf + 384], psE[:, 0:384], AF.Exp, [ptE], [Et])
                    sc, sct = scb.get()
                    for h in range(6):
                        g = h // 3
                        tt(sc[:, 128 * h:128 * h + 128], Ee[:, 128 * h:128 * h + 128], gm[d][0][:, 128 * g:128 * g + 128], ALU.mult,
                           [Et, gm[d][1]], [sct])
                    xd, xdt_ = xdb.get()
                    tt(xd[:].rearrange("p (h d) -> p h d", h=6), xTM[:, i, :].rearrange("p (h d) -> p h d", h=6),
                       dtTM[:, i, 6 * d:6 * d + 6].unsqueeze(2).broadcast_to([128, 6, 64]), ALU.mult, [xtm_t[i], dt_t], [xdt_])
                    psY, ptY = psum("acc")
                    for h in range(6):
                        mm(psY[:, 64 * h:64 * h + 64], sc[:, 128 * h:128 * h + 128], xd[:, 64 * h:64 * h + 64], True, True, [sct, xdt_], [ptY])
                    psFs = [psum("acc"), psum("gen")]
                    yo, yot = tmpf.get()
                    for g in range(2):
                        mm(psFs[g][0][:, 0:192], CT[64 * g:64 * g + 64, cs], Hbf[64 * g:64 * g + 64, i, d, :], True, True,
                           [bt_t, hb_t[i][d]], [psFs[g][1]])
                    for g in range(2):
                        tt(yo[:, 192 * g:192 * g + 192].rearrange("p (h d) -> p h d", h=3), psFs[g][0][:, 0:192].rearrange("p (h d) -> p h d", h=3),
                           EX[:, i, 6 * d + 3 * g:6 * d + 3 * g + 3].unsqueeze(2).broadcast_to([128, 3, 64]), ALU.mult, [psFs[g][1], ex_t[i]], [yot])
                    if d == 0:
                        tt(y[:], yo[:, 0:384], psY[:, 0:384], ALU.add, [yot, ptY], [yt])
                    else:
                        tt(y[:], y[:], yo[:, 0:384], ALU.add, [yot, yt], [yt])
                        tt(y[:], y[:], psY[:, 0:384], ALU.add, [ptY, yt], [yt])
                xD, xDt = tmpf.get()
                tt(xD[:, 0:384], xTM[:, i, :], BCt[:, 24:408], ALU.mult, [xtm_t[i], bc_t], [xDt])
                tt(y[:], y[:], xD[:, 0:384], ALU.add, [xDt, yt], [yt])
                tt(y[:], y[:], szTM[:, i, :], ALU.mult, [sz_t, yt], [yt])
                ss, sst = sml.get()
                junk, jt = tmpf.get()
                act(junk[:, 0:384], y[:], AF.Square, [yt], [jt, sst], accum_out=ss[:, 0:1])
                act(ss[:, 0:1], ss[:, 0:1], AF.Sqrt, [sst], [sst], scale=1.0 / 384, bias=EPS)
                kb.op("dve", lambda e: e.reciprocal(out=ss[:, 0:1], in_=ss[:, 0:1]), r=[sst], w=[sst])
                yn, ynt = ynb.get()
                stt(yn[:], y[:], ss[:, 0:1], BCt[:, 408:792], ALU.mult, ALU.mult, [yt, sst, bc_t], [ynt])
                psA_, ptA_ = psum("gen")
                for c3 in range(3):
                    kb.op("pe", lambda e: e.transpose(out=psA_[:, c3 * 128:(c3 + 1) * 128],
                                                      in_=yn[:, c3 * 128:(c3 + 1) * 128], identity=IDENT),
                          r=[ynt, cst_t], w=[ptA_])
                vcopy(mixT[:, 5:8, i * 128:(i + 1) * 128], psA_[:, 0:384].rearrange("p (c n) -> p c n", c=3),
                      [ptA_], [mix_t[i]])
            kb.barrier()
            ckpt("C" + ("s" if sample else "p"))
        M.close()
        OPEN.remove(M)

        if sample:
            KS = ExitStack()
            OPEN.append(KS)
            alloc_keys(KS)
            KaT, Va, KbT, Vb = KK["KaT"], KK["Va"], KK["KbT"], KK["Vb"]
            with ExitStack() as Dk:
                ckp = Rot(Dk, "ckp", [128, 512], BF16, 2)
                vstg = Rot(Dk, "vstg", [128, 1024], BF16, 2)
                wk = sb(Dk, "wkvb2", [128, 512], BF16)
                wkV = sb(Dk, "wkvV2", [128, 256], BF16)
                wk_t = Tok()
                kb.dma("pool", wk[:], wkvb_d[l], w=[wk_t])
                for h in range(4):
                    kb.dma("pool", wkV[:, 64 * h:64 * h + 64], wkvb_d[l, :, 128 * h + 64:128 * h + 128], w=[wk_t])
                kb.dma("pool", KaT[:, 0:256], ctxK_d[l], w=[Tok()])
                for h in range(4):
                    kb.dma("pool", KbT[64:96, h, 0:256], ctxP_d[l, 64:96, :], w=[Tok()])
                for t_ in range(2):
                    kb.dma("pool", Va[:, t_, :, 0:64], ctxV_d[l, t_ * 128:(t_ + 1) * 128, :].rearrange("p (g d) -> p g d", g=2), w=[Tok()])
                gv = gv1
                for r_ in range(4):
                    kb.dma("pool", KaT[:, 256 + 1024 * r_:1280 + 1024 * r_], gv[:, r_, 0:1024], r=[gout_t[0]], w=[Tok()])
                    kp_t = Tok()
                    kb.dma("pool", KbT[64:96, 0, 256 + 1024 * r_:1280 + 1024 * r_], gv[64:96, r_, 2048:3072], r=[gout_t[1]], w=[kp_t])
                    for h in range(1, 4):
                        (vcopy if h % 2 else acopy)(KbT[64:96, h, 256 + 1024 * r_:1280 + 1024 * r_], KbT[64:96, 0, 256 + 1024 * r_:1280 + 1024 * r_], [kp_t], [Tok()])
                    vs, vst = vstg.get()
                    kb.dma("pool", vs[:], gv[:, r_, 3072:4096], r=[gout_t[1]], w=[vst])
                    for g in range(2):
                        vcopy(Va[:, 2 + 8 * r_:10 + 8 * r_, g, 0:64], vs[:].rearrange("p (t g d) -> p t g d", t=8, g=2)[:, :, g, :], [vst], [Tok()])
                for pc in range(9):
                    n = 256 if pc == 0 else 512
                    k0 = 0 if pc == 0 else 256 + (pc - 1) * 512
                    cb, cbt = ckp.get()
                    if pc == 0:
                        kb.dma("pool", cb[:, 0:256], ctxC_d[l], w=[cbt])
                    else:
                        r_, hf = (pc - 1) // 2, (pc - 1) % 2
                        kb.dma("pool", cb[:], gv[:, r_, 1024 + 512 * hf:1536 + 512 * hf], r=[gout_t[0]], w=[cbt])
                    for h in range(4):
                        ps2, pt2 = psum("gen")
                        mm(ps2[0:64, 0:n], wk[:, 128 * h:128 * h + 64], cb[:, 0:n], True, True, [wk_t, cbt], [pt2])
                        acopy(KbT[0:64, h, k0:k0 + n], ps2[0:64, 0:n], [pt2], [Tok()])
                    for kt in range(n // 128):
                        ps2, pt2 = psum("gen")
                        mm(ps2[:, 0:256], cb[:, kt * 128:(kt + 1) * 128], wkV[:],
                           True, True, [wk_t, cbt], [pt2])
                        acopy(Vb[:, k0 // 128 + kt, :, 0:64], ps2[:, 0:256].rearrange("p (h d) -> p h d", h=4), [pt2], [Tok()])
                kb.barrier()
                ckpt("Dk")
            attention()
            KS.close()
            OPEN.remove(KS)

        with ExitStack() as Fz:
            Wo = sb(Fz, "Wo", [128, 8, 1024], BF16)
            wo_ts = toks(4)
            wv = wout_d[l].rearrange("(c p) n -> p c n", p=128)
            for c0 in range(0, 8, 2):
                kb.dma("pool", Wo[:, c0:c0 + 2, :], wv[:, c0:c0 + 2, :], w=[wo_ts[c0 // 2]])
            mo = sb(Fz, "mo", [128, 8, 512])
            mo_t = Tok()
            sq = Rot(Fz, "sqf", [128, 512], BF16, 2)
            tmpf = Rot(Fz, "tmpf2", [128, 512], F32, 3)
            rsb = Rot(Fz, "rsb2", [128, 512], F32, 2)
            for ti in range(TG // 512):
                g0 = ti * 512
                for oc in range(8):
                    ps, pt = psum("big")
                    for k in range(8):
                        mm(ps[:, :], Wo[:, k, oc * 128:(oc + 1) * 128], mixT[:, k, g0:g0 + 512], k == 0, k == 7,
                           [wo_ts[k // 2]] + mix_t[ti * 4:ti * 4 + 4], [pt])
                    acopy(mo[:, oc, :], ps[:, :], [pt], [mo_t])
                resid_add(l, 1, mg, t0 + g0, 512, mo, mo_t, sq, tmpf, rsb)
            kb.barrier()
            ckpt("F" + ("s" if sample else "p"))
        L.close()
        OPEN.remove(L)

    def ffn(l):
        with ExitStack() as Gz:
            h2 = sb(Gz, "h2", [128, 8, 768], BF16)
            f1 = sb(Gz, "f1", [128, 32, 768], BF16)
            fo = sb(Gz, "fo", [128, 8, 768])
            W1 = [sb(Gz, f"W1_{i}", [128, 8, 512], BF16) for i in range(2)]
            W2 = [sb(Gz, f"W2_{i}", [128, 32, 128], BF16) for i in range(2)]
            w1_t, w2_t = toks(2), toks(2)
            h2_t, f1_t, fo_t = toks(2), [toks(2) for _ in range(32)], toks(2)
            sq = Rot(Gz, "sqg", [128, 512], BF16, 2)
            tmpf = Rot(Gz, "tmpg", [128, 512], F32, 3)
            rsb = Rot(Gz, "rsg", [128, 512], F32, 2)
            w1v = w1_d[l].rearrange("(c p) n -> p c n", p=128)
            w2v = w2_d[l].rearrange("(c p) n -> p c n", p=128)
            for half in range(2):
                segs = [(0, 512, 0), (512, 256, 1)] if half == 0 else [(1024, 512, 1), (768, 256, 1)]
                hoff = {s[0]: o for s, o in zip(segs, (0, segs[0][1]))}
                kb.dma("pool", W1[0][:], w1v[:, :, 0:512], w=[w1_t[0]])
                for si, (a0, n, mg) in enumerate(segs):
                    o = hoff[a0]
                    norm_mod(l, 2, 24, mg, a0, n, h2[:, :, o:o + n], h2_t[si], sq, tmpf, rsb)
                for pc in range(8):
                    b = pc % 2
                    if pc + 1 < 8:
                        kb.dma("pool", W1[1 - b][:], w1v[:, :, (pc + 1) * 512:(pc + 2) * 512], w=[w1_t[1 - b]])
                    else:
                        kb.dma("pool", W2[0][:], w2v[:, :, 0:128], w=[w2_t[0]])
                    for j in range(4):
                        fc = pc * 4 + j
                        for si, (a0, n, mg) in enumerate(segs):
                            o = hoff[a0]
                            ps, pt = psum("big")
                            for k in range(8):
                                mm(ps[:, :n], W1[b][:, k, j * 128:(j + 1) * 128], h2[:, k, o:o + n], k == 0, k == 7,
                                   [w1_t[b], h2_t[si]], [pt])
                            tf, tft = tmpf.get()
                            act(tf[:, :n], ps[:, :n], AF.Relu, [pt], [tft])
                            tt(f1[:, fc, o:o + n], tf[:, :n], tf[:, :n], ALU.mult, [tft], [f1_t[fc][si]])
                for oc in range(8):
                    b = oc % 2
                    if oc + 1 < 8:
                        kb.dma("pool", W2[1 - b][:], w2v[:, :, (oc + 1) * 128:(oc + 2) * 128], w=[w2_t[1 - b]])
                    for si, (a0, n, mg) in enumerate(segs):
                        o = hoff[a0]
                        ps, pt = psum("big")
                        for k in range(32):
                            mm(ps[:, :n], W2[b][:, k, :], f1[:, k, o:o + n], k == 0, k == 31, [w2_t[b], f1_t[k][si]], [pt])
                        acopy(fo[:, oc, o:o + n], ps[:, :n], [pt], [fo_t[si]])
                for si, (a0, n, mg) in enumerate(segs):
                    o = hoff[a0]
                    resid_add(l, 3, mg, a0, n, fo[:, :, o:o + n], fo_t[si], sq, tmpf, rsb)
            kb.barrier()

    try:
        for l in range(depth):
            mixer_pass(l, False)
            mixer_pass(l, True)
            ffn(l)
    except StopBuild:
        if DEBUG.get("padact"):
            pt_ = Tok()
            for _ in range(DEBUG["padact"]):
                kb.op(DEBUG.get("padeng_name", "act"), lambda e: (e.activation(out=MSK[:, 0:1], in_=MSK[:, 0:1], func=AF.Copy) if DEBUG.get("padeng_name", "act") == "act" else e.tensor_copy(out=MSK[:, 0:1], in_=MSK[:, 0:1])), r=[pt_], w=[])
        kb.barrier()
        return nc, kb

    for c in range(8):
        kb.dma("sp", yT_d[c * 128:(c + 1) * 128, :], xT[:, c, :], r=x_t[c])
    kb.barrier(final=True)
    es.close()
    return nc, kb


def _consts():
    k = np.arange(128)
    ident = np.eye(128, dtype=np.float32)
    U = (k[:, None] <= k[None, :]).astype(np.float32)
    Ub = (k[:, None] >= k[None, :]).astype(np.float32)
    SLm = (k[:, None] > k[None, :]).astype(np.float32)
    SLb = (k[:, None] < k[None, :]).astype(np.float32)
    ones = np.ones((128, 128), np.float32)

    def perm(blk):
        q = blk // 2
        Pm = np.zeros((128, 128), np.float32)
        for b in range(128 // blk):
            for i in range(blk):
                o = b * blk + i
                if i < q:
                    Pm[o + q, o] = -1.0
                else:
                    Pm[o - q, o] = 1.0
        return Pm
    blkm = np.zeros((128, 128), np.float32)
    blkm[:64, :64] = 1
    blkm[64:, 64:] = 1
    return np.concatenate([ident, U, Ub, SLm, SLb, ones, perm(32), perm(16), blkm], axis=1)


def _rope_tables(q):
    t = (1024 * q + np.arange(1024))
    row = (t // 64).astype(np.float32)
    col = (t % 64).astype(np.float32)
    out = []
    for rot_dim, reps in ((64, 2), (32, 4)):
        half = rot_dim // 2
        inv = (1.0 / (np.float32(10000.0) ** (np.arange(0, half, 2, dtype=np.float32) / np.float32(half)))).astype(np.float32)
        angr = row[:, None] * inv[None, :]
        angc = col[:, None] * inv[None, :]
        ang = np.concatenate([angr, angr, angc, angc], axis=1).astype(np.float32)
        cos = np.tile(np.cos(ang).astype(np.float32).T, (reps, 1))
        sin = np.tile(np.sin(ang).astype(np.float32).T, (reps, 1))
        out += [cos, sin]
    return np.ascontiguousarray(np.stack(out, 0).astype(np.float32))


def _pl(v, nch):
    return np.asarray(v, np.float32).reshape(nch, 128).T


def kernel(**inp):
    inp = {k: np.asarray(v) for k, v in inp.items()}
    f = lambda a: np.ascontiguousarray(np.asarray(a, dtype=np.float32))
    nc, kb = build_program(RUN_DEPTH)
    cst = f(_consts())
    vec = np.zeros((128, DEPTH, NV), np.float32)
    bc = np.zeros((DEPTH, 128, NBC), np.float32)
    for l in range(DEPTH):
        vec[:, l, 0:8] = _pl(inp["norm_mix_pre"][l], 8)
        vec[:, l, 8:16] = _pl(inp["norm_mix_post"][l], 8)
        vec[:, l, 16:24] = _pl(inp["norm_ffn_pre"][l], 8)
        vec[:, l, 24:32] = _pl(inp["norm_ffn_post"][l], 8)
        vec[:, l, 32:80] = _pl(inp["b_mod"][l], 48)
        vec[:, l, 80] = np.tile(inp["attn_q_norm"][l], 2)
        vec[:, l, 81] = np.tile(inp["attn_k_norm"][l], 2)
        vec[:, l, 82:84] = _pl(inp["mla_q_norm"][l], 2)
        vec[:, l, 84] = inp["mla_kv_norm"][l]
        vec[:, l, 85:110] = inp["ssm_conv_w"][l].reshape(5, 128, 5).transpose(1, 0, 2).reshape(128, 25)
        vec[:, l, 110:115] = _pl(inp["ssm_conv_b"][l], 5)
        row = np.concatenate([inp["ssm_dt_bias"][l].reshape(12), inp["ssm_a_log"][l].reshape(12),
                              np.repeat(inp["ssm_d"][l], 64), inp["ssm_norm"][l]]).astype(np.float32)
        bc[l] = row[None, :]
    vec = f(vec.reshape(128, DEPTH * NV))
    shared = dict(vec=vec, bc=f(bc), cst=cst, w_in=f(inp["w_in"][:RUN_DEPTH]), w_qb=f(inp["mla_w_qb"][:RUN_DEPTH]),
                  w_kvb=f(inp["mla_w_kvb"][:RUN_DEPTH]), w_out=f(inp["w_out"][:RUN_DEPTH]), w_ffn1=f(inp["w_ffn1"][:RUN_DEPTH]),
                  w_ffn2=f(inp["w_ffn2"][:RUN_DEPTH]))
    in_maps = []
    for r in range(8):
        s, q = r // 4, r % 4
        xT = np.concatenate([inp["x_prompt"][2 * r], inp["x_prompt"][2 * r + 1],
                             inp["x_sample"][s, 1024 * q:1024 * q + 1024]], axis=0).T
        cT = np.zeros((128, 8, 4), np.float32)
        cT[:, :, 0] = _pl(inp["c_ctx"], 8)
        cT[:, :, 1] = _pl(inp["c"][0], 8)
        cT[:, :, 2] = _pl(inp["c"][1], 8)
        cT[:, :, 3] = _pl(inp["c_ctx"], 8)
        msk = np.zeros((128, 18), np.float32)
        msk[:, 16 + s] = 1.0
        for j in range(4):
            msk[:, j] = 1.0 if j < q else 0.0
            msk[:, 4 + j] = 1.0 if j > q else 0.0
            msk[:, 8 + j] = 1.0 if j == q - 1 else 0.0
            msk[:, 12 + j] = 1.0 if j == q + 1 else 0.0
        ck = inp["cache_attn_k"][s].reshape(DEPTH, 256, 128).transpose(0, 2, 1)
        cv = inp["cache_attn_v"][s].reshape(DEPTH, 256, 128)
        cc = inp["cache_mla_ckv"][s].transpose(0, 2, 1)
        cp = np.tile(inp["cache_mla_kpe"][s].transpose(0, 2, 1), (1, 4, 1))
        h0 = inp["state_ssm"][s].reshape(DEPTH, 2, 2, 3, 64, 64).transpose(0, 1, 2, 5, 3, 4).reshape(DEPTH, 2, 128, 192)
        m = dict(shared)
        m.update(xT=f(xT), cT=f(cT.reshape(128, 32)), w_mod=f(inp["w_mod"][:RUN_DEPTH, :, 1536 * q:1536 * q + 1536]), rope=_rope_tables(q), msk=f(msk), ctxK=f(ck), ctxV=f(cv),
                 ctxC=f(cc), ctxP=f(cp), h0=f(h0))
        in_maps.append(m)
    res = run_bass_kernel_spmd(nc, in_maps, core_ids=list(range(8)))
    R = res.results
    yp = np.zeros((16, 256, D), np.float32)
    ys = np.zeros((2, 4096, D), np.float32)
    nk = np.zeros((16, DEPTH, 256, 2, 64), np.float32)
    nv = np.zeros((16, DEPTH, 256, 2, 64), np.float32)
    nckv = np.zeros((16, DEPTH, 256, 128), np.float32)
    nkpe = np.zeros((16, DEPTH, 256, 32), np.float32)
    nssm = np.zeros((16, DEPTH, 2, 6, 64, 64), np.float32)
    for r in range(8):
        s, q = r // 4, r % 4
        yT = np.asarray(R[r]["yT"])
        for j in range(2):
            b = 2 * r + j
            yp[b] = yT[:, 256 * j:256 * j + 256].T
            for l in range(DEPTH):
                nk[b, l] = np.asarray(R[r]["nkT"])[l][:, 256 * j:256 * j + 256].T.reshape(256, 2, 64)
                nv[b, l] = np.asarray(R[r]["nv"])[l][256 * j:256 * j + 256, :].reshape(256, 2, 64)
                nckv[b, l] = np.asarray(R[r]["nckvT"])[l][:, 256 * j:256 * j + 256].T
                nkpe[b, l] = np.asarray(R[r]["nkpeT"])[l][:, 256 * j:256 * j + 256].T
                for d in range(2):
                    a = np.asarray(R[r]["nssm"])[l, j, d]
                    nssm[b, l, d] = a.reshape(2, 64, 3, 64).transpose(0, 2, 3, 1).reshape(6, 64, 64)
        ys[s, 1024 * q:1024 * q + 1024] = yT[:, 512:].T
    return yp, ys, nk, nv, nckv, nkpe, nssm
```

```python
import numpy as np
from contextlib import ExitStack
import concourse.bass as bass
import concourse.mybir as mybir
from concourse.bass_utils import run_bass_kernel_spmd

F32 = mybir.dt.float32
BF16 = mybir.dt.bfloat16
ALU = mybir.AluOpType
AF = mybir.ActivationFunctionType

DEPTH = 4
D = 1024
T = 1536
EPS = 1e-6
NV = 115
NBC = 792
G1W = 4128
G2W = 392
STOP = None
SEM_LIMIT = 2000
NDMA = 12
DEBUG = {}
RUN_DEPTH = DEPTH


class StopBuild(Exception):
    pass


class Sem:
    __slots__ = ("i", "h")

    def __init__(s, i, h):
        s.i = i
        s.h = h


class Tok:
    __slots__ = ("w", "r")

    def __init__(s):
        s.w = None
        s.r = {}


def toks(n):
    return [Tok() for _ in range(n)]


class KB:
    def __init__(s, nc):
        s.nc = nc
        s.es = ExitStack()
        s.E = dict(pe=nc.tensor, act=nc.scalar, dve=nc.vector, pool=nc.gpsimd, sp=nc.sync)
        s.nsem = 0
        s.cur = {}
        s.known = {e: {} for e in s.E}
        s.retired = []
        for e in s.E:
            s.cur[e] = [s.newsem(), 0]
        s.dq = {q: [[s.newsem(), 0] for _ in range(NDMA)] for q in ("sp", "pool")}
        s.dqi = {"sp": 0, "pool": 0}
        s.cc = [s.newsem(), 0]
        s.nins = {e: 0 for e in s.E}

    def newsem(s):
        s.nsem += 1
        return Sem(s.nsem, s.es.enter_context(s.nc.semaphore(f"s{s.nsem}")))

    def _wait(s, eng, deps):
        kn = s.known[eng]
        best = {}
        own = s.cur[eng][0]
        for (sm, v) in deps:
            if eng == "pe" and sm is own:
                continue
            if v > kn.get(sm.i, 0) and v > best.get(sm.i, (None, 0))[1]:
                best[sm.i] = (sm, v)
        for sm, v in best.values():
            s.E[eng].wait_ge(sm.h, v)
            kn[sm.i] = v

    def _deps(s, eng, r, w):
        deps = []
        own = s.cur[eng][0]
        for t in r:
            if t.w:
                deps.append(t.w)
        for t in w:
            if t.w:
                deps.append(t.w)
            for ev in t.r.values():
                if ev[0] is not own or eng != "pe":
                    deps.append(ev)
        return deps

    def _mark(s, ev, r, w):
        for t in r:
            t.r[ev[0].i] = ev
        for t in w:
            t.w = ev
            t.r = {}

    def op(s, eng, fn, r=(), w=()):
        s._wait(eng, s._deps(eng, r, w))
        ins = fn(s.E[eng])
        c = s.cur[eng]
        c[1] += 1
        ins.then_inc(c[0].h, 1)
        s.nins[eng] += 1
        s._mark((c[0], c[1]), r, w)
        if c[1] >= SEM_LIMIT:
            s.retired.append((c[0], c[1]))
            s.cur[eng] = [s.newsem(), 0]

    def dma(s, q, out, in_, r=(), w=()):
        deps = s._deps(q, r, w)
        slot = s.dq[q][s.dqi[q] % NDMA]
        s.dqi[q] += 1
        if slot[1] > 0:
            deps.append((slot[0], slot[1]))
        s._wait(q, deps)
        ins = s.E[q].dma_start(out=out, in_=in_)
        slot[1] += 16
        ins.then_inc(slot[0].h, 16)
        s.nins[q] += 1
        s._mark((slot[0], slot[1]), r, w)

    def allgather(s, gin, gout, r=(), w=(), groups=None):
        s._wait("pool", s._deps("pool", r, w))
        ins = s.nc.gpsimd.collective_compute(
            "AllGather", ALU.bypass, replica_groups=groups or [[0, 1, 2, 3], [4, 5, 6, 7]],
            ins=[gin], outs=[gout])
        s.cc[1] += 1
        ins.then_inc(s.cc[0].h, 1)
        s._mark((s.cc[0], s.cc[1]), r, w)

    def barrier(s, final=False):
        evs = list(s.retired)
        for e in s.E:
            if s.cur[e][1] > 0:
                evs.append((s.cur[e][0], s.cur[e][1]))
        for q in s.dq:
            for sl in s.dq[q]:
                if sl[1] > 0:
                    evs.append((sl[0], sl[1]))
        if final and s.cc[1] > 0:
            evs.append((s.cc[0], s.cc[1]))
        for e in s.E:
            s._wait(e, evs)


def build_program(depth=DEPTH):
    nc = bass.Bass("TRN2", target_bir_lowering=False)
    kb = KB(nc)
    es = kb.es

    def din(name, shape, dt=F32):
        return nc.dram_tensor(name, list(shape), dt, kind="ExternalInput").ap()

    def dout(name, shape, dt=F32):
        return nc.dram_tensor(name, list(shape), dt, kind="ExternalOutput").ap()

    xT_d = din("xT", [D, T])
    cT_d = din("cT", [128, 32])
    vec_d = din("vec", [128, DEPTH * NV])
    bc_d = din("bc", [DEPTH, 128, NBC])
    cst_d = din("cst", [128, 9 * 128])
    rope_d = din("rope", [4, 128, 1024])
    msk_d = din("msk", [128, 18])
    ctxK_d = din("ctxK", [DEPTH, 128, 256])
    ctxV_d = din("ctxV", [DEPTH, 256, 128])
    ctxC_d = din("ctxC", [DEPTH, 128, 256])
    ctxP_d = din("ctxP", [DEPTH, 128, 256])
    h0_d = din("h0", [DEPTH, 2, 128, 192])
    wmod_d = din("w_mod", [depth, D, 1536])
    win_d = din("w_in", [depth, D, 2092])
    wqb_d = din("w_qb", [depth, 256, 384])
    wkvb_d = din("w_kvb", [depth, 128, 512])
    wout_d = din("w_out", [depth, D, D])
    w1_d = din("w_ffn1", [depth, D, 4 * D])
    w2_d = din("w_ffn2", [depth, 4 * D, D])

    yT_d = dout("yT", [D, T])
    nkT_d = dout("nkT", [DEPTH, 128, 512])
    nv_d = dout("nv", [DEPTH, 512, 128])
    nckvT_d = dout("nckvT", [DEPTH, 128, 512])
    nkpeT_d = dout("nkpeT", [DEPTH, 32, 512])
    nssm_d = dout("nssm", [DEPTH, 2, 2, 128, 192])

    g1i = [nc.dram_tensor(f"gin1{k}", [128, w_], F32).ap() for k, w_ in enumerate((2048, 2048, 32))]
    g1o = [nc.dram_tensor(f"gout1{k}", [512, w_], F32).ap() for k, w_ in enumerate((2048, 2048, 32))]

    class _G1:
        def __init__(s, lst, out):
            s.lst = lst
            s.out = out

        def __getitem__(s, key):
            if s.out:
                p, r_, c = key
            else:
                p, c = key
            k = c.start // 2048
            c2 = slice(c.start - 2048 * k, c.stop - 2048 * k)
            if s.out:
                return s.lst[k].rearrange("(r p) c -> p r c", p=128)[p, r_, c2]
            return s.lst[k][p, c2]
    gin1 = _G1(g1i, False)
    gv1 = _G1(g1o, True)
    gmi = nc.dram_tensor("gmi", [128, 192], F32).ap()
    gmo = nc.dram_tensor("gmo", [512, 192], F32).ap()
    gin2 = nc.dram_tensor("gin2", [128, G2W], F32).ap()
    gout2 = nc.dram_tensor("gout2", [512, G2W], F32).ap()
    gin1_t, gout1_t, gin2_t, gout2_t = toks(4)
    gin_t = toks(3)
    gout_t = toks(3)

    uid = [0]

    def sb(stack, name, shape, dt=F32):
        uid[0] += 1
        return stack.enter_context(nc.sbuf_tensor(f"sb{uid[0]}_{name}", list(shape), dt))

    PS = {}
    for pool, n in (("acc", 2), ("S", 3), ("gen", 2)):
        PS[pool] = [[es.enter_context(nc.psum_tensor(f"ps_{pool}{i}", [128, 512], F32)), Tok()] for i in range(n)]
    PS["big"] = PS["acc"] + PS["S"]
    psi = {"acc": 0, "S": 0, "gen": 0, "bf": 0, "big": 0}

    def psum(pool="gen"):
        lst = PS[pool]
        p = lst[psi[pool] % len(lst)]
        psi[pool] += 1
        return p[0], p[1]

    def mm(out, lhsT, rhs, start, stop, r, w):
        kb.op("pe", lambda e: e.matmul(out, lhsT, rhs, start=start, stop=stop), r=r, w=w)

    def act(out, in_, func, r, w, **kw):
        kb.op("act", lambda e: e.activation(out=out, in_=in_, func=func, **kw), r=r, w=w)

    def tt(out, in0, in1, op, r, w):
        kb.op("dve", lambda e: e.tensor_tensor(out=out, in0=in0, in1=in1, op=op), r=r, w=w)

    def stt(out, in0, scalar, in1, op0, op1, r, w):
        kb.op("dve", lambda e: e.scalar_tensor_tensor(out=out, in0=in0, scalar=scalar, in1=in1, op0=op0, op1=op1), r=r, w=w)

    def vcopy(out, in_, r, w):
        kb.op("dve", lambda e: e.tensor_copy(out=out, in_=in_), r=r, w=w)

    def acopy(out, in_, r, w):
        kb.op("act", lambda e: e.activation(out=out, in_=in_, func=AF.Copy), r=r, w=w)

    P = ExitStack()
    es.enter_context(P)
    xT = sb(P, "xT", [128, 8, T])
    x_t = [toks(6) for _ in range(8)]

    def xtk(c, a, n):
        return [x_t[c][b] for b in range(a // 256, (a + n + 255) // 256)]

    CF = sb(P, "CF", [128, 9, 128])
    CB = sb(P, "CB", [128, 3, 128], BF16)
    VEC = sb(P, "VEC", [128, DEPTH, NV])
    MSK = sb(P, "MSK", [128, 18])
    cT = sb(P, "cT", [128, 8, 4])
    MOD = sb(P, "MOD", [128, DEPTH, 48, 2])
    DER = sb(P, "DER", [128, DEPTH, 4, 8, 2])
    cst_t, vec_t, msk_t, c_t, mod_t, der_t = toks(6)
    IDENT, U_, UB, SL, SLB, ONES, PERMA, PERMB, BLK = [CF[:, i, :] for i in range(9)]
    ONESB, BLKB, IDENTB = [CB[:, i, :] for i in range(3)]

    for c in range(8):
        kb.dma("sp", xT[:, c, :], xT_d[c * 128:(c + 1) * 128, :], w=x_t[c])
    kb.dma("sp", CF[:], cst_d.rearrange("p (a b) -> p a b", a=9), w=[cst_t])
    kb.dma("sp", VEC[:], vec_d.rearrange("p (a b) -> p a b", a=DEPTH), w=[vec_t])
    kb.dma("sp", MSK[:], msk_d, w=[msk_t])
    kb.dma("sp", cT[:], cT_d.rearrange("p (a b) -> p a b", a=8), w=[c_t])
    vcopy(CB[:, 0, :], ONES, [cst_t], [cst_t])
    vcopy(CB[:, 1, :], BLK, [cst_t], [cst_t])
    vcopy(CB[:, 2, :], IDENT, [cst_t], [cst_t])
    act(cT[:], cT[:], AF.Silu, [c_t], [c_t])

    with ExitStack() as S0:
        wm = [sb(S0, f"wm{i}", [128, 8, 384]) for i in range(2)]
        wm_t = toks(2)
        GMs = sb(S0, "GMs", [128, 192])
        GM = sb(S0, "GM", [128, 4, 192])
        TM0 = sb(S0, "TM0", [128, 4, 12])
        gms_t, gm_t, gmi_t, gmo_t, tm0_t = toks(5)
        ps, pt = psum("gen")
        for l in range(depth):
            for pc in range(4):
                b = pc % 2
                kb.dma("sp", wm[b][:], wmod_d[l].rearrange("(c p) n -> p c n", p=128)[:, :, pc * 384:(pc + 1) * 384], w=[wm_t[b]])
                for j in range(3):
                    ch = l * 12 + pc * 3 + j
                    for k in range(8):
                        mm(ps[:, ch * 4:ch * 4 + 4], wm[b][:, k, j * 128:(j + 1) * 128], cT[:, k, :], k == 0, k == 7,
                           [wm_t[b], c_t], [pt])
        vcopy(GMs[:, 0:48 * depth], ps[:, 0:48 * depth], [pt], [gms_t])
        kb.dma("sp", gmi[:, 0:48 * depth], GMs[:, 0:48 * depth], r=[gms_t], w=[gmi_t])
        kb.allgather(gmi, gmo, r=[gmi_t], w=[gmo_t])
        kb.dma("sp", GM[:], gmo.rearrange("(r p) c -> p r c", p=128), r=[gmo_t], w=[gm_t])
        for l in range(depth):
            gl = GM[:, :, 48 * l:48 * l + 48].rearrange("p r (j v) -> p r j v", v=4)
            bm = VEC[:, l, 32:80].rearrange("p (r j) -> p r j", r=4)
            mo = MOD[:, l].rearrange("p (r j) g -> p r j g", r=4)
            tt(mo[:, :, :, 0], gl[:, :, :, 0], bm, ALU.add, [gm_t, vec_t], [mod_t])
            kb.op("dve", lambda e: e.tensor_scalar_mul(out=TM0[:], in0=gl[:, :, :, 1], scalar1=MSK[:, 16:17]), r=[gm_t, msk_t], w=[tm0_t])
            stt(TM0[:], gl[:, :, :, 2], MSK[:, 17:18], TM0[:], ALU.mult, ALU.add, [gm_t, msk_t, tm0_t], [tm0_t])
            tt(mo[:, :, :, 1], TM0[:], bm, ALU.add, [tm0_t, vec_t, mod_t], [mod_t])
            for i, (gcol, mch, plus1) in enumerate(((0, 8, True), (8, 16, False), (16, 32, True), (24, 40, False))):
                gb = VEC[:, l, gcol:gcol + 8].unsqueeze(2).broadcast_to([128, 8, 2])
                if plus1:
                    stt(DER[:, l, i], MOD[:, l, mch:mch + 8, :], 1.0, gb, ALU.add, ALU.mult, [mod_t, vec_t], [der_t])
                else:
                    tt(DER[:, l, i], MOD[:, l, mch:mch + 8, :], gb, ALU.mult, [mod_t, vec_t], [der_t])
        kb.barrier()

    OPEN = []

    def ckpt(name):
        if STOP == name:
            raise StopBuild()

    class Rot:
        def __init__(s, stack, name, shape, dt, n):
            s.b = [sb(stack, f"{name}{i}", shape, dt) for i in range(n)]
            s.t = toks(n)
            s.i = 0

        def get(s):
            j = s.i % len(s.b)
            s.i += 1
            return s.b[j], s.t[j]

    def stats(srcs, n, ones_b, inv, sq, rs_out, rs_tok):
        ps, pt = psum("gen")
        for i, (ap, tk) in enumerate(srcs):
            q, qt = sq.get()
            act(q[:, :n], ap, AF.Square, tk, [qt])
            mm(ps[:, :n], ones_b, q[:, :n], i == 0, i == len(srcs) - 1, [qt, cst_t], [pt])
        act(rs_out, ps[:, :n], AF.Ln, [pt], [rs_tok], scale=inv, bias=EPS)
        act(rs_out, rs_out, AF.Exp, [rs_tok], [rs_tok], scale=-0.5)

    def norm_mod(l, di_scale, shift_ch, mg, a0, n, hT, h_tok, sq, tmpf, rsb):
        rs, rst = rsb.get()
        stats([(xT[:, c, a0:a0 + n], xtk(c, a0, n)) for c in range(8)], n, ONESB, 1.0 / D, sq, rs[:, :n], rst)
        for c in range(8):
            tf, tft = tmpf.get()
            stt(tf[:, :n], xT[:, c, a0:a0 + n], DER[:, l, di_scale, c, mg:mg + 1], rs[:, :n], ALU.mult, ALU.mult,
                xtk(c, a0, n) + [rst, der_t], [tft])
            act(hT[:, c, :n], tf[:, :n], AF.Identity, [tft, mod_t], [h_tok], bias=MOD[:, l, shift_ch + c, mg:mg + 1], scale=1.0)

    def resid_add(l, di_gate, mg, a0, n, src, src_tok, sq, tmpf, rsb):
        rs, rst = rsb.get()
        stats([(src[:, c, :n], [src_tok]) for c in range(8)], n, ONESB, 1.0 / D, sq, rs[:, :n], rst)
        for c in range(8):
            tf, tft = tmpf.get()
            stt(tf[:, :n], src[:, c, :n], DER[:, l, di_gate, c, mg:mg + 1], rs[:, :n], ALU.mult, ALU.mult,
                [src_tok, rst, der_t], [tft])
            tt(xT[:, c, a0:a0 + n], xT[:, c, a0:a0 + n], tf[:, :n], ALU.add, xtk(c, a0, n) + [tft], xtk(c, a0, n))

    def mixer_pass(l, sample):
        t0 = 512 if sample else 0
        TG = 1024 if sample else 512
        ntile = TG // 128
        mg = 1 if sample else 0
        Tp = 1028 if sample else 520
        W = Tp - 4
        NK = 4352 if sample else 512
        NKT = NK // 128

        def acol(i):
            return i * 128 if sample else (i // 2) * 260 + (i % 2) * 128

        ckpt("P0")
        L = ExitStack()
        OPEN.append(L)
        mixT = sb(L, "mixT", [128, 8, TG], BF16)
        mix_t = toks(ntile)
        QaT = sb(L, "QaT", [128, 3, TG], BF16)
        QbT = sb(L, "QbT", [128, 4, TG], BF16)
        q_t = toks(3)
        BCt = sb(L, "BCt", [128, NBC])
        bc_t = Tok()
        kb.dma("sp", BCt[:], bc_d[l], w=[bc_t])
        key_t = Tok()
        KK = {}

        def alloc_keys(stack):
            KK["KaT"] = sb(stack, "KaT", [128, NK], BF16)
            KK["Va"] = sb(stack, "Va", [128, NKT, 2, 65], BF16)
            KK["KbT"] = sb(stack, "KbT", [128, 4, NK], BF16)
            KK["Vb"] = sb(stack, "Vb", [128, NKT, 4, 65], BF16)
            kb.op("dve", lambda e: e.memset(KK["Va"][:, :, :, 64:65], 1.0), w=[key_t])
            kb.op("dve", lambda e: e.memset(KK["Vb"][:, :, :, 64:65], 1.0), w=[key_t])

        if not sample:
            alloc_keys(L)

        M = ExitStack()
        OPEN.append(M)
        xpad = sb(M, "xpad", [128, 5, Tp])
        xpad_t = Tok()
        szTM = sb(M, "szTM", [128, ntile, 384], BF16)
        dtTM = sb(M, "dtTM", [128, ntile, 12])
        dA = sb(M, "dA", [128, ntile, 12])
        sz_t, dt_t = toks(2)

        with ExitStack() as A:
            Win = sb(A, "Win", [128, 8, 2092], BF16)
            Wkp = sb(A, "Wkp", [128, 8, 128], BF16)
            Wqb = sb(A, "Wqb", [128, 2, 384], BF16)
            Wkvb = sb(A, "Wkvb", [128, 512], BF16)
            WkvV = sb(A, "WkvV", [128, 256], BF16)
            Wqa = sb(A, "Wqa", [128, 8, 3, 128], BF16)
            w_t = []

            def wtk():
                w_t.append(Tok())
                return [w_t[-1]]
            wv = win_d[l].rearrange("(c p) n -> p c n", p=128)
            for j in range(3):
                for a_ in range(2):
                    hh = j + 3 * a_
                    kb.dma("pool", Wqa[:, :, j, 64 * a_:64 * a_ + 64], wv[:, :, 64 * hh:64 * hh + 64], w=wtk())
            for h in range(4):
                kb.dma("pool", WkvV[:, 64 * h:64 * h + 64], wkvb_d[l, :, 128 * h + 64:128 * h + 128], w=wtk())
            for c0 in range(0, 8, 2):
                kb.dma("pool", Win[:, c0:c0 + 2, :], wv[:, c0:c0 + 2, :], w=wtk())
            for j in range(4):
                kb.dma("pool", Wkp[:, :, 32 * j:32 * j + 32], wv[:, :, 1024:1056], w=wtk())
            kb.dma("pool", Wqb[:], wqb_d[l].rearrange("(c p) n -> p c n", p=128), w=wtk())
            kb.dma("pool", Wkvb[:], wkvb_d[l], w=wtk())
            hT = sb(A, "hT", [128, 8, 512], BF16)
            h_t = Tok()
            sq = Rot(A, "sq", [128, 512], BF16, 2)
            tmpf = Rot(A, "tmpf", [128, 512], F32, 4)
            rsb = Rot(A, "rsb", [128, 512], F32, 2)
            qcf = sb(A, "qcf", [128, 2, 512])
            qcn = sb(A, "qcn", [128, 2, 512], BF16)
            qc_t, qcn_t = toks(2)
            ckb = sb(A, "ckb", [128, 512], BF16)
            ck_t = Tok()
            if sample:
                rope = sb(A, "rope", [128, 4, 512])
                rope_t = Tok()

            def proj(lhs_fn, nk=8, rhs_fn=None, pool="big", m=128):
                ps, pt = psum(pool)
                for k in range(nk):
                    mm(ps[0:m, :], lhs_fn(k), hT[:, k, :] if rhs_fn is None else rhs_fn(k), k == 0, k == nk - 1,
                       w_t + [h_t] if rhs_fn is None else w_t + [qcn_t], [pt])
                return ps, pt

            def headnorm(ps, pt, gcol):
                rs, rst = rsb.get()
                stats([(ps[:, :], [pt])], 512, BLKB, 1.0 / 64, sq, rs[:, :], rst)
                o, ot = tmpf.get()
                stt(o[:], ps[:, :], VEC[:, l, gcol:gcol + 1], rs[:], ALU.mult, ALU.mult, [pt, rst, vec_t], [ot])
                return o, ot

            def do_rope(src, st, perm, ci, out, r, w):
                ps, pt = psum("gen")
                mm(ps[:, :], perm, src[:], True, True, [st, cst_t], [pt])
                a, at = tmpf.get()
                tt(a[:], src[:], rope[:, ci, :], ALU.mult, [st, rope_t], [at])
                b, bt = tmpf.get()
                tt(b[:], ps[:, :], rope[:, ci + 1, :], ALU.mult, [pt, rope_t], [bt])
                tt(out, a[:], b[:], ALU.add, [at, bt] + list(r), list(w))

            for ti in range(TG // 512):
                g0 = ti * 512
                a0 = t0 + g0
                if sample:
                    kb.dma("sp", rope[:], rope_d[:, :, g0:g0 + 512].rearrange("a p n -> p a n"), w=[rope_t])
                norm_mod(l, 0, 0, mg, a0, 512, hT, h_t, sq, tmpf, rsb)
                if sample and DEBUG.get('as') == 0 and ti == DEBUG.get('as_ti', 0):
                    kb.barrier()
                    raise StopBuild()
                for j in range(3):
                    ps, pt = proj(lambda k: Wqa[:, k, j, :])
                    o, ot = headnorm(ps, pt, 80)
                    if sample:
                        do_rope(o, ot, PERMA, 0, QaT[:, j, g0:g0 + 512], [], [q_t[0]])
                    else:
                        vcopy(QaT[:, j, g0:g0 + 512], o[:], [ot], [q_t[0]])
                if sample and DEBUG.get('as') == 1 and ti == DEBUG.get('as_ti', 0):
                    kb.barrier()
                    raise StopBuild()
                ps, pt = proj(lambda k: Win[:, k, 384:512])
                o, ot = headnorm(ps, pt, 81)
                if sample:
                    o2, o2t = tmpf.get()
                    do_rope(o, ot, PERMA, 0, o2[:], [], [o2t])
                    kb.dma("sp", gin1[:, g0:g0 + 512], o2[:], r=[o2t], w=[gin_t[0]])
                else:
                    kb.dma("sp", nkT_d[l, :, g0:g0 + 512], o[:], r=[ot])
                    vcopy(KK["KaT"][:, g0:g0 + 512], o[:], [ot], [key_t])
                if sample and DEBUG.get('as') == 2 and ti == DEBUG.get('as_ti', 0):
                    kb.barrier()
                    raise StopBuild()
                for i in range(2):
                    ps, pt = proj(lambda k: Win[:, k, 640 + 128 * i:768 + 128 * i])
                    acopy(qcf[:, i, :], ps[:, :], [pt], [qc_t])
                rs, rst = rsb.get()
                stats([(qcf[:, i, :], [qc_t]) for i in range(2)], 512, ONESB, 1.0 / 256, sq, rs[:], rst)
                for i in range(2):
                    stt(qcn[:, i, :], qcf[:, i, :], VEC[:, l, 82 + i:83 + i], rs[:], ALU.mult, ALU.mult, [qc_t, rst, vec_t], [qcn_t])
                for h in range(4):
                    ps, pt = proj(lambda k: Wqb[:, k, 96 * h:96 * h + 96], nk=2, rhs_fn=lambda k: qcn[:, k, :], m=96)
                    if sample:
                        o, ot = tmpf.get()
                        acopy(o[0:96, :], ps[0:96, :], [pt], [ot])
                        vcopy(QbT[0:64, h, g0:g0 + 512], o[0:64, :], [ot], [q_t[1]])
                        kb.op("dve", lambda e: e.memset(o[96:128, :], 0.0), r=[ot], w=[ot])
                        ps3, pt3 = psum("gen")
                        mm(ps3[:, :], PERMB, o[:, :], True, True, [ot, cst_t], [pt3])
                        a_, at_ = tmpf.get()
                        tt(a_[64:96, :], o[64:96, :], rope[64:96, 2, :], ALU.mult, [ot, rope_t], [at_])
                        b_, bt_ = tmpf.get()
                        tt(b_[64:96, :], ps3[64:96, :], rope[64:96, 3, :], ALU.mult, [pt3, rope_t], [bt_])
                        tt(QbT[64:96, h, g0:g0 + 512], a_[64:96, :], b_[64:96, :], ALU.add, [at_, bt_], [q_t[1]])
                    else:
                        acopy(QbT[0:96, h, g0:g0 + 512], ps[0:96, :], [pt], [q_t[1]])
                if sample and DEBUG.get('as') == 3 and ti == DEBUG.get('as_ti', 0):
                    kb.barrier()
                    raise StopBuild()
                ps, pt = proj(lambda k: Win[:, k, 896:1024])
                rs, rst = rsb.get()
                stats([(ps[:, :], [pt])], 512, ONESB, 1.0 / 128, sq, rs[:], rst)
                o, ot = tmpf.get()
                stt(o[:], ps[:, :], VEC[:, l, 84:85], rs[:], ALU.mult, ALU.mult, [pt, rst, vec_t], [ot])
                if sample:
                    kb.dma("sp", gin1[:, 1024 + g0:1024 + g0 + 512], o[:], r=[ot], w=[gin_t[0]])

                else:
                    kb.dma("sp", nckvT_d[l, :, g0:g0 + 512], o[:], r=[ot])
                    vcopy(ckb[:], o[:], [ot], [ck_t])
                    for h in range(4):
                        ps2, pt2 = psum("gen")
                        mm(ps2[0:64, :], Wkvb[:, 128 * h:128 * h + 64], ckb[:], True, True, w_t + [ck_t], [pt2])
                        acopy(KK["KbT"][0:64, h, g0:g0 + 512], ps2[0:64, :], [pt2], [key_t])
                    for kt in range(4):
                        ps2, pt2 = psum("gen")
                        mm(ps2[:, 0:256], ckb[:, kt * 128:(kt + 1) * 128], WkvV[:],
                           True, True, w_t + [ck_t], [pt2])
                        acopy(KK["Vb"][:, kt, :, 0:64], ps2[:, 0:256].rearrange("p (h d) -> p h d", h=4), [pt2], [key_t])
                if sample and DEBUG.get('as') == 4 and ti == DEBUG.get('as_ti', 0):
                    kb.barrier()
                    raise StopBuild()
                ps, pt = proj(lambda k: Wkp[:, k, :])
                o, ot = tmpf.get()
                acopy(o[:], ps[:, :], [pt], [ot])
                if sample:
                    o2, o2t = tmpf.get()
                    do_rope(o, ot, PERMB, 2, o2[:], [], [o2t])
                    kb.dma("sp", gin1[:, 2048 + g0:2048 + g0 + 512], o2[:], r=[o2t], w=[gin_t[1]])
                else:
                    kb.dma("sp", nkpeT_d[l, :, g0:g0 + 512], o[0:32, :], r=[ot])
                    for h in range(4):
                        vcopy(KK["KbT"][64:96, h, g0:g0 + 512], o[64:96, :], [ot], [key_t])
                if sample and DEBUG.get('as') == 5 and ti == DEBUG.get('as_ti', 0):
                    kb.barrier()
                    raise StopBuild()
                for i in range(5):
                    ps, pt = proj(lambda k: Win[:, k, 1440 + 128 * i:1568 + 128 * i])
                    if sample:
                        acopy(xpad[:, i, 2 + g0:2 + g0 + 512], ps[:, :], [pt], [xpad_t])
                    else:
                        acopy(xpad[:, i, :].rearrange("p (s w) -> p s w", s=2)[:, :, 2:258],
                              ps[:, :].rearrange("p (s w) -> p s w", s=2), [pt], [xpad_t])
                if sample and DEBUG.get('as') == 6 and ti == DEBUG.get('as_ti', 0):
                    kb.barrier()
                    raise StopBuild()
                if sample and ti == TG // 512 - 1:
                    HS = sb(A, "HS", [128, 5, 4])
                    hs_t = Tok()
                    vcopy(HS[:, :, 0:2], xpad[:, :, 2:4], [xpad_t], [hs_t])
                    vcopy(HS[:, :, 2:4], xpad[:, :, 1024:1026], [xpad_t], [hs_t])
                    kb.dma("sp", gin1[:, 4096:4116], HS[:].rearrange("p c j -> p (c j)"), r=[hs_t], w=[gin_t[2]])
                    kb.allgather(g1i[2], g1o[2], r=[gin_t[2]], w=[gout_t[2]])
                    kb.allgather(g1i[0], g1o[0], r=[gin_t[0]], w=[gout_t[0]])
                for i4 in range(4):
                    gi = ti * 4 + i4
                    psA, ptA = psum("gen")
                    for k in range(8):
                        mm(psA[:, 0:128], hT[:, k, i4 * 128:(i4 + 1) * 128], Win[:, k, 512:640], k == 0, k == 7, w_t + [h_t], [ptA])
                    for k in range(8):
                        mm(psA[:, 128:140], hT[:, k, i4 * 128:(i4 + 1) * 128], Win[:, k, 2080:2092], k == 0, k == 7, w_t + [h_t], [ptA])
                    psZ, ptZ = psum("gen")
                    for k in range(8):
                        mm(psZ[:, 0:384], hT[:, k, i4 * 128:(i4 + 1) * 128], Win[:, k, 1056:1440], k == 0, k == 7, w_t + [h_t], [ptZ])
                    sk = DEBUG.get("skip2", 0) if (sample and gi == 2) else 0
                    o, ot = tmpf.get()
                    if not sk & 1:
                        vcopy(o[:, 0:128], psA[:, 0:128], [ptA], [ot])
                    if sample:
                        if not (DEBUG.get("skipv") and gi >= 2):
                            kb.dma("sp", gin1[:, 3072 + gi * 128:3072 + (gi + 1) * 128], o[:, 0:128], r=[ot], w=[gin_t[1]])
                    else:
                        kb.dma("sp", nv_d[l, gi * 128:(gi + 1) * 128, :], o[:, 0:128], r=[ot])
                        vcopy(KK["Va"][:, gi, :, 0:64], o[:, 0:128].rearrange("p (g d) -> p g d", g=2), [ot], [key_t])
                    if not sk & 2:
                        tt(dtTM[:, gi, :], psA[:, 128:140], BCt[:, 0:12], ALU.add, [ptA, bc_t], [dt_t])
                    if not sk & 4:
                        act(szTM[:, gi, :], psZ[:, 0:384], AF.Silu, [ptZ], [sz_t])
                    if sample and DEBUG.get('as') == 9 and i4 == DEBUG.get('as_i4', 3):
                        kb.barrier()
                        raise StopBuild()
            if sample and DEBUG.get('as') == 7:
                kb.barrier()
                raise StopBuild()
            nd = ntile * 12
            dtf = dtTM[:].rearrange("p a b -> p (a b)")
            sp1, sp1t = tmpf.get()
            act(sp1[:, :nd], dtf, AF.Abs, [dt_t], [sp1t])
            act(sp1[:, :nd], sp1[:, :nd], AF.Exp, [sp1t], [sp1t], scale=-1.0)
            act(sp1[:, :nd], sp1[:, :nd], AF.Ln, [sp1t], [sp1t], bias=1.0, scale=1.0)
            stt(dtf, dtf, 0.0, sp1[:, :nd], ALU.max, ALU.add, [dt_t, sp1t], [dt_t])
            act(BCt[:, 12:24], BCt[:, 12:24], AF.Exp, [bc_t], [bc_t])
            stt(dA[:], dtTM[:], -1.0, BCt[:, 12:24].unsqueeze(1).broadcast_to([128, ntile, 12]), ALU.mult, ALU.mult,
                [dt_t, bc_t], [dt_t])
            if sample and DEBUG.get('as') == 8:
                kb.barrier()
                raise StopBuild()
            if sample:
                kb.allgather(g1i[1], g1o[1], r=[gin_t[1]], w=[gout_t[1]])
            kb.barrier()
            ckpt("A" + ("s" if sample else "p"))

        def attention():
            KaT, Va, KbT, Vb = KK["KaT"], KK["Va"], KK["KbT"], KK["Vb"]
            with ExitStack() as B:
                Pb = Rot(B, "Pb", [128, 512], BF16, 4)
                oS = Rot(B, "oS", [128, 640], F32, 2)
                rc = Rot(B, "rc", [128, 16], F32, 2)
                for qt in range(ntile):
                    q0 = qt * 128
                    if sample:
                        kts = list(range(NKT))
                    else:
                        kts = [2 * (qt // 2), 2 * (qt // 2) + 1]
                    groups = [kts[i:i + 4] for i in range(0, len(kts), 4)]
                    ob, obt = oS.get()
                    for mixer in range(2):
                        if DEBUG.get("skip_mixer") == mixer:
                            continue
                        nh = 6 if mixer == 0 else 4
                        psO, ptO = psum("acc")
                        items = [(h, grp) for h in range(nh) for grp in groups]

                        def emit_S(h, grp):
                            psS, ptS = psum("S")
                            for i, kt in enumerate(grp):
                                kc = slice(kt * 128, (kt + 1) * 128)
                                if mixer == 0:
                                    g = h // 3
                                    j = h % 3
                                    mm(psS[:, i * 128:(i + 1) * 128], KaT[64 * g:64 * g + 64, kc], QaT[64 * g:64 * g + 64, j, q0:q0 + 128],
                                       True, True, [key_t, q_t[0]], [ptS])
                                else:
                                    mm(psS[:, i * 128:(i + 1) * 128], KbT[0:96, h, kc], QbT[0:96, h, q0:q0 + 128],
                                       True, True, [key_t, q_t[1]], [ptS])
                            n = len(grp) * 128
                            pb, pbt = Pb.get()
                            act(pb[:, :n], psS[:, :n], AF.Exp, [ptS], [pbt], scale=(0.125 if mixer == 0 else 96.0 ** -0.5))
                            return pb, pbt

                        def emit_PV(h, grp, pb, pbt):
                            for i, kt in enumerate(grp):
                                v = Va[:, kt, h // 3, :] if mixer == 0 else Vb[:, kt, h, :]
                                mm(psO[:, 65 * h:65 * h + 65], pb[:, i * 128:(i + 1) * 128], v,
                                   kt == kts[0], kt == kts[-1], [pbt, key_t], [ptO])

                        DEPTH_P = 2
                        pend = [emit_S(*items[k]) for k in range(min(DEPTH_P, len(items)))]
                        for ii, (h, grp) in enumerate(items):
                            cur = pend.pop(0)
                            if ii + DEPTH_P < len(items):
                                pend.append(emit_S(*items[ii + DEPTH_P]))
                            emit_PV(h, grp, *cur)
                        r_, rt = rc.get()
                        pv = psO[:, 0:65 * nh].rearrange("p (h d) -> p h d", h=nh)
                        kb.op("dve", lambda e: e.reciprocal(out=r_[:, 0:nh], in_=pv[:, :, 64]), r=[ptO], w=[rt])
                        off = 0 if mixer == 0 else 384
                        tt(ob[:, off:off + 64 * nh].rearrange("p (h d) -> p h d", h=nh), pv[:, :, 0:64],
                           r_[:, 0:nh].unsqueeze(2).broadcast_to([128, nh, 64]), ALU.mult, [ptO, rt], [obt])
                    psA_, ptA_ = psum("gen")
                    psB_, ptB_ = psum("gen")
                    for c5 in range(5):
                        dst, dt_ = (psA_[:, c5 * 128:(c5 + 1) * 128], ptA_) if c5 < 4 else (psB_[:, 0:128], ptB_)
                        kb.op("pe", lambda e: e.transpose(out=dst, in_=ob[:, c5 * 128:(c5 + 1) * 128], identity=IDENT),
                              r=[obt, cst_t], w=[dt_])
                    vcopy(mixT[:, 0:4, q0:q0 + 128], psA_[:, 0:512].rearrange("p (c n) -> p c n", c=4), [ptA_], [mix_t[qt]])
                    vcopy(mixT[:, 4, q0:q0 + 128], psB_[:, 0:128], [ptB_], [mix_t[qt]])
                kb.barrier()
                ckpt("B" + ("s" if sample else "p"))

        if not sample:
            attention()

        with ExitStack() as C:
            xTM = sb(C, "xTM", [128, ntile, 384])
            BTM = sb(C, "BTM", [128, ntile, 128], BF16)
            BT = sb(C, "BT", [128, W], BF16)
            CT = sb(C, "CT", [128, W], BF16)
            St = sb(C, "St", [128, ntile, 2, 192])
            Hbf = sb(C, "Hbf", [128, ntile, 2, 192], BF16)
            EX = sb(C, "EX", [128, ntile, 36])
            CDm = sb(C, "CDm", [128, ntile, 2, 3])
            bt_t, cd_t = toks(2)
            xtm_t = toks(ntile)
            btm_t = toks(ntile)
            ex_t = toks(ntile)
            st_t = [toks(2) for _ in range(ntile)]
            hb_t = [toks(2) for _ in range(ntile)]
            tmpf = Rot(C, "tmpc", [128, 768], F32, 3)
            xwr = Rot(C, "xwr", [128, 384], BF16, 2)
            sml = Rot(C, "sml", [128, 64], F32, 4)
            hw = Rot(C, "hw", [128, 192], F32, 4)
            with ExitStack() as C1:
                acc = sb(C1, "acc", [128, W])
                xc = sb(C1, "xc", [128, W])
                acc_t, xc_t = toks(2)
                if sample:
                    HL = sb(C1, "HL", [128, 4, 20])
                    hl_t = Tok()
                    kb.dma("sp", HL[:], gv1[:, :, 4096:4116], r=[gout_t[2]], w=[hl_t])
                    HLv = HL[:].rearrange("p r (c j) -> p r c j", j=4)
                    for side in range(2):
                        dst = xpad[:, :, 0:2] if side == 0 else xpad[:, :, 1026:1028]
                        for j in range(4):
                            src = HLv[:, j, :, 2:4] if side == 0 else HLv[:, j, :, 0:2]
                            mcol = MSK[:, 8 + 4 * side + j:9 + 4 * side + j]
                            if j == 0:
                                kb.op("dve", lambda e: e.tensor_scalar_mul(out=dst, in0=src, scalar1=mcol), r=[hl_t, msk_t], w=[xpad_t])
                            else:
                                stt(dst, src, mcol, dst, ALU.mult, ALU.add, [hl_t, msk_t, xpad_t], [xpad_t])
                else:
                    for (a, b) in ((0, 2), (258, 262), (518, 520)):
                        kb.op("dve", lambda e: e.memset(xpad[:, :, a:b], 0.0), w=[xpad_t])
                for c in range(5):
                    cw = lambda j: VEC[:, l, 85 + c * 5 + j:86 + c * 5 + j]
                    kb.op("dve", lambda e: e.tensor_scalar_mul(out=acc[:], in0=xpad[:, c, 0:W], scalar1=cw(0)), r=[xpad_t, vec_t], w=[acc_t])
                    for j in range(1, 5):
                        stt(acc[:], xpad[:, c, j:j + W], cw(j), acc[:], ALU.mult, ALU.add, [xpad_t, vec_t, acc_t], [acc_t])
                    act(xc[:], acc[:], AF.Silu, [acc_t, vec_t], [xc_t], bias=VEC[:, l, 110 + c:111 + c], scale=1.0)
                    if c < 4:
                        for i in range(ntile):
                            ps, pt = psum("gen")
                            kb.op("pe", lambda e: e.transpose(out=ps[:, 0:128], in_=xc[:, acol(i):acol(i) + 128], identity=IDENT),
                                  r=[xc_t, cst_t], w=[pt])
                            if c < 3:
                                acopy(xTM[:, i, c * 128:(c + 1) * 128], ps[:, 0:128], [pt], [xtm_t[i]])
                            else:
                                acopy(BTM[:, i, :], ps[:, 0:128], [pt], [btm_t[i]])
                    if c == 3:
                        vcopy(BT[:], xc[:], [xc_t], [bt_t])
                    if c == 4:
                        vcopy(CT[:], xc[:], [xc_t], [bt_t])
            kb.barrier()
            ckpt("C1")
            for i in range(ntile):
                psM, ptM = psum("gen")
                for (c0, lhs, d0, n) in ((0, U_, 0, 6), (6, UB, 6, 6), (12, SL, 0, 6), (18, SLB, 6, 6), (24, ONES, 0, 12)):
                    mm(psM[:, c0:c0 + n], lhs, dA[:, i, d0:d0 + n], True, True, [dt_t, cst_t], [ptM])
                act(EX[:, i, :], psM[:, 0:36], AF.Exp, [ptM], [ex_t[i]])
                wdt, wdtt = sml.get()
                tt(wdt[:, 0:12], EX[:, i, 12:24], dtTM[:, i, :], ALU.mult, [ex_t[i], dt_t], [wdtt])
                for d in range(2):
                    xwb, xwt = xwr.get()
                    tt(xwb[:].rearrange("p (h d) -> p h d", h=6), xTM[:, i, :].rearrange("p (h d) -> p h d", h=6),
                       wdt[:, 6 * d:6 * d + 6].unsqueeze(2).broadcast_to([128, 6, 64]), ALU.mult, [xtm_t[i], wdtt], [xwt])
                    psT, ptT = psum("gen")
                    for g in range(2):
                        mm(psT[64 * g:64 * g + 64, 0:192], BTM[:, i, 64 * g:64 * g + 64], xwb[:, 192 * g:192 * g + 192], True, True,
                           [btm_t[i], xwt], [ptT])
                    acopy(St[:, i, d, :], psT[:, 0:192], [ptT], [st_t[i][d]])
            for d in range(2):
                for g in range(2):
                    vcopy(CDm[64 * g:64 * g + 64, :, d, :], EX[64 * g:64 * g + 64, :, 24 + 6 * d + 3 * g:27 + 6 * d + 3 * g], ex_t, [cd_t])

            ckpt("C2")

            def step(out, h, i, d, r_extra=(), w_extra=()):
                tt(out.rearrange("p (h d) -> p h d", h=3), h.rearrange("p (h d) -> p h d", h=3),
                   CDm[:, i, d, :].unsqueeze(2).broadcast_to([128, 3, 64]), ALU.mult, [cd_t] + list(r_extra), list(w_extra))
                tt(out, out, St[:, i, d, :], ALU.add, [st_t[i][d]] + list(w_extra), list(w_extra))

            def scan(h_init, hit, i_list, d, final=None):
                h, ht = h_init, hit
                for n_, i in enumerate(i_list):
                    vcopy(Hbf[:, i, d, :], h, [ht], [hb_t[i][d]])
                    if n_ == len(i_list) - 1 and final is None:
                        break
                    hn, hnt = hw.get()
                    step(hn[:], h, i, d, [ht], [hnt])
                    h, ht = hn[:], hnt
                return h, ht

            zero = sb(C, "zero", [128, 192])
            zt = Tok()
            kb.op("dve", lambda e: e.memset(zero[:], 0.0), w=[zt])
            if not sample:
                for sq_ in range(2):
                    for d in range(2):
                        il = [2 * sq_, 2 * sq_ + 1] if d == 0 else [2 * sq_ + 1, 2 * sq_]
                        h, ht = scan(zero[:], zt, il, d, final=True)
                        kb.dma("sp", nssm_d[l, sq_, d], h, r=[ht])
            else:
                G2 = sb(C, "G2", [128, G2W])
                g2_t = Tok()
                for d in range(2):
                    il = list(range(ntile)) if d == 0 else list(range(ntile - 1, -1, -1))
                    h, ht = hw.get()
                    vcopy(h[:], St[:, il[0], d, :], [st_t[il[0]][d]], [ht])
                    for i in il[1:]:
                        hn, hnt = hw.get()
                        step(hn[:], h[:], i, d, [ht], [hnt])
                        h, ht = hn, hnt
                    vcopy(G2[:, 192 * d:192 * d + 192], h[:], [ht], [g2_t])
                    vcopy(G2[:, 384 + 3 * d:387 + 3 * d], CDm[:, 0, d, :], [cd_t], [g2_t])
                    for i in range(1, ntile):
                        tt(G2[:, 384 + 3 * d:387 + 3 * d], G2[:, 384 + 3 * d:387 + 3 * d], CDm[:, i, d, :], ALU.mult, [cd_t, g2_t], [g2_t])
                kb.dma("sp", gin2, G2[:], r=[g2_t], w=[gin2_t])
                kb.allgather(gin2, gout2, r=[gin2_t], w=[gout2_t])
                GS = sb(C, "GS", [128, 4, G2W])
                H0 = sb(C, "H0", [128, 2, 192])
                gs_t, h0_t = toks(2)
                kb.dma("sp", GS[:], gout2.rearrange("(r p) c -> p r c", p=128), r=[gout2_t], w=[gs_t])
                kb.dma("sp", H0[:], h0_d[l].rearrange("d p c -> p d c"), w=[h0_t])
                for d in range(2):
                    h, ht = hw.get()
                    vcopy(h[:], H0[:, d, :], [h0_t], [ht])
                    for j in (range(4) if d == 0 else range(3, -1, -1)):
                        t1, t1t = hw.get()
                        tt(t1[:].rearrange("p (h d) -> p h d", h=3), h[:].rearrange("p (h d) -> p h d", h=3),
                           GS[:, j, 384 + 3 * d:387 + 3 * d].unsqueeze(2).broadcast_to([128, 3, 64]), ALU.mult, [ht, gs_t], [t1t])
                        tt(t1[:], t1[:], GS[:, j, 192 * d:192 * d + 192], ALU.add, [t1t, gs_t], [t1t])
                        tt(t1[:], t1[:], h[:], ALU.subtract, [t1t, ht], [t1t])
                        hn, hnt = hw.get()
                        stt(hn[:], t1[:], MSK[:, 4 * d + j:4 * d + j + 1], h[:], ALU.mult, ALU.add, [t1t, ht, msk_t], [hnt])
                        h, ht = hn, hnt
                    il = list(range(ntile)) if d == 0 else list(range(ntile - 1, -1, -1))
                    scan(h[:], ht, il, d)

            ckpt("C3")
            ysb = Rot(C, "ysb", [128, 384], F32, 2)
            ynb = Rot(C, "ynb", [128, 384], F32, 2)
            gmb = Rot(C, "gmb", [128, 256], F32, 4)
            scb = Rot(C, "scb", [128, 768], BF16, 2)
            xdb = Rot(C, "xdb", [128, 384], BF16, 2)
            for i in range(ntile):
                cs = slice(acol(i), acol(i) + 128)
                psGs = [psum("gen"), psum("S")]
                for g in range(2):
                    mm(psGs[g][0][:, 0:128], BT[64 * g:64 * g + 64, cs], CT[64 * g:64 * g + 64, cs], True, True, [bt_t], [psGs[g][1]])
                gm = []
                for d in range(2):
                    m_, mt = gmb.get()
                    for g in range(2):
                        tt(m_[:, 128 * g:128 * g + 128], psGs[g][0][:, 0:128], (U_ if d == 0 else UB), ALU.mult, [psGs[g][1], cst_t], [mt])
                    gm.append((m_, mt))
                y, yt = ysb.get()
                for d in range(2):
                    R, Rt = tmpf.get()
                    for h in range(6):
                        kb.op("dve", lambda e: e.tensor_scalar_mul(out=R[:, 128 * h:128 * h + 128], in0=(U_ if d == 0 else UB),
                                                                   scalar1=dA[:, i, 6 * d + h:6 * d + h + 1]), r=[dt_t, cst_t], w=[Rt])
                    Ee, Et = tmpf.get()
                    for hf in range(2):
                        psE, ptE = psum("S")
                        mm(psE[:, 0:384], SL if d == 0 else SLB, R[:, 384 * hf:384 * hf + 384], True, True, [Rt, cst_t], [ptE])
                        act(Ee[:, 384 * hf:384 * hf + 384], psE[:, 0:384], AF.Exp, [ptE], [Et])
                    sc, sct = scb.get()
                    for h in range(6):
                        g = h // 3
                        tt(sc[:, 128 * h:128 * h + 128], Ee[:, 128 * h:128 * h + 128], gm[d][0][:, 128 * g:128 * g + 128], ALU.mult,
                           [Et, gm[d][1]], [sct])
                    xd, xdt_ = xdb.get()
                    tt(xd[:].rearrange("p (h d) -> p h d", h=6), xTM[:, i, :].rearrange("p (h d) -> p h d", h=6),
                       dtTM[:, i, 6 * d:6 * d + 6].unsqueeze(2).broadcast_to([128, 6, 64]), ALU.mult, [xtm_t[i], dt_t], [xdt_])
                    psY, ptY = psum("acc")
                    for h in range(6):
                        mm(psY[:, 64 * h:64 * h + 64], sc[:, 128 * h:128 * h + 128], xd[:, 64 * h:64 * h + 64], True, True, [sct, xdt_], [ptY])
                    psFs = [psum("acc"), psum("gen")]
                    yo, yot = tmpf.get()
                    for g in range(2):
                        mm(psFs[g][0][:, 0:192], CT[64 * g:64 * g + 64, cs], Hbf[64 * g:64 * g + 64, i, d, :], True, True,
                           [bt_t, hb_t[i][d]], [psFs[g][1]])
                    for g in range(2):
                        tt(yo[:, 192 * g:192 * g + 192].rearrange("p (h d) -> p h d", h=3), psFs[g][0][:, 0:192].rearrange("p (h d) -> p h d", h=3),
                           EX[:, i, 6 * d + 3 * g:6 * d + 3 * g + 3].unsqueeze(2).broadcast_to([128, 3, 64]), ALU.mult, [psFs[g][1], ex_t[i]], [yot])
                    if d == 0:
                        tt(y[:], yo[:, 0:384], psY[:, 0:384], ALU.add, [yot, ptY], [yt])
                    else:
                        tt(y[:], y[:], yo[:, 0:384], ALU.add, [yot, yt], [yt])
                        tt(y[:], y[:], psY[:, 0:384], ALU.add, [ptY, yt], [yt])
                xD, xDt = tmpf.get()
                tt(xD[:, 0:384], xTM[:, i, :], BCt[:, 24:408], ALU.mult, [xtm_t[i], bc_t], [xDt])
                tt(y[:], y[:], xD[:, 0:384], ALU.add, [xDt, yt], [yt])
                tt(y[:], y[:], szTM[:, i, :], ALU.mult, [sz_t, yt], [yt])
                ss, sst = sml.get()
                junk, jt = tmpf.get()
                act(junk[:, 0:384], y[:], AF.Square, [yt], [jt, sst], accum_out=ss[:, 0:1])
                act(ss[:, 0:1], ss[:, 0:1], AF.Sqrt, [sst], [sst], scale=1.0 / 384, bias=EPS)
                kb.op("dve", lambda e: e.reciprocal(out=ss[:, 0:1], in_=ss[:, 0:1]), r=[sst], w=[sst])
                yn, ynt = ynb.get()
                stt(yn[:], y[:], ss[:, 0:1], BCt[:, 408:792], ALU.mult, ALU.mult, [yt, sst, bc_t], [ynt])
                psA_, ptA_ = psum("gen")
                for c3 in range(3):
                    kb.op("pe", lambda e: e.transpose(out=psA_[:, c3 * 128:(c3 + 1) * 128],
                                                      in_=yn[:, c3 * 128:(c3 + 1) * 128], identity=IDENT),
                          r=[ynt, cst_t], w=[ptA_])
                vcopy(mixT[:, 5:8, i * 128:(i + 1) * 128], psA_[:, 0:384].rearrange("p (c n) -> p c n", c=3),
                      [ptA_], [mix_t[i]])
            kb.barrier()
            ckpt("C" + ("s" if sample else "p"))
        M.close()
        OPEN.remove(M)

        if sample:
            KS = ExitStack()
            OPEN.append(KS)
            alloc_keys(KS)
            KaT, Va, KbT, Vb = KK["KaT"], KK["Va"], KK["KbT"], KK["Vb"]
            with ExitStack() as Dk:
                ckp = Rot(Dk, "ckp", [128, 512], BF16, 2)
                vstg = Rot(Dk, "vstg", [128, 1024], BF16, 2)
                wk = sb(Dk, "wkvb2", [128, 512], BF16)
                wkV = sb(Dk, "wkvV2", [128, 256], BF16)
                wk_t = Tok()
                kb.dma("pool", wk[:], wkvb_d[l], w=[wk_t])
                for h in range(4):
                    kb.dma("pool", wkV[:, 64 * h:64 * h + 64], wkvb_d[l, :, 128 * h + 64:128 * h + 128], w=[wk_t])
                kb.dma("pool", KaT[:, 0:256], ctxK_d[l], w=[Tok()])
                for h in range(4):
                    kb.dma("pool", KbT[64:96, h, 0:256], ctxP_d[l, 64:96, :], w=[Tok()])
                for t_ in range(2):
                    kb.dma("pool", Va[:, t_, :, 0:64], ctxV_d[l, t_ * 128:(t_ + 1) * 128, :].rearrange("p (g d) -> p g d", g=2), w=[Tok()])
                gv = gv1
                for r_ in range(4):
                    kb.dma("pool", KaT[:, 256 + 1024 * r_:1280 + 1024 * r_], gv[:, r_, 0:1024], r=[gout_t[0]], w=[Tok()])
                    kp_t = Tok()
                    kb.dma("pool", KbT[64:96, 0, 256 + 1024 * r_:1280 + 1024 * r_], gv[64:96, r_, 2048:3072], r=[gout_t[1]], w=[kp_t])
                    for h in range(1, 4):
                        (vcopy if h % 2 else acopy)(KbT[64:96, h, 256 + 1024 * r_:1280 + 1024 * r_], KbT[64:96, 0, 256 + 1024 * r_:1280 + 1024 * r_], [kp_t], [Tok()])
                    vs, vst = vstg.get()
                    kb.dma("pool", vs[:], gv[:, r_, 3072:4096], r=[gout_t[1]], w=[vst])
                    for g in range(2):
                        vcopy(Va[:, 2 + 8 * r_:10 + 8 * r_, g, 0:64], vs[:].rearrange("p (t g d) -> p t g d", t=8, g=2)[:, :, g, :], [vst], [Tok()])
                for pc in range(9):
                    n = 256 if pc == 0 else 512
                    k0 = 0 if pc == 0 else 256 + (pc - 1) * 512
                    cb, cbt = ckp.get()
                    if pc == 0:
                        kb.dma("pool", cb[:, 0:256], ctxC_d[l], w=[cbt])
                    else:
                        r_, hf = (pc - 1) // 2, (pc - 1) % 2
                        kb.dma("pool", cb[:], gv[:, r_, 1024 + 512 * hf:1536 + 512 * hf], r=[gout_t[0]], w=[cbt])
                    for h in range(4):
                        ps2, pt2 = psum("gen")
                        mm(ps2[0:64, 0:n], wk[:, 128 * h:128 * h + 64], cb[:, 0:n], True, True, [wk_t, cbt], [pt2])
                        acopy(KbT[0:64, h, k0:k0 + n], ps2[0:64, 0:n], [pt2], [Tok()])
                    for kt in range(n // 128):
                        ps2, pt2 = psum("gen")
                        mm(ps2[:, 0:256], cb[:, kt * 128:(kt + 1) * 128], wkV[:],
                           True, True, [wk_t, cbt], [pt2])
                        acopy(Vb[:, k0 // 128 + kt, :, 0:64], ps2[:, 0:256].rearrange("p (h d) -> p h d", h=4), [pt2], [Tok()])
                kb.barrier()
                ckpt("Dk")
            attention()
            KS.close()
            OPEN.remove(KS)

        with ExitStack() as Fz:
            Wo = sb(Fz, "Wo", [128, 8, 1024], BF16)
            wo_ts = toks(4)
            wv = wout_d[l].rearrange("(c p) n -> p c n", p=128)
            for c0 in range(0, 8, 2):
                kb.dma("pool", Wo[:, c0:c0 + 2, :], wv[:, c0:c0 + 2, :], w=[wo_ts[c0 // 2]])
            mo = sb(Fz, "mo", [128, 8, 512])
            mo_t = Tok()
            sq = Rot(Fz, "sqf", [128, 512], BF16, 2)
            tmpf = Rot(Fz, "tmpf2", [128, 512], F32, 3)
            rsb = Rot(Fz, "rsb2", [128, 512], F32, 2)
            for ti in range(TG // 512):
                g0 = ti * 512
                for oc in range(8):
                    ps, pt = psum("big")
                    for k in range(8):
                        mm(ps[:, :], Wo[:, k, oc * 128:(oc + 1) * 128], mixT[:, k, g0:g0 + 512], k == 0, k == 7,
                           [wo_ts[k // 2]] + mix_t[ti * 4:ti * 4 + 4], [pt])
                    acopy(mo[:, oc, :], ps[:, :], [pt], [mo_t])
                resid_add(l, 1, mg, t0 + g0, 512, mo, mo_t, sq, tmpf, rsb)
            kb.barrier()
            ckpt("F" + ("s" if sample else "p"))
        L.close()
        OPEN.remove(L)

    def ffn(l):
        with ExitStack() as Gz:
            h2 = sb(Gz, "h2", [128, 8, 768], BF16)
            f1 = sb(Gz, "f1", [128, 32, 768], BF16)
            fo = sb(Gz, "fo", [128, 8, 768])
            W1 = [sb(Gz, f"W1_{i}", [128, 8, 512], BF16) for i in range(2)]
            W2 = [sb(Gz, f"W2_{i}", [128, 32, 128], BF16) for i in range(2)]
            w1_t, w2_t = toks(2), toks(2)
            h2_t, f1_t, fo_t = toks(2), [toks(2) for _ in range(32)], toks(2)
            sq = Rot(Gz, "sqg", [128, 512], BF16, 2)
            tmpf = Rot(Gz, "tmpg", [128, 512], F32, 5)
            rsb = Rot(Gz, "rsg", [128, 512], F32, 2)
            w1v = w1_d[l].rearrange("(c p) n -> p c n", p=128)
            w2v = w2_d[l].rearrange("(c p) n -> p c n", p=128)
            for half in range(2):
                segs = [(0, 512, 0), (512, 256, 1)] if half == 0 else [(1024, 512, 1), (768, 256, 1)]
                hoff = {s[0]: o for s, o in zip(segs, (0, segs[0][1]))}
                kb.dma("pool", W1[0][:], w1v[:, :, 0:512], w=[w1_t[0]])
                for si, (a0, n, mg) in enumerate(segs):
                    o = hoff[a0]
                    norm_mod(l, 2, 24, mg, a0, n, h2[:, :, o:o + n], h2_t[si], sq, tmpf, rsb)
                for pc in range(8):
                    b = pc % 2
                    if pc + 1 < 8:
                        kb.dma("pool", W1[1 - b][:], w1v[:, :, (pc + 1) * 512:(pc + 2) * 512], w=[w1_t[1 - b]])
                    else:
                        kb.dma("pool", W2[0][:], w2v[:, :, 0:128], w=[w2_t[0]])
                    for j in range(4):
                        fc = pc * 4 + j
                        for si, (a0, n, mg) in enumerate(segs):
                            o = hoff[a0]
                            ps, pt = psum("big")
                            for k in range(8):
                                mm(ps[:, :n], W1[b][:, k, j * 128:(j + 1) * 128], h2[:, k, o:o + n], k == 0, k == 7,
                                   [w1_t[b], h2_t[si]], [pt])
                            tf, tft = tmpf.get()
                            act(tf[:, :n], ps[:, :n], AF.Relu, [pt], [tft])
                            tt(f1[:, fc, o:o + n], tf[:, :n], tf[:, :n], ALU.mult, [tft], [f1_t[fc][si]])
                for oc in range(8):
                    b = oc % 2
                    if oc + 1 < 8:
                        kb.dma("pool", W2[1 - b][:], w2v[:, :, (oc + 1) * 128:(oc + 2) * 128], w=[w2_t[1 - b]])
                    for si, (a0, n, mg) in enumerate(segs):
                        o = hoff[a0]
                        ps, pt = psum("big")
                        for k in range(32):
                            mm(ps[:, :n], W2[b][:, k, :], f1[:, k, o:o + n], k == 0, k == 31, [w2_t[b], f1_t[k][si]], [pt])
                        acopy(fo[:, oc, o:o + n], ps[:, :n], [pt], [fo_t[si]])
                for si, (a0, n, mg) in enumerate(segs):
                    o = hoff[a0]
                    resid_add(l, 3, mg, a0, n, fo[:, :, o:o + n], fo_t[si], sq, tmpf, rsb)
            kb.barrier()

    try:
        for l in range(depth):
            mixer_pass(l, False)
            mixer_pass(l, True)
            ffn(l)
    except StopBuild:
        if DEBUG.get("padact"):
            pt_ = Tok()
            for _ in range(DEBUG["padact"]):
                kb.op(DEBUG.get("padeng_name", "act"), lambda e: (e.activation(out=MSK[:, 0:1], in_=MSK[:, 0:1], func=AF.Copy) if DEBUG.get("padeng_name", "act") == "act" else e.tensor_copy(out=MSK[:, 0:1], in_=MSK[:, 0:1])), r=[pt_], w=[])
        kb.barrier()
        return nc, kb

    for c in range(8):
        kb.dma("sp", yT_d[c * 128:(c + 1) * 128, :], xT[:, c, :], r=x_t[c])
    kb.barrier(final=True)
    es.close()
    return nc, kb


def _consts():
    k = np.arange(128)
    ident = np.eye(128, dtype=np.float32)
    U = (k[:, None] <= k[None, :]).astype(np.float32)
    Ub = (k[:, None] >= k[None, :]).astype(np.float32)
    SLm = (k[:, None] > k[None, :]).astype(np.float32)
    SLb = (k[:, None] < k[None, :]).astype(np.float32)
    ones = np.ones((128, 128), np.float32)

    def perm(blk):
        q = blk // 2
        Pm = np.zeros((128, 128), np.float32)
        for b in range(128 // blk):
            for i in range(blk):
                o = b * blk + i
                if i < q:
                    Pm[o + q, o] = -1.0
                else:
                    Pm[o - q, o] = 1.0
        return Pm
    blkm = np.zeros((128, 128), np.float32)
    blkm[:64, :64] = 1
    blkm[64:, 64:] = 1
    return np.concatenate([ident, U, Ub, SLm, SLb, ones, perm(32), perm(16), blkm], axis=1)


def _rope_tables(q):
    t = (1024 * q + np.arange(1024))
    row = (t // 64).astype(np.float32)
    col = (t % 64).astype(np.float32)
    out = []
    for rot_dim, reps in ((64, 2), (32, 4)):
        half = rot_dim // 2
        inv = (1.0 / (np.float32(10000.0) ** (np.arange(0, half, 2, dtype=np.float32) / np.float32(half)))).astype(np.float32)
        angr = row[:, None] * inv[None, :]
        angc = col[:, None] * inv[None, :]
        ang = np.concatenate([angr, angr, angc, angc], axis=1).astype(np.float32)
        cos = np.tile(np.cos(ang).astype(np.float32).T, (reps, 1))
        sin = np.tile(np.sin(ang).astype(np.float32).T, (reps, 1))
        out += [cos, sin]
    return np.ascontiguousarray(np.stack(out, 0).astype(np.float32))


def _pl(v, nch):
    return np.asarray(v, np.float32).reshape(nch, 128).T


def kernel(**inp):
    inp = {k: np.asarray(v) for k, v in inp.items()}
    f = lambda a: np.ascontiguousarray(np.asarray(a, dtype=np.float32))
    nc, kb = build_program(RUN_DEPTH)
    cst = f(_consts())
    vec = np.zeros((128, DEPTH, NV), np.float32)
    bc = np.zeros((DEPTH, 128, NBC), np.float32)
    for l in range(DEPTH):
        vec[:, l, 0:8] = _pl(inp["norm_mix_pre"][l], 8)
        vec[:, l, 8:16] = _pl(inp["norm_mix_post"][l], 8)
        vec[:, l, 16:24] = _pl(inp["norm_ffn_pre"][l], 8)
        vec[:, l, 24:32] = _pl(inp["norm_ffn_post"][l], 8)
        vec[:, l, 32:80] = _pl(inp["b_mod"][l], 48)
        vec[:, l, 80] = np.tile(inp["attn_q_norm"][l], 2)
        vec[:, l, 81] = np.tile(inp["attn_k_norm"][l], 2)
        vec[:, l, 82:84] = _pl(inp["mla_q_norm"][l], 2)
        vec[:, l, 84] = inp["mla_kv_norm"][l]
        vec[:, l, 85:110] = inp["ssm_conv_w"][l].reshape(5, 128, 5).transpose(1, 0, 2).reshape(128, 25)
        vec[:, l, 110:115] = _pl(inp["ssm_conv_b"][l], 5)
        row = np.concatenate([inp["ssm_dt_bias"][l].reshape(12), inp["ssm_a_log"][l].reshape(12),
                              np.repeat(inp["ssm_d"][l], 64), inp["ssm_norm"][l]]).astype(np.float32)
        bc[l] = row[None, :]
    vec = f(vec.reshape(128, DEPTH * NV))
    shared = dict(vec=vec, bc=f(bc), cst=cst, w_in=f(inp["w_in"][:RUN_DEPTH]), w_qb=f(inp["mla_w_qb"][:RUN_DEPTH]),
                  w_kvb=f(inp["mla_w_kvb"][:RUN_DEPTH]), w_out=f(inp["w_out"][:RUN_DEPTH]), w_ffn1=f(inp["w_ffn1"][:RUN_DEPTH]),
                  w_ffn2=f(inp["w_ffn2"][:RUN_DEPTH]))
    in_maps = []
    for r in range(8):
        s, q = r // 4, r % 4
        xT = np.concatenate([inp["x_prompt"][2 * r], inp["x_prompt"][2 * r + 1],
                             inp["x_sample"][s, 1024 * q:1024 * q + 1024]], axis=0).T
        cT = np.zeros((128, 8, 4), np.float32)
        cT[:, :, 0] = _pl(inp["c_ctx"], 8)
        cT[:, :, 1] = _pl(inp["c"][0], 8)
        cT[:, :, 2] = _pl(inp["c"][1], 8)
        cT[:, :, 3] = _pl(inp["c_ctx"], 8)
        msk = np.zeros((128, 18), np.float32)
        msk[:, 16 + s] = 1.0
        for j in range(4):
            msk[:, j] = 1.0 if j < q else 0.0
            msk[:, 4 + j] = 1.0 if j > q else 0.0
            msk[:, 8 + j] = 1.0 if j == q - 1 else 0.0
            msk[:, 12 + j] = 1.0 if j == q + 1 else 0.0
        ck = inp["cache_attn_k"][s].reshape(DEPTH, 256, 128).transpose(0, 2, 1)
        cv = inp["cache_attn_v"][s].reshape(DEPTH, 256, 128)
        cc = inp["cache_mla_ckv"][s].transpose(0, 2, 1)
        cp = np.tile(inp["cache_mla_kpe"][s].transpose(0, 2, 1), (1, 4, 1))
        h0 = inp["state_ssm"][s].reshape(DEPTH, 2, 2, 3, 64, 64).transpose(0, 1, 2, 5, 3, 4).reshape(DEPTH, 2, 128, 192)
        m = dict(shared)
        m.update(xT=f(xT), cT=f(cT.reshape(128, 32)), w_mod=f(inp["w_mod"][:RUN_DEPTH, :, 1536 * q:1536 * q + 1536]), rope=_rope_tables(q), msk=f(msk), ctxK=f(ck), ctxV=f(cv),
                 ctxC=f(cc), ctxP=f(cp), h0=f(h0))
        in_maps.append(m)
    res = run_bass_kernel_spmd(nc, in_maps, core_ids=list(range(8)))
    R = res.results
    yp = np.zeros((16, 256, D), np.float32)
    ys = np.zeros((2, 4096, D), np.float32)
    nk = np.zeros((16, DEPTH, 256, 2, 64), np.float32)
    nv = np.zeros((16, DEPTH, 256, 2, 64), np.float32)
    nckv = np.zeros((16, DEPTH, 256, 128), np.float32)
    nkpe = np.zeros((16, DEPTH, 256, 32), np.float32)
    nssm = np.zeros((16, DEPTH, 2, 6, 64, 64), np.float32)
    for r in range(8):
        s, q = r // 4, r % 4
        yT = np.asarray(R[r]["yT"])
        for j in range(2):
            b = 2 * r + j
            yp[b] = yT[:, 256 * j:256 * j + 256].T
            for l in range(DEPTH):
                nk[b, l] = np.asarray(R[r]["nkT"])[l][:, 256 * j:256 * j + 256].T.reshape(256, 2, 64)
                nv[b, l] = np.asarray(R[r]["nv"])[l][256 * j:256 * j + 256, :].reshape(256, 2, 64)
                nckv[b, l] = np.asarray(R[r]["nckvT"])[l][:, 256 * j:256 * j + 256].T
                nkpe[b, l] = np.asarray(R[r]["nkpeT"])[l][:, 256 * j:256 * j + 256].T
                for d in range(2):
                    a = np.asarray(R[r]["nssm"])[l, j, d]
                    nssm[b, l, d] = a.reshape(2, 64, 3, 64).transpose(0, 2, 3, 1).reshape(6, 64, 64)
        ys[s, 1024 * q:1024 * q + 1024] = yT[:, 512:].T
    return yp, ys, nk, nv, nckv, nkpe, nssm
```

```python
import numpy as np
from contextlib import ExitStack
import concourse.bass as bass
import concourse.mybir as mybir
from concourse.bass_utils import run_bass_kernel_spmd

F32 = mybir.dt.float32
BF16 = mybir.dt.bfloat16
ALU = mybir.AluOpType
AF = mybir.ActivationFunctionType

DEPTH = 4
D = 1024
T = 1536
EPS = 1e-6
NV = 115
NBC = 792
G1W = 4128
G2W = 392
STOP = None
SEM_LIMIT = 2000
NDMA = 12
DEBUG = {}
RUN_DEPTH = DEPTH


class StopBuild(Exception):
    pass


class Sem:
    __slots__ = ("i", "h")

    def __init__(s, i, h):
        s.i = i
        s.h = h


class Tok:
    __slots__ = ("w", "r")

    def __init__(s):
        s.w = None
        s.r = {}


def toks(n):
    return [Tok() for _ in range(n)]


class KB:
    def __init__(s, nc):
        s.nc = nc
        s.es = ExitStack()
        s.E = dict(pe=nc.tensor, act=nc.scalar, dve=nc.vector, pool=nc.gpsimd, sp=nc.sync)
        s.nsem = 0
        s.cur = {}
        s.known = {e: {} for e in s.E}
        s.retired = []
        for e in s.E:
            s.cur[e] = [s.newsem(), 0]
        s.dq = {q: [[s.newsem(), 0] for _ in range(NDMA)] for q in ("sp", "pool")}
        s.dqi = {"sp": 0, "pool": 0}
        s.cc = [s.newsem(), 0]
        s.nins = {e: 0 for e in s.E}

    def newsem(s):
        s.nsem += 1
        return Sem(s.nsem, s.es.enter_context(s.nc.semaphore(f"s{s.nsem}")))

    def _wait(s, eng, deps):
        kn = s.known[eng]
        best = {}
        own = s.cur[eng][0]
        for (sm, v) in deps:
            if eng == "pe" and sm is own:
                continue
            if v > kn.get(sm.i, 0) and v > best.get(sm.i, (None, 0))[1]:
                best[sm.i] = (sm, v)
        for sm, v in best.values():
            s.E[eng].wait_ge(sm.h, v)
            kn[sm.i] = v

    def _deps(s, eng, r, w):
        deps = []
        own = s.cur[eng][0]
        for t in r:
            if t.w:
                deps.append(t.w)
        for t in w:
            if t.w:
                deps.append(t.w)
            for ev in t.r.values():
                if ev[0] is not own or eng != "pe":
                    deps.append(ev)
        return deps

    def _mark(s, ev, r, w):
        for t in r:
            t.r[ev[0].i] = ev
        for t in w:
            t.w = ev
            t.r = {}

    def op(s, eng, fn, r=(), w=()):
        s._wait(eng, s._deps(eng, r, w))
        ins = fn(s.E[eng])
        c = s.cur[eng]
        c[1] += 1
        ins.then_inc(c[0].h, 1)
        s.nins[eng] += 1
        s._mark((c[0], c[1]), r, w)
        if c[1] >= SEM_LIMIT:
            s.retired.append((c[0], c[1]))
            s.cur[eng] = [s.newsem(), 0]

    def dma(s, q, out, in_, r=(), w=()):
        deps = s._deps(q, r, w)
        slot = s.dq[q][s.dqi[q] % NDMA]
        s.dqi[q] += 1
        if slot[1] > 0:
            deps.append((slot[0], slot[1]))
        s._wait(q, deps)
        ins = s.E[q].dma_start(out=out, in_=in_)
        slot[1] += 16
        ins.then_inc(slot[0].h, 16)
        s.nins[q] += 1
        s._mark((slot[0], slot[1]), r, w)

    def allgather(s, gin, gout, r=(), w=(), groups=None):
        s._wait("pool", s._deps("pool", r, w))
        ins = s.nc.gpsimd.collective_compute(
            "AllGather", ALU.bypass, replica_groups=groups or [[0, 1, 2, 3], [4, 5, 6, 7]],
            ins=[gin], outs=[gout])
        s.cc[1] += 1
        ins.then_inc(s.cc[0].h, 1)
        s._mark((s.cc[0], s.cc[1]), r, w)

    def barrier(s, final=False):
        evs = list(s.retired)
        for e in s.E:
            if s.cur[e][1] > 0:
                evs.append((s.cur[e][0], s.cur[e][1]))
        for q in s.dq:
            for sl in s.dq[q]:
                if sl[1] > 0:
                    evs.append((sl[0], sl[1]))
        if final and s.cc[1] > 0:
            evs.append((s.cc[0], s.cc[1]))
        for e in s.E:
            s._wait(e, evs)


def build_program(depth=DEPTH):
    nc = bass.Bass("TRN2", target_bir_lowering=False)
    kb = KB(nc)
    es = kb.es

    def din(name, shape, dt=F32):
        return nc.dram_tensor(name, list(shape), dt, kind="ExternalInput").ap()

    def dout(name, shape, dt=F32):
        return nc.dram_tensor(name, list(shape), dt, kind="ExternalOutput").ap()

    xT_d = din("xT", [D, T])
    cT_d = din("cT", [128, 32])
    vec_d = din("vec", [128, DEPTH * NV])
    bc_d = din("bc", [DEPTH, 128, NBC])
    cst_d = din("cst", [128, 9 * 128])
    rope_d = din("rope", [4, 128, 1024])
    msk_d = din("msk", [128, 18])
    ctxK_d = din("ctxK", [DEPTH, 128, 256])
    ctxV_d = din("ctxV", [DEPTH, 256, 128])
    ctxC_d = din("ctxC", [DEPTH, 128, 256])
    ctxP_d = din("ctxP", [DEPTH, 128, 256])
    h0_d = din("h0", [DEPTH, 2, 128, 192])
    wmod_d = din("w_mod", [depth, D, 1536])
    win_d = din("w_in", [depth, D, 2092])
    wqb_d = din("w_qb", [depth, 256, 384])
    wkvb_d = din("w_kvb", [depth, 128, 512])
    wout_d = din("w_out", [depth, D, D])
    w1_d = din("w_ffn1", [depth, D, 4 * D])
    w2_d = din("w_ffn2", [depth, 4 * D, D])

    yT_d = dout("yT", [D, T])
    nkT_d = dout("nkT", [DEPTH, 128, 512])
    nv_d = dout("nv", [DEPTH, 512, 128])
    nckvT_d = dout("nckvT", [DEPTH, 128, 512])
    nkpeT_d = dout("nkpeT", [DEPTH, 32, 512])
    nssm_d = dout("nssm", [DEPTH, 2, 2, 128, 192])

    g1i = [nc.dram_tensor(f"gin1{k}", [128, w_], F32).ap() for k, w_ in enumerate((2048, 2048, 32))]
    g1o = [nc.dram_tensor(f"gout1{k}", [512, w_], F32).ap() for k, w_ in enumerate((2048, 2048, 32))]

    class _G1:
        def __init__(s, lst, out):
            s.lst = lst
            s.out = out

        def __getitem__(s, key):
            if s.out:
                p, r_, c = key
            else:
                p, c = key
            k = c.start // 2048
            c2 = slice(c.start - 2048 * k, c.stop - 2048 * k)
            if s.out:
                return s.lst[k].rearrange("(r p) c -> p r c", p=128)[p, r_, c2]
            return s.lst[k][p, c2]
    gin1 = _G1(g1i, False)
    gv1 = _G1(g1o, True)
    gmi = nc.dram_tensor("gmi", [128, 192], F32).ap()
    gmo = nc.dram_tensor("gmo", [512, 192], F32).ap()
    gin2 = nc.dram_tensor("gin2", [128, G2W], F32).ap()
    gout2 = nc.dram_tensor("gout2", [512, G2W], F32).ap()
    gin1_t, gout1_t, gin2_t, gout2_t = toks(4)
    gin_t = toks(3)
    gout_t = toks(3)

    uid = [0]

    def sb(stack, name, shape, dt=F32):
        uid[0] += 1
        return stack.enter_context(nc.sbuf_tensor(f"sb{uid[0]}_{name}", list(shape), dt))

    PS = {}
    for pool, n in (("acc", 2), ("S", 3), ("gen", 2)):
        PS[pool] = [[es.enter_context(nc.psum_tensor(f"ps_{pool}{i}", [128, 512], F32)), Tok()] for i in range(n)]
    PS["big"] = PS["acc"] + PS["S"]
    psi = {"acc": 0, "S": 0, "gen": 0, "bf": 0, "big": 0}

    def psum(pool="gen"):
        lst = PS[pool]
        p = lst[psi[pool] % len(lst)]
        psi[pool] += 1
        return p[0], p[1]

    def mm(out, lhsT, rhs, start, stop, r, w):
        kb.op("pe", lambda e: e.matmul(out, lhsT, rhs, start=start, stop=stop), r=r, w=w)

    def act(out, in_, func, r, w, **kw):
        kb.op("act", lambda e: e.activation(out=out, in_=in_, func=func, **kw), r=r, w=w)

    def tt(out, in0, in1, op, r, w):
        kb.op("dve", lambda e: e.tensor_tensor(out=out, in0=in0, in1=in1, op=op), r=r, w=w)

    def stt(out, in0, scalar, in1, op0, op1, r, w):
        kb.op("dve", lambda e: e.scalar_tensor_tensor(out=out, in0=in0, scalar=scalar, in1=in1, op0=op0, op1=op1), r=r, w=w)

    def vcopy(out, in_, r, w):
        kb.op("dve", lambda e: e.tensor_copy(out=out, in_=in_), r=r, w=w)

    def acopy(out, in_, r, w):
        kb.op("act", lambda e: e.activation(out=out, in_=in_, func=AF.Copy), r=r, w=w)

    P = ExitStack()
    es.enter_context(P)
    xT = sb(P, "xT", [128, 8, T])
    x_t = [toks(6) for _ in range(8)]

    def xtk(c, a, n):
        return [x_t[c][b] for b in range(a // 256, (a + n + 255) // 256)]

    CF = sb(P, "CF", [128, 9, 128])
    CB = sb(P, "CB", [128, 3, 128], BF16)
    VEC = sb(P, "VEC", [128, DEPTH, NV])
    MSK = sb(P, "MSK", [128, 18])
    cT = sb(P, "cT", [128, 8, 4])
    MOD = sb(P, "MOD", [128, DEPTH, 48, 2])
    DER = sb(P, "DER", [128, DEPTH, 4, 8, 2])
    cst_t, vec_t, msk_t, c_t, mod_t, der_t = toks(6)
    IDENT, U_, UB, SL, SLB, ONES, PERMA, PERMB, BLK = [CF[:, i, :] for i in range(9)]
    ONESB, BLKB, IDENTB = [CB[:, i, :] for i in range(3)]

    for c in range(8):
        kb.dma("sp", xT[:, c, :], xT_d[c * 128:(c + 1) * 128, :], w=x_t[c])
    kb.dma("sp", CF[:], cst_d.rearrange("p (a b) -> p a b", a=9), w=[cst_t])
    kb.dma("sp", VEC[:], vec_d.rearrange("p (a b) -> p a b", a=DEPTH), w=[vec_t])
    kb.dma("sp", MSK[:], msk_d, w=[msk_t])
    kb.dma("sp", cT[:], cT_d.rearrange("p (a b) -> p a b", a=8), w=[c_t])
    vcopy(CB[:, 0, :], ONES, [cst_t], [cst_t])
    vcopy(CB[:, 1, :], BLK, [cst_t], [cst_t])
    vcopy(CB[:, 2, :], IDENT, [cst_t], [cst_t])
    act(cT[:], cT[:], AF.Silu, [c_t], [c_t])

    with ExitStack() as S0:
        wm = [sb(S0, f"wm{i}", [128, 8, 384]) for i in range(2)]
        wm_t = toks(2)
        GMs = sb(S0, "GMs", [128, 192])
        GM = sb(S0, "GM", [128, 4, 192])
        TM0 = sb(S0, "TM0", [128, 4, 12])
        gms_t, gm_t, gmi_t, gmo_t, tm0_t = toks(5)
        ps, pt = psum("gen")
        for l in range(depth):
            for pc in range(4):
                b = pc % 2
                kb.dma("sp", wm[b][:], wmod_d[l].rearrange("(c p) n -> p c n", p=128)[:, :, pc * 384:(pc + 1) * 384], w=[wm_t[b]])
                for j in range(3):
                    ch = l * 12 + pc * 3 + j
                    for k in range(8):
                        mm(ps[:, ch * 4:ch * 4 + 4], wm[b][:, k, j * 128:(j + 1) * 128], cT[:, k, :], k == 0, k == 7,
                           [wm_t[b], c_t], [pt])
        vcopy(GMs[:, 0:48 * depth], ps[:, 0:48 * depth], [pt], [gms_t])
        kb.dma("sp", gmi[:, 0:48 * depth], GMs[:, 0:48 * depth], r=[gms_t], w=[gmi_t])
        kb.allgather(gmi, gmo, r=[gmi_t], w=[gmo_t])
        kb.dma("sp", GM[:], gmo.rearrange("(r p) c -> p r c", p=128), r=[gmo_t], w=[gm_t])
        for l in range(depth):
            gl = GM[:, :, 48 * l:48 * l + 48].rearrange("p r (j v) -> p r j v", v=4)
            bm = VEC[:, l, 32:80].rearrange("p (r j) -> p r j", r=4)
            mo = MOD[:, l].rearrange("p (r j) g -> p r j g", r=4)
            tt(mo[:, :, :, 0], gl[:, :, :, 0], bm, ALU.add, [gm_t, vec_t], [mod_t])
            kb.op("dve", lambda e: e.tensor_scalar_mul(out=TM0[:], in0=gl[:, :, :, 1], scalar1=MSK[:, 16:17]), r=[gm_t, msk_t], w=[tm0_t])
            stt(TM0[:], gl[:, :, :, 2], MSK[:, 17:18], TM0[:], ALU.mult, ALU.add, [gm_t, msk_t, tm0_t], [tm0_t])
            tt(mo[:, :, :, 1], TM0[:], bm, ALU.add, [tm0_t, vec_t, mod_t], [mod_t])
            for i, (gcol, mch, plus1) in enumerate(((0, 8, True), (8, 16, False), (16, 32, True), (24, 40, False))):
                gb = VEC[:, l, gcol:gcol + 8].unsqueeze(2).broadcast_to([128, 8, 2])
                if plus1:
                    stt(DER[:, l, i], MOD[:, l, mch:mch + 8, :], 1.0, gb, ALU.add, ALU.mult, [mod_t, vec_t], [der_t])
                else:
                    tt(DER[:, l, i], MOD[:, l, mch:mch + 8, :], gb, ALU.mult, [mod_t, vec_t], [der_t])
        kb.barrier()

    OPEN = []

    def ckpt(name):
        if STOP == name:
            raise StopBuild()

    class Rot:
        def __init__(s, stack, name, shape, dt, n):
            s.b = [sb(stack, f"{name}{i}", shape, dt) for i in range(n)]
            s.t = toks(n)
            s.i = 0

        def get(s):
            j = s.i % len(s.b)
            s.i += 1
            return s.b[j], s.t[j]

    def stats(srcs, n, ones_b, inv, sq, rs_out, rs_tok):
        ps, pt = psum("gen")
        for i, (ap, tk) in enumerate(srcs):
            q, qt = sq.get()
            act(q[:, :n], ap, AF.Square, tk, [qt])
            mm(ps[:, :n], ones_b, q[:, :n], i == 0, i == len(srcs) - 1, [qt, cst_t], [pt])
        act(rs_out, ps[:, :n], AF.Ln, [pt], [rs_tok], scale=inv, bias=EPS)
        act(rs_out, rs_out, AF.Exp, [rs_tok], [rs_tok], scale=-0.5)

    def norm_mod(l, di_scale, shift_ch, mg, a0, n, hT, h_tok, sq, tmpf, rsb):
        rs, rst = rsb.get()
        stats([(xT[:, c, a0:a0 + n], xtk(c, a0, n)) for c in range(8)], n, ONESB, 1.0 / D, sq, rs[:, :n], rst)
        for c in range(8):
            tf, tft = tmpf.get()
            stt(tf[:, :n], xT[:, c, a0:a0 + n], DER[:, l, di_scale, c, mg:mg + 1], rs[:, :n], ALU.mult, ALU.mult,
                xtk(c, a0, n) + [rst, der_t], [tft])
            act(hT[:, c, :n], tf[:, :n], AF.Identity, [tft, mod_t], [h_tok], bias=MOD[:, l, shift_ch + c, mg:mg + 1], scale=1.0)

    def resid_add(l, di_gate, mg, a0, n, src, src_tok, sq, tmpf, rsb):
        rs, rst = rsb.get()
        stats([(src[:, c, :n], [src_tok]) for c in range(8)], n, ONESB, 1.0 / D, sq, rs[:, :n], rst)
        for c in range(8):
            tf, tft = tmpf.get()
            stt(tf[:, :n], src[:, c, :n], DER[:, l, di_gate, c, mg:mg + 1], rs[:, :n], ALU.mult, ALU.mult,
                [src_tok, rst, der_t], [tft])
            tt(xT[:, c, a0:a0 + n], xT[:, c, a0:a0 + n], tf[:, :n], ALU.add, xtk(c, a0, n) + [tft], xtk(c, a0, n))

    def mixer_pass(l, sample):
        t0 = 512 if sample else 0
        TG = 1024 if sample else 512
        ntile = TG // 128
        mg = 1 if sample else 0
        Tp = 1028 if sample else 520
        W = Tp - 4
        NK = 4352 if sample else 512
        NKT = NK // 128

        def acol(i):
            return i * 128 if sample else (i // 2) * 260 + (i % 2) * 128

        ckpt("P0")
        L = ExitStack()
        OPEN.append(L)
        mixT = sb(L, "mixT", [128, 8, TG], BF16)
        mix_t = toks(ntile)
        QaT = sb(L, "QaT", [128, 3, TG], BF16)
        QbT = sb(L, "QbT", [128, 4, TG], BF16)
        q_t = toks(3)
        BCt = sb(L, "BCt", [128, NBC])
        bc_t = Tok()
        kb.dma("sp", BCt[:], bc_d[l], w=[bc_t])
        key_t = Tok()
        KK = {}

        def alloc_keys(stack):
            KK["KaT"] = sb(stack, "KaT", [128, NK], BF16)
            KK["Va"] = sb(stack, "Va", [128, NKT, 2, 65], BF16)
            KK["KbT"] = sb(stack, "KbT", [128, 4, NK], BF16)
            KK["Vb"] = sb(stack, "Vb", [128, NKT, 4, 65], BF16)
            kb.op("dve", lambda e: e.memset(KK["Va"][:, :, :, 64:65], 1.0), w=[key_t])
            kb.op("dve", lambda e: e.memset(KK["Vb"][:, :, :, 64:65], 1.0), w=[key_t])

        if not sample:
            alloc_keys(L)

        M = ExitStack()
        OPEN.append(M)
        xpad = sb(M, "xpad", [128, 5, Tp])
        xpad_t = Tok()
        szTM = sb(M, "szTM", [128, ntile, 384], BF16)
        dtTM = sb(M, "dtTM", [128, ntile, 12])
        dA = sb(M, "dA", [128, ntile, 12])
        sz_t, dt_t = toks(2)

        with ExitStack() as A:
            Win = sb(A, "Win", [128, 8, 2092], BF16)
            Wkp = sb(A, "Wkp", [128, 8, 128], BF16)
            Wqb = sb(A, "Wqb", [128, 2, 384], BF16)
            Wkvb = sb(A, "Wkvb", [128, 512], BF16)
            WkvV = sb(A, "WkvV", [128, 256], BF16)
            Wqa = sb(A, "Wqa", [128, 8, 3, 128], BF16)
            w_t = []

            def wtk():
                w_t.append(Tok())
                return [w_t[-1]]
            wv = win_d[l].rearrange("(c p) n -> p c n", p=128)
            for j in range(3):
                for a_ in range(2):
                    hh = j + 3 * a_
                    kb.dma("pool", Wqa[:, :, j, 64 * a_:64 * a_ + 64], wv[:, :, 64 * hh:64 * hh + 64], w=wtk())
            for h in range(4):
                kb.dma("pool", WkvV[:, 64 * h:64 * h + 64], wkvb_d[l, :, 128 * h + 64:128 * h + 128], w=wtk())
            for c0 in range(0, 8, 2):
                kb.dma("pool", Win[:, c0:c0 + 2, :], wv[:, c0:c0 + 2, :], w=wtk())
            for j in range(4):
                kb.dma("pool", Wkp[:, :, 32 * j:32 * j + 32], wv[:, :, 1024:1056], w=wtk())
            kb.dma("pool", Wqb[:], wqb_d[l].rearrange("(c p) n -> p c n", p=128), w=wtk())
            kb.dma("pool", Wkvb[:], wkvb_d[l], w=wtk())
            hT = sb(A, "hT", [128, 8, 512], BF16)
            h_t = Tok()
            sq = Rot(A, "sq", [128, 512], BF16, 2)
            tmpf = Rot(A, "tmpf", [128, 512], F32, 4)
            rsb = Rot(A, "rsb", [128, 512], F32, 2)
            qcf = sb(A, "qcf", [128, 2, 512])
            qcn = sb(A, "qcn", [128, 2, 512], BF16)
            qc_t, qcn_t = toks(2)
            ckb = sb(A, "ckb", [128, 512], BF16)
            ck_t = Tok()
            if sample:
                rope = sb(A, "rope", [128, 4, 512])
                rope_t = Tok()

            def proj(lhs_fn, nk=8, rhs_fn=None, pool="big", m=128):
                ps, pt = psum(pool)
                for k in range(nk):
                    mm(ps[0:m, :], lhs_fn(k), hT[:, k, :] if rhs_fn is None else rhs_fn(k), k == 0, k == nk - 1,
                       w_t + [h_t] if rhs_fn is None else w_t + [qcn_t], [pt])
                return ps, pt

            def headnorm(ps, pt, gcol):
                rs, rst = rsb.get()
                stats([(ps[:, :], [pt])], 512, BLKB, 1.0 / 64, sq, rs[:, :], rst)
                o, ot = tmpf.get()
                stt(o[:], ps[:, :], VEC[:, l, gcol:gcol + 1], rs[:], ALU.mult, ALU.mult, [pt, rst, vec_t], [ot])
                return o, ot

            def do_rope(src, st, perm, ci, out, r, w):
                ps, pt = psum("gen")
                mm(ps[:, :], perm, src[:], True, True, [st, cst_t], [pt])
                a, at = tmpf.get()
                tt(a[:], src[:], rope[:, ci, :], ALU.mult, [st, rope_t], [at])
                b, bt = tmpf.get()
                tt(b[:], ps[:, :], rope[:, ci + 1, :], ALU.mult, [pt, rope_t], [bt])
                tt(out, a[:], b[:], ALU.add, [at, bt] + list(r), list(w))

            for ti in range(TG // 512):
                g0 = ti * 512
                a0 = t0 + g0
                if sample:
                    kb.dma("sp", rope[:], rope_d[:, :, g0:g0 + 512].rearrange("a p n -> p a n"), w=[rope_t])
                norm_mod(l, 0, 0, mg, a0, 512, hT, h_t, sq, tmpf, rsb)
                if sample and DEBUG.get('as') == 0 and ti == DEBUG.get('as_ti', 0):
                    kb.barrier()
                    raise StopBuild()
                for j in range(3):
                    ps, pt = proj(lambda k: Wqa[:, k, j, :])
                    o, ot = headnorm(ps, pt, 80)
                    if sample:
                        do_rope(o, ot, PERMA, 0, QaT[:, j, g0:g0 + 512], [], [q_t[0]])
                    else:
                        vcopy(QaT[:, j, g0:g0 + 512], o[:], [ot], [q_t[0]])
                if sample and DEBUG.get('as') == 1 and ti == DEBUG.get('as_ti', 0):
                    kb.barrier()
                    raise StopBuild()
                ps, pt = proj(lambda k: Win[:, k, 384:512])
                o, ot = headnorm(ps, pt, 81)
                if sample:
                    o2, o2t = tmpf.get()
                    do_rope(o, ot, PERMA, 0, o2[:], [], [o2t])
                    kb.dma("sp", gin1[:, g0:g0 + 512], o2[:], r=[o2t], w=[gin_t[0]])
                else:
                    kb.dma("sp", nkT_d[l, :, g0:g0 + 512], o[:], r=[ot])
                    vcopy(KK["KaT"][:, g0:g0 + 512], o[:], [ot], [key_t])
                if sample and DEBUG.get('as') == 2 and ti == DEBUG.get('as_ti', 0):
                    kb.barrier()
                    raise StopBuild()
                for i in range(2):
                    ps, pt = proj(lambda k: Win[:, k, 640 + 128 * i:768 + 128 * i])
                    acopy(qcf[:, i, :], ps[:, :], [pt], [qc_t])
                rs, rst = rsb.get()
                stats([(qcf[:, i, :], [qc_t]) for i in range(2)], 512, ONESB, 1.0 / 256, sq, rs[:], rst)
                for i in range(2):
                    stt(qcn[:, i, :], qcf[:, i, :], VEC[:, l, 82 + i:83 + i], rs[:], ALU.mult, ALU.mult, [qc_t, rst, vec_t], [qcn_t])
                for h in range(4):
                    ps, pt = proj(lambda k: Wqb[:, k, 96 * h:96 * h + 96], nk=2, rhs_fn=lambda k: qcn[:, k, :], m=96)
                    if sample:
                        o, ot = tmpf.get()
                        acopy(o[0:96, :], ps[0:96, :], [pt], [ot])
                        vcopy(QbT[0:64, h, g0:g0 + 512], o[0:64, :], [ot], [q_t[1]])
                        kb.op("dve", lambda e: e.memset(o[96:128, :], 0.0), r=[ot], w=[ot])
                        ps3, pt3 = psum("gen")
                        mm(ps3[:, :], PERMB, o[:, :], True, True, [ot, cst_t], [pt3])
                        a_, at_ = tmpf.get()
                        tt(a_[64:96, :], o[64:96, :], rope[64:96, 2, :], ALU.mult, [ot, rope_t], [at_])
                        b_, bt_ = tmpf.get()
                        tt(b_[64:96, :], ps3[64:96, :], rope[64:96, 3, :], ALU.mult, [pt3, rope_t], [bt_])
                        tt(QbT[64:96, h, g0:g0 + 512], a_[64:96, :], b_[64:96, :], ALU.add, [at_, bt_], [q_t[1]])
                    else:
                        acopy(QbT[0:96, h, g0:g0 + 512], ps[0:96, :], [pt], [q_t[1]])
                if sample and DEBUG.get('as') == 3 and ti == DEBUG.get('as_ti', 0):
                    kb.barrier()
                    raise StopBuild()
                ps, pt = proj(lambda k: Win[:, k, 896:1024])
                rs, rst = rsb.get()
                stats([(ps[:, :], [pt])], 512, ONESB, 1.0 / 128, sq, rs[:], rst)
                o, ot = tmpf.get()
                stt(o[:], ps[:, :], VEC[:, l, 84:85], rs[:], ALU.mult, ALU.mult, [pt, rst, vec_t], [ot])
                if sample:
                    kb.dma("sp", gin1[:, 1024 + g0:1024 + g0 + 512], o[:], r=[ot], w=[gin_t[0]])

                else:
                    kb.dma("sp", nckvT_d[l, :, g0:g0 + 512], o[:], r=[ot])
                    vcopy(ckb[:], o[:], [ot], [ck_t])
                    for h in range(4):
                        ps2, pt2 = psum("gen")
                        mm(ps2[0:64, :], Wkvb[:, 128 * h:128 * h + 64], ckb[:], True, True, w_t + [ck_t], [pt2])
                        acopy(KK["KbT"][0:64, h, g0:g0 + 512], ps2[0:64, :], [pt2], [key_t])
                    for kt in range(4):
                        ps2, pt2 = psum("gen")
                        mm(ps2[:, 0:256], ckb[:, kt * 128:(kt + 1) * 128], WkvV[:],
                           True, True, w_t + [ck_t], [pt2])
                        acopy(KK["Vb"][:, kt, :, 0:64], ps2[:, 0:256].rearrange("p (h d) -> p h d", h=4), [pt2], [key_t])
                if sample and DEBUG.get('as') == 4 and ti == DEBUG.get('as_ti', 0):
                    kb.barrier()
                    raise StopBuild()
                ps, pt = proj(lambda k: Wkp[:, k, :])
                o, ot = tmpf.get()
                acopy(o[:], ps[:, :], [pt], [ot])
                if sample:
                    o2, o2t = tmpf.get()
                    do_rope(o, ot, PERMB, 2, o2[:], [], [o2t])
                    kb.dma("sp", gin1[:, 2048 + g0:2048 + g0 + 512], o2[:], r=[o2t], w=[gin_t[1]])
                else:
                    kb.dma("sp", nkpeT_d[l, :, g0:g0 + 512], o[0:32, :], r=[ot])
                    for h in range(4):
                        vcopy(KK["KbT"][64:96, h, g0:g0 + 512], o[64:96, :], [ot], [key_t])
                if sample and DEBUG.get('as') == 5 and ti == DEBUG.get('as_ti', 0):
                    kb.barrier()
                    raise StopBuild()
                for i in range(5):
                    ps, pt = proj(lambda k: Win[:, k, 1440 + 128 * i:1568 + 128 * i])
                    if sample:
                        acopy(xpad[:, i, 2 + g0:2 + g0 + 512], ps[:, :], [pt], [xpad_t])
                    else:
                        acopy(xpad[:, i, :].rearrange("p (s w) -> p s w", s=2)[:, :, 2:258],
                              ps[:, :].rearrange("p (s w) -> p s w", s=2), [pt], [xpad_t])
                if sample and DEBUG.get('as') == 6 and ti == DEBUG.get('as_ti', 0):
                    kb.barrier()
                    raise StopBuild()
                if sample and ti == TG // 512 - 1:
                    HS = sb(A, "HS", [128, 5, 4])
                    hs_t = Tok()
                    vcopy(HS[:, :, 0:2], xpad[:, :, 2:4], [xpad_t], [hs_t])
                    vcopy(HS[:, :, 2:4], xpad[:, :, 1024:1026], [xpad_t], [hs_t])
                    kb.dma("sp", gin1[:, 4096:4116], HS[:].rearrange("p c j -> p (c j)"), r=[hs_t], w=[gin_t[2]])
                    kb.allgather(g1i[2], g1o[2], r=[gin_t[2]], w=[gout_t[2]])
                    kb.allgather(g1i[0], g1o[0], r=[gin_t[0]], w=[gout_t[0]])
                for i4 in range(4):
                    gi = ti * 4 + i4
                    psA, ptA = psum("gen")
                    for k in range(8):
                        mm(psA[:, 0:128], hT[:, k, i4 * 128:(i4 + 1) * 128], Win[:, k, 512:640], k == 0, k == 7, w_t + [h_t], [ptA])
                    for k in range(8):
                        mm(psA[:, 128:140], hT[:, k, i4 * 128:(i4 + 1) * 128], Win[:, k, 2080:2092], k == 0, k == 7, w_t + [h_t], [ptA])
                    psZ, ptZ = psum("gen")
                    for k in range(8):
                        mm(psZ[:, 0:384], hT[:, k, i4 * 128:(i4 + 1) * 128], Win[:, k, 1056:1440], k == 0, k == 7, w_t + [h_t], [ptZ])
                    sk = DEBUG.get("skip2", 0) if (sample and gi == 2) else 0
                    o, ot = tmpf.get()
                    if not sk & 1:
                        vcopy(o[:, 0:128], psA[:, 0:128], [ptA], [ot])
                    if sample:
                        if not (DEBUG.get("skipv") and gi >= 2):
                            kb.dma("sp", gin1[:, 3072 + gi * 128:3072 + (gi + 1) * 128], o[:, 0:128], r=[ot], w=[gin_t[1]])
                    else:
                        kb.dma("sp", nv_d[l, gi * 128:(gi + 1) * 128, :], o[:, 0:128], r=[ot])
                        vcopy(KK["Va"][:, gi, :, 0:64], o[:, 0:128].rearrange("p (g d) -> p g d", g=2), [ot], [key_t])
                    if not sk & 2:
                        tt(dtTM[:, gi, :], psA[:, 128:140], BCt[:, 0:12], ALU.add, [ptA, bc_t], [dt_t])
                    if not sk & 4:
                        act(szTM[:, gi, :], psZ[:, 0:384], AF.Silu, [ptZ], [sz_t])
                    if sample and DEBUG.get('as') == 9 and i4 == DEBUG.get('as_i4', 3):
                        kb.barrier()
                        raise StopBuild()
            if sample and DEBUG.get('as') == 7:
                kb.barrier()
                raise StopBuild()
            nd = ntile * 12
            dtf = dtTM[:].rearrange("p a b -> p (a b)")
            sp1, sp1t = tmpf.get()
            act(sp1[:, :nd], dtf, AF.Abs, [dt_t], [sp1t])
            act(sp1[:, :nd], sp1[:, :nd], AF.Exp, [sp1t], [sp1t], scale=-1.0)
            act(sp1[:, :nd], sp1[:, :nd], AF.Ln, [sp1t], [sp1t], bias=1.0, scale=1.0)
            stt(dtf, dtf, 0.0, sp1[:, :nd], ALU.max, ALU.add, [dt_t, sp1t], [dt_t])
            act(BCt[:, 12:24], BCt[:, 12:24], AF.Exp, [bc_t], [bc_t])
            stt(dA[:], dtTM[:], -1.0, BCt[:, 12:24].unsqueeze(1).broadcast_to([128, ntile, 12]), ALU.mult, ALU.mult,
                [dt_t, bc_t], [dt_t])
            if sample and DEBUG.get('as') == 8:
                kb.barrier()
                raise StopBuild()
            if sample:
                kb.allgather(g1i[1], g1o[1], r=[gin_t[1]], w=[gout_t[1]])
            kb.barrier()
            ckpt("A" + ("s" if sample else "p"))

        def attention():
            KaT, Va, KbT, Vb = KK["KaT"], KK["Va"], KK["KbT"], KK["Vb"]
            with ExitStack() as B:
                Pb = Rot(B, "Pb", [128, 512], BF16, 4)
                oS = Rot(B, "oS", [128, 640], F32, 2)
                rc = Rot(B, "rc", [128, 16], F32, 2)
                for qt in range(ntile):
                    q0 = qt * 128
                    if sample:
                        kts = list(range(NKT))
                    else:
                        kts = [2 * (qt // 2), 2 * (qt // 2) + 1]
                    groups = [kts[i:i + 4] for i in range(0, len(kts), 4)]
                    ob, obt = oS.get()
                    for mixer in range(2):
                        if DEBUG.get("skip_mixer") == mixer:
                            continue
                        nh = 6 if mixer == 0 else 4
                        psO, ptO = psum("acc")
                        items = [(h, grp) for h in range(nh) for grp in groups]

                        def emit_S(h, grp):
                            psS, ptS = psum("S")
                            for i, kt in enumerate(grp):
                                kc = slice(kt * 128, (kt + 1) * 128)
                                if mixer == 0:
                                    g = h // 3
                                    j = h % 3
                                    mm(psS[:, i * 128:(i + 1) * 128], KaT[64 * g:64 * g + 64, kc], QaT[64 * g:64 * g + 64, j, q0:q0 + 128],
                                       True, True, [key_t, q_t[0]], [ptS])
                                else:
                                    mm(psS[:, i * 128:(i + 1) * 128], KbT[0:96, h, kc], QbT[0:96, h, q0:q0 + 128],
                                       True, True, [key_t, q_t[1]], [ptS])
                            n = len(grp) * 128
                            pb, pbt = Pb.get()
                            act(pb[:, :n], psS[:, :n], AF.Exp, [ptS], [pbt], scale=(0.125 if mixer == 0 else 96.0 ** -0.5))
                            return pb, pbt

                        def emit_PV(h, grp, pb, pbt):
                            for i, kt in enumerate(grp):
                                v = Va[:, kt, h // 3, :] if mixer == 0 else Vb[:, kt, h, :]
                                mm(psO[:, 65 * h:65 * h + 65], pb[:, i * 128:(i + 1) * 128], v,
                                   kt == kts[0], kt == kts[-1], [pbt, key_t], [ptO])

                        DEPTH_P = 2
                        pend = [emit_S(*items[k]) for k in range(min(DEPTH_P, len(items)))]
                        for ii, (h, grp) in enumerate(items):
                            cur = pend.pop(0)
                            if ii + DEPTH_P < len(items):
                                pend.append(emit_S(*items[ii + DEPTH_P]))
                            emit_PV(h, grp, *cur)
                        r_, rt = rc.get()
                        pv = psO[:, 0:65 * nh].rearrange("p (h d) -> p h d", h=nh)
                        kb.op("dve", lambda e: e.reciprocal(out=r_[:, 0:nh], in_=pv[:, :, 64]), r=[ptO], w=[rt])
                        off = 0 if mixer == 0 else 384
                        tt(ob[:, off:off + 64 * nh].rearrange("p (h d) -> p h d", h=nh), pv[:, :, 0:64],
                           r_[:, 0:nh].unsqueeze(2).broadcast_to([128, nh, 64]), ALU.mult, [ptO, rt], [obt])
                    psA_, ptA_ = psum("gen")
                    psB_, ptB_ = psum("gen")
                    for c5 in range(5):
                        dst, dt_ = (psA_[:, c5 * 128:(c5 + 1) * 128], ptA_) if c5 < 4 else (psB_[:, 0:128], ptB_)
                        kb.op("pe", lambda e: e.transpose(out=dst, in_=ob[:, c5 * 128:(c5 + 1) * 128], identity=IDENT),
                              r=[obt, cst_t], w=[dt_])
                    vcopy(mixT[:, 0:4, q0:q0 + 128], psA_[:, 0:512].rearrange("p (c n) -> p c n", c=4), [ptA_], [mix_t[qt]])
                    vcopy(mixT[:, 4, q0:q0 + 128], psB_[:, 0:128], [ptB_], [mix_t[qt]])
                kb.barrier()
                ckpt("B" + ("s" if sample else "p"))

        if not sample:
            attention()

        with ExitStack() as C:
            xTM = sb(C, "xTM", [128, ntile, 384])
            BTM = sb(C, "BTM", [128, ntile, 128], BF16)
            BT = sb(C, "BT", [128, W], BF16)
            CT = sb(C, "CT", [128, W], BF16)
            St = sb(C, "St", [128, ntile, 2, 192])
            Hbf = sb(C, "Hbf", [128, ntile, 2, 192], BF16)
            EX = sb(C, "EX", [128, ntile, 36])
            CDm = sb(C, "CDm", [128, ntile, 2, 3])
            bt_t, cd_t = toks(2)
            xtm_t = toks(ntile)
            btm_t = toks(ntile)
            ex_t = toks(ntile)
            st_t = [toks(2) for _ in range(ntile)]
            hb_t = [toks(2) for _ in range(ntile)]
            tmpf = Rot(C, "tmpc", [128, 768], F32, 3)
            xwr = Rot(C, "xwr", [128, 384], BF16, 2)
            sml = Rot(C, "sml", [128, 64], F32, 4)
            hw = Rot(C, "hw", [128, 192], F32, 4)
            with ExitStack() as C1:
                acc = sb(C1, "acc", [128, W])
                xc = sb(C1, "xc", [128, W])
                acc_t, xc_t = toks(2)
                if sample:
                    HL = sb(C1, "HL", [128, 4, 20])
                    hl_t = Tok()
                    kb.dma("sp", HL[:], gv1[:, :, 4096:4116], r=[gout_t[2]], w=[hl_t])
                    HLv = HL[:].rearrange("p r (c j) -> p r c j", j=4)
                    for side in range(2):
                        dst = xpad[:, :, 0:2] if side == 0 else xpad[:, :, 1026:1028]
                        for j in range(4):
                            src = HLv[:, j, :, 2:4] if side == 0 else HLv[:, j, :, 0:2]
                            mcol = MSK[:, 8 + 4 * side + j:9 + 4 * side + j]
                            if j == 0:
                                kb.op("dve", lambda e: e.tensor_scalar_mul(out=dst, in0=src, scalar1=mcol), r=[hl_t, msk_t], w=[xpad_t])
                            else:
                                stt(dst, src, mcol, dst, ALU.mult, ALU.add, [hl_t, msk_t, xpad_t], [xpad_t])
                else:
                    for (a, b) in ((0, 2), (258, 262), (518, 520)):
                        kb.op("dve", lambda e: e.memset(xpad[:, :, a:b], 0.0), w=[xpad_t])
                for c in range(5):
                    cw = lambda j: VEC[:, l, 85 + c * 5 + j:86 + c * 5 + j]
                    kb.op("dve", lambda e: e.tensor_scalar_mul(out=acc[:], in0=xpad[:, c, 0:W], scalar1=cw(0)), r=[xpad_t, vec_t], w=[acc_t])
                    for j in range(1, 5):
                        stt(acc[:], xpad[:, c, j:j + W], cw(j), acc[:], ALU.mult, ALU.add, [xpad_t, vec_t, acc_t], [acc_t])
                    act(xc[:], acc[:], AF.Silu, [acc_t, vec_t], [xc_t], bias=VEC[:, l, 110 + c:111 + c], scale=1.0)
                    if c < 4:
                        for i in range(ntile):
                            ps, pt = psum("gen")
                            kb.op("pe", lambda e: e.transpose(out=ps[:, 0:128], in_=xc[:, acol(i):acol(i) + 128], identity=IDENT),
                                  r=[xc_t, cst_t], w=[pt])
                            if c < 3:
                                acopy(xTM[:, i, c * 128:(c + 1) * 128], ps[:, 0:128], [pt], [xtm_t[i]])
                            else:
                                acopy(BTM[:, i, :], ps[:, 0:128], [pt], [btm_t[i]])
                    if c == 3:
                        vcopy(BT[:], xc[:], [xc_t], [bt_t])
                    if c == 4:
                        vcopy(CT[:], xc[:], [xc_t], [bt_t])
            kb.barrier()
            ckpt("C1")
            for i in range(ntile):
                psM, ptM = psum("gen")
                for (c0, lhs, d0, n) in ((0, U_, 0, 6), (6, UB, 6, 6), (12, SL, 0, 6), (18, SLB, 6, 6), (24, ONES, 0, 12)):
                    mm(psM[:, c0:c0 + n], lhs, dA[:, i, d0:d0 + n], True, True, [dt_t, cst_t], [ptM])
                act(EX[:, i, :], psM[:, 0:36], AF.Exp, [ptM], [ex_t[i]])
                wdt, wdtt = sml.get()
                tt(wdt[:, 0:12], EX[:, i, 12:24], dtTM[:, i, :], ALU.mult, [ex_t[i], dt_t], [wdtt])
                for d in range(2):
                    xwb, xwt = xwr.get()
                    tt(xwb[:].rearrange("p (h d) -> p h d", h=6), xTM[:, i, :].rearrange("p (h d) -> p h d", h=6),
                       wdt[:, 6 * d:6 * d + 6].unsqueeze(2).broadcast_to([128, 6, 64]), ALU.mult, [xtm_t[i], wdtt], [xwt])
                    psT, ptT = psum("gen")
                    for g in range(2):
                        mm(psT[64 * g:64 * g + 64, 0:192], BTM[:, i, 64 * g:64 * g + 64], xwb[:, 192 * g:192 * g + 192], True, True,
                           [btm_t[i], xwt], [ptT])
                    acopy(St[:, i, d, :], psT[:, 0:192], [ptT], [st_t[i][d]])
            for d in range(2):
                for g in range(2):
                    vcopy(CDm[64 * g:64 * g + 64, :, d, :], EX[64 * g:64 * g + 64, :, 24 + 6 * d + 3 * g:27 + 6 * d + 3 * g], ex_t, [cd_t])

            ckpt("C2")

            def step(out, h, i, d, r_extra=(), w_extra=()):
                tt(out.rearrange("p (h d) -> p h d", h=3), h.rearrange("p (h d) -> p h d", h=3),
                   CDm[:, i, d, :].unsqueeze(2).broadcast_to([128, 3, 64]), ALU.mult, [cd_t] + list(r_extra), list(w_extra))
                tt(out, out, St[:, i, d, :], ALU.add, [st_t[i][d]] + list(w_extra), list(w_extra))

            def scan(h_init, hit, i_list, d, final=None):
                h, ht = h_init, hit
                for n_, i in enumerate(i_list):
                    vcopy(Hbf[:, i, d, :], h, [ht], [hb_t[i][d]])
                    if n_ == len(i_list) - 1 and final is None:
                        break
                    hn, hnt = hw.get()
                    step(hn[:], h, i, d, [ht], [hnt])
                    h, ht = hn[:], hnt
                return h, ht

            zero = sb(C, "zero", [128, 192])
            zt = Tok()
            kb.op("dve", lambda e: e.memset(zero[:], 0.0), w=[zt])
            if not sample:
                for sq_ in range(2):
                    for d in range(2):
                        il = [2 * sq_, 2 * sq_ + 1] if d == 0 else [2 * sq_ + 1, 2 * sq_]
                        h, ht = scan(zero[:], zt, il, d, final=True)
                        kb.dma("sp", nssm_d[l, sq_, d], h, r=[ht])
            else:
                G2 = sb(C, "G2", [128, G2W])
                g2_t = Tok()
                for d in range(2):
                    il = list(range(ntile)) if d == 0 else list(range(ntile - 1, -1, -1))
                    h, ht = hw.get()
                    vcopy(h[:], St[:, il[0], d, :], [st_t[il[0]][d]], [ht])
                    for i in il[1:]:
                        hn, hnt = hw.get()
                        step(hn[:], h[:], i, d, [ht], [hnt])
                        h, ht = hn, hnt
                    vcopy(G2[:, 192 * d:192 * d + 192], h[:], [ht], [g2_t])
                    vcopy(G2[:, 384 + 3 * d:387 + 3 * d], CDm[:, 0, d, :], [cd_t], [g2_t])
                    for i in range(1, ntile):
                        tt(G2[:, 384 + 3 * d:387 + 3 * d], G2[:, 384 + 3 * d:387 + 3 * d], CDm[:, i, d, :], ALU.mult, [cd_t, g2_t], [g2_t])
                kb.dma("sp", gin2, G2[:], r=[g2_t], w=[gin2_t])
                kb.allgather(gin2, gout2, r=[gin2_t], w=[gout2_t])
                GS = sb(C, "GS", [128, 4, G2W])
                H0 = sb(C, "H0", [128, 2, 192])
                gs_t, h0_t = toks(2)
                kb.dma("sp", GS[:], gout2.rearrange("(r p) c -> p r c", p=128), r=[gout2_t], w=[gs_t])
                kb.dma("sp", H0[:], h0_d[l].rearrange("d p c -> p d c"), w=[h0_t])
                for d in range(2):
                    h, ht = hw.get()
                    vcopy(h[:], H0[:, d, :], [h0_t], [ht])
                    for j in (range(4) if d == 0 else range(3, -1, -1)):
                        t1, t1t = hw.get()
                        tt(t1[:].rearrange("p (h d) -> p h d", h=3), h[:].rearrange("p (h d) -> p h d", h=3),
                           GS[:, j, 384 + 3 * d:387 + 3 * d].unsqueeze(2).broadcast_to([128, 3, 64]), ALU.mult, [ht, gs_t], [t1t])
                        tt(t1[:], t1[:], GS[:, j, 192 * d:192 * d + 192], ALU.add, [t1t, gs_t], [t1t])
                        tt(t1[:], t1[:], h[:], ALU.subtract, [t1t, ht], [t1t])
                        hn, hnt = hw.get()
                        stt(hn[:], t1[:], MSK[:, 4 * d + j:4 * d + j + 1], h[:], ALU.mult, ALU.add, [t1t, ht, msk_t], [hnt])
                        h, ht = hn, hnt
                    il = list(range(ntile)) if d == 0 else list(range(ntile - 1, -1, -1))
                    scan(h[:], ht, il, d)

            ckpt("C3")
            ysb = Rot(C, "ysb", [128, 384], F32, 2)
            ynb = Rot(C, "ynb", [128, 384], F32, 2)
            gmb = Rot(C, "gmb", [128, 256], F32, 4)
            scb = Rot(C, "scb", [128, 768], BF16, 2)
            xdb = Rot(C, "xdb", [128, 384], BF16, 2)
            for i in range(ntile):
                cs = slice(acol(i), acol(i) + 128)
                psGs = [psum("gen"), psum("S")]
                for g in range(2):
                    mm(psGs[g][0][:, 0:128], BT[64 * g:64 * g + 64, cs], CT[64 * g:64 * g + 64, cs], True, True, [bt_t], [psGs[g][1]])
                gm = []
                for d in range(2):
                    m_, mt = gmb.get()
                    for g in range(2):
                        tt(m_[:, 128 * g:128 * g + 128], psGs[g][0][:, 0:128], (U_ if d == 0 else UB), ALU.mult, [psGs[g][1], cst_t], [mt])
                    gm.append((m_, mt))
                y, yt = ysb.get()
                for d in range(2):
                    R, Rt = tmpf.get()
                    for h in range(6):
                        kb.op("dve", lambda e: e.tensor_scalar_mul(out=R[:, 128 * h:128 * h + 128], in0=(U_ if d == 0 else UB),
                                                                   scalar1=dA[:, i, 6 * d + h:6 * d + h + 1]), r=[dt_t, cst_t], w=[Rt])
                    Ee, Et = tmpf.get()
                    for hf in range(2):
                        psE, ptE = psum("S")
                        mm(psE[:, 0:384], SL if d == 0 else SLB, R[:, 384 * hf:384 * hf + 384], True, True, [Rt, cst_t], [ptE])
                        act(Ee[:, 384 * hf:384 * hf + 384], psE[:, 0:384], AF.Exp, [ptE], [Et])
                    sc, sct = scb.get()
                    for h in range(6):
                        g = h // 3
                        tt(sc[:, 128 * h:128 * h + 128], Ee[:, 128 * h:128 * h + 128], gm[d][0][:, 128 * g:128 * g + 128], ALU.mult,
                           [Et, gm[d][1]], [sct])
                    xd, xdt_ = xdb.get()
                    tt(xd[:].rearrange("p (h d) -> p h d", h=6), xTM[:, i, :].rearrange("p (h d) -> p h d", h=6),
                       dtTM[:, i, 6 * d:6 * d + 6].unsqueeze(2).broadcast_to([128, 6, 64]), ALU.mult, [xtm_t[i], dt_t], [xdt_])
                    psY, ptY = psum("acc")
                    for h in range(6):
                        mm(psY[:, 64 * h:64 * h + 64], sc[:, 128 * h:128 * h + 128], xd[:, 64 * h:64 * h + 64], True, True, [sct, xdt_], [ptY])
                    psFs = [psum("acc"), psum("gen")]
                    yo, yot = tmpf.get()
                    for g in range(2):
                        mm(psFs[g][0][:, 0:192], CT[64 * g:64 * g + 64, cs], Hbf[64 * g:64 * g + 64, i, d, :], True, True,
                           [bt_t, hb_t[i][d]], [psFs[g][1]])
                    for g in range(2):
                        tt(yo[:, 192 * g:192 * g + 192].rearrange("p (h d) -> p h d", h=3), psFs[g][0][:, 0:192].rearrange("p (h d) -> p h d", h=3),
                           EX[:, i, 6 * d + 3 * g:6 * d + 3 * g + 3].unsqueeze(2).broadcast_to([128, 3, 64]), ALU.mult, [psFs[g][1], ex_t[i]], [yot])
                    if d == 0:
                        tt(y[:], yo[:, 0:384], psY[:, 0:384], ALU.add, [yot, ptY], [yt])
                    else:
                        tt(y[:], y[:], yo[:, 0:384], ALU.add, [yot, yt], [yt])
                        tt(y[:], y[:], psY[:, 0:384], ALU.add, [ptY, yt], [yt])
                xD, xDt = tmpf.get()
                tt(xD[:, 0:384], xTM[:, i, :], BCt[:, 24:408], ALU.mult, [xtm_t[i], bc_t], [xDt])
                tt(y[:], y[:], xD[:, 0:384], ALU.add, [xDt, yt], [yt])
                tt(y[:], y[:], szTM[:, i, :], ALU.mult, [sz_t, yt], [yt])
                ss, sst = sml.get()
                junk, jt = tmpf.get()
                act(junk[:, 0:384], y[:], AF.Square, [yt], [jt, sst], accum_out=ss[:, 0:1])
                act(ss[:, 0:1], ss[:, 0:1], AF.Sqrt, [sst], [sst], scale=1.0 / 384, bias=EPS)
                kb.op("dve", lambda e: e.reciprocal(out=ss[:, 0:1], in_=ss[:, 0:1]), r=[sst], w=[sst])
                yn, ynt = ynb.get()
                stt(yn[:], y[:], ss[:, 0:1], BCt[:, 408:792], ALU.mult, ALU.mult, [yt, sst, bc_t], [ynt])
                psA_, ptA_ = psum("gen")
                for c3 in range(3):
                    kb.op("pe", lambda e: e.transpose(out=psA_[:, c3 * 128:(c3 + 1) * 128],
                                                      in_=yn[:, c3 * 128:(c3 + 1) * 128], identity=IDENT),
                          r=[ynt, cst_t], w=[ptA_])
                vcopy(mixT[:, 5:8, i * 128:(i + 1) * 128], psA_[:, 0:384].rearrange("p (c n) -> p c n", c=3),
                      [ptA_], [mix_t[i]])
            kb.barrier()
            ckpt("C" + ("s" if sample else "p"))
        M.close()
        OPEN.remove(M)

        if sample:
            KS = ExitStack()
            OPEN.append(KS)
            alloc_keys(KS)
            KaT, Va, KbT, Vb = KK["KaT"], KK["Va"], KK["KbT"], KK["Vb"]
            with ExitStack() as Dk:
                ckp = Rot(Dk, "ckp", [128, 512], BF16, 2)
                vstg = Rot(Dk, "vstg", [128, 1024], BF16, 2)
                wk = sb(Dk, "wkvb2", [128, 512], BF16)
                wkV = sb(Dk, "wkvV2", [128, 256], BF16)
                wk_t = Tok()
                kb.dma("pool", wk[:], wkvb_d[l], w=[wk_t])
                for h in range(4):
                    kb.dma("pool", wkV[:, 64 * h:64 * h + 64], wkvb_d[l, :, 128 * h + 64:128 * h + 128], w=[wk_t])
                kb.dma("pool", KaT[:, 0:256], ctxK_d[l], w=[Tok()])
                for h in range(4):
                    kb.dma("pool", KbT[64:96, h, 0:256], ctxP_d[l, 64:96, :], w=[Tok()])
                for t_ in range(2):
                    kb.dma("pool", Va[:, t_, :, 0:64], ctxV_d[l, t_ * 128:(t_ + 1) * 128, :].rearrange("p (g d) -> p g d", g=2), w=[Tok()])
                gv = gv1
                for r_ in range(4):
                    kb.dma("pool", KaT[:, 256 + 1024 * r_:1280 + 1024 * r_], gv[:, r_, 0:1024], r=[gout_t[0]], w=[Tok()])
                    kp_t = Tok()
                    kb.dma("pool", KbT[64:96, 0, 256 + 1024 * r_:1280 + 1024 * r_], gv[64:96, r_, 2048:3072], r=[gout_t[1]], w=[kp_t])
                    for h in range(1, 4):
                        (vcopy if h % 2 else acopy)(KbT[64:96, h, 256 + 1024 * r_:1280 + 1024 * r_], KbT[64:96, 0, 256 + 1024 * r_:1280 + 1024 * r_], [kp_t], [Tok()])
                    vs, vst = vstg.get()
                    kb.dma("pool", vs[:], gv[:, r_, 3072:4096], r=[gout_t[1]], w=[vst])
                    for g in range(2):
                        vcopy(Va[:, 2 + 8 * r_:10 + 8 * r_, g, 0:64], vs[:].rearrange("p (t g d) -> p t g d", t=8, g=2)[:, :, g, :], [vst], [Tok()])
                for pc in range(9):
                    n = 256 if pc == 0 else 512
                    k0 = 0 if pc == 0 else 256 + (pc - 1) * 512
                    cb, cbt = ckp.get()
                    if pc == 0:
                        kb.dma("pool", cb[:, 0:256], ctxC_d[l], w=[cbt])
                    else:
                        r_, hf = (pc - 1) // 2, (pc - 1) % 2
                        kb.dma("pool", cb[:], gv[:, r_, 1024 + 512 * hf:1536 + 512 * hf], r=[gout_t[0]], w=[cbt])
                    for h in range(4):
                        ps2, pt2 = psum("gen")
                        mm(ps2[0:64, 0:n], wk[:, 128 * h:128 * h + 64], cb[:, 0:n], True, True, [wk_t, cbt], [pt2])
                        acopy(KbT[0:64, h, k0:k0 + n], ps2[0:64, 0:n], [pt2], [Tok()])
                    for kt in range(n // 128):
                        ps2, pt2 = psum("gen")
                        mm(ps2[:, 0:256], cb[:, kt * 128:(kt + 1) * 128], wkV[:],
                           True, True, [wk_t, cbt], [pt2])
                        acopy(Vb[:, k0 // 128 + kt, :, 0:64], ps2[:, 0:256].rearrange("p (h d) -> p h d", h=4), [pt2], [Tok()])
                kb.barrier()
                ckpt("Dk")
            attention()
            KS.close()
            OPEN.remove(KS)

        with ExitStack() as Fz:
            Wo = sb(Fz, "Wo", [128, 8, 1024], BF16)
            wo_ts = toks(4)
            wv = wout_d[l].rearrange("(c p) n -> p c n", p=128)
            for c0 in range(0, 8, 2):
                kb.dma("pool", Wo[:, c0:c0 + 2, :], wv[:, c0:c0 + 2, :], w=[wo_ts[c0 // 2]])
            mo = sb(Fz, "mo", [128, 8, 512])
            mo_t = Tok()
            sq = Rot(Fz, "sqf", [128, 512], BF16, 2)
            tmpf = Rot(Fz, "tmpf2", [128, 512], F32, 5)
            rsb = Rot(Fz, "rsb2", [128, 512], F32, 2)
            for ti in range(TG // 512):
                g0 = ti * 512
                for oc in range(8):
                    ps, pt = psum("big")
                    for k in range(8):
                        mm(ps[:, :], Wo[:, k, oc * 128:(oc + 1) * 128], mixT[:, k, g0:g0 + 512], k == 0, k == 7,
                           [wo_ts[k // 2]] + mix_t[ti * 4:ti * 4 + 4], [pt])
                    acopy(mo[:, oc, :], ps[:, :], [pt], [mo_t])
                resid_add(l, 1, mg, t0 + g0, 512, mo, mo_t, sq, tmpf, rsb)
            kb.barrier()
            ckpt("F" + ("s" if sample else "p"))
        L.close()
        OPEN.remove(L)

    def ffn(l):
        with ExitStack() as Gz:
            h2 = sb(Gz, "h2", [128, 8, 768], BF16)
            f1 = sb(Gz, "f1", [128, 32, 768], BF16)
            fo = sb(Gz, "fo", [128, 8, 768])
            W1 = [sb(Gz, f"W1_{i}", [128, 8, 512], BF16) for i in range(2)]
            W2 = [sb(Gz, f"W2_{i}", [128, 32, 128], BF16) for i in range(2)]
            w1_t, w2_t = toks(2), toks(2)
            h2_t, f1_t, fo_t = toks(2), [toks(2) for _ in range(32)], toks(2)
            sq = Rot(Gz, "sqg", [128, 512], BF16, 3)
            tmpf = Rot(Gz, "tmpg", [128, 512], F32, 5)
            rsb = Rot(Gz, "rsg", [128, 512], F32, 3)
            w1v = w1_d[l].rearrange("(c p) n -> p c n", p=128)
            w2v = w2_d[l].rearrange("(c p) n -> p c n", p=128)
            for half in range(2):
                segs = [(0, 512, 0), (512, 256, 1)] if half == 0 else [(1024, 512, 1), (768, 256, 1)]
                hoff = {s[0]: o for s, o in zip(segs, (0, segs[0][1]))}
                kb.dma("pool", W1[0][:], w1v[:, :, 0:512], w=[w1_t[0]])
                for si, (a0, n, mg) in enumerate(segs):
                    o = hoff[a0]
                    norm_mod(l, 2, 24, mg, a0, n, h2[:, :, o:o + n], h2_t[si], sq, tmpf, rsb)
                for pc in range(8):
                    b = pc % 2
                    if pc + 1 < 8:
                        kb.dma("pool", W1[1 - b][:], w1v[:, :, (pc + 1) * 512:(pc + 2) * 512], w=[w1_t[1 - b]])
                    else:
                        kb.dma("pool", W2[0][:], w2v[:, :, 0:128], w=[w2_t[0]])
                    for j in range(4):
                        fc = pc * 4 + j
                        for si, (a0, n, mg) in enumerate(segs):
                            o = hoff[a0]
                            ps, pt = psum("big")
                            for k in range(8):
                                mm(ps[:, :n], W1[b][:, k, j * 128:(j + 1) * 128], h2[:, k, o:o + n], k == 0, k == 7,
                                   [w1_t[b], h2_t[si]], [pt])
                            tf, tft = tmpf.get()
                            act(tf[:, :n], ps[:, :n], AF.Relu, [pt], [tft])
                            tt(f1[:, fc, o:o + n], tf[:, :n], tf[:, :n], ALU.mult, [tft], [f1_t[fc][si]])
                for oc in range(8):
                    b = oc % 2
                    if oc + 1 < 8:
                        kb.dma("pool", W2[1 - b][:], w2v[:, :, (oc + 1) * 128:(oc + 2) * 128], w=[w2_t[1 - b]])
                    for si, (a0, n, mg) in enumerate(segs):
                        o = hoff[a0]
                        ps, pt = psum("big")
                        for k in range(32):
                            mm(ps[:, :n], W2[b][:, k, :], f1[:, k, o:o + n], k == 0, k == 31, [w2_t[b], f1_t[k][si]], [pt])
                        acopy(fo[:, oc, o:o + n], ps[:, :n], [pt], [fo_t[si]])
                for si, (a0, n, mg) in enumerate(segs):
                    o = hoff[a0]
                    resid_add(l, 3, mg, a0, n, fo[:, :, o:o + n], fo_t[si], sq, tmpf, rsb)
            kb.barrier()

    try:
        for l in range(depth):
            mixer_pass(l, False)
            mixer_pass(l, True)
            ffn(l)
    except StopBuild:
        if DEBUG.get("padact"):
            pt_ = Tok()
            for _ in range(DEBUG["padact"]):
                kb.op(DEBUG.get("padeng_name", "act"), lambda e: (e.activation(out=MSK[:, 0:1], in_=MSK[:, 0:1], func=AF.Copy) if DEBUG.get("padeng_name", "act") == "act" else e.tensor_copy(out=MSK[:, 0:1], in_=MSK[:, 0:1])), r=[pt_], w=[])
        kb.barrier()
        return nc, kb

    for c in range(8):
        kb.dma("sp", yT_d[c * 128:(c + 1) * 128, :], xT[:, c, :], r=x_t[c])
    kb.barrier(final=True)
    es.close()
    return nc, kb


def _consts():
    k = np.arange(128)
    ident = np.eye(128, dtype=np.float32)
    U = (k[:, None] <= k[None, :]).astype(np.float32)
    Ub = (k[:, None] >= k[None, :]).astype(np.float32)
    SLm = (k[:, None] > k[None, :]).astype(np.float32)
    SLb = (k[:, None] < k[None, :]).astype(np.float32)
    ones = np.ones((128, 128), np.float32)

    def perm(blk):
        q = blk // 2
        Pm = np.zeros((128, 128), np.float32)
        for b in range(128 // blk):
            for i in range(blk):
                o = b * blk + i
                if i < q:
                    Pm[o + q, o] = -1.0
                else:
                    Pm[o - q, o] = 1.0
        return Pm
    blkm = np.zeros((128, 128), np.float32)
    blkm[:64, :64] = 1
    blkm[64:, 64:] = 1
    return np.concatenate([ident, U, Ub, SLm, SLb, ones, perm(32), perm(16), blkm], axis=1)


def _rope_tables(q):
    t = (1024 * q + np.arange(1024))
    row = (t // 64).astype(np.float32)
    col = (t % 64).astype(np.float32)
    out = []
    for rot_dim, reps in ((64, 2), (32, 4)):
        half = rot_dim // 2
        inv = (1.0 / (np.float32(10000.0) ** (np.arange(0, half, 2, dtype=np.float32) / np.float32(half)))).astype(np.float32)
        angr = row[:, None] * inv[None, :]
        angc = col[:, None] * inv[None, :]
        ang = np.concatenate([angr, angr, angc, angc], axis=1).astype(np.float32)
        cos = np.tile(np.cos(ang).astype(np.float32).T, (reps, 1))
        sin = np.tile(np.sin(ang).astype(np.float32).T, (reps, 1))
        out += [cos, sin]
    return np.ascontiguousarray(np.stack(out, 0).astype(np.float32))


def _pl(v, nch):
    return np.asarray(v, np.float32).reshape(nch, 128).T


def kernel(**inp):
    inp = {k: np.asarray(v) for k, v in inp.items()}
    f = lambda a: np.ascontiguousarray(np.asarray(a, dtype=np.float32))
    nc, kb = build_program(RUN_DEPTH)
    cst = f(_consts())
    vec = np.zeros((128, DEPTH, NV), np.float32)
    bc = np.zeros((DEPTH, 128, NBC), np.float32)
    for l in range(DEPTH):
        vec[:, l, 0:8] = _pl(inp["norm_mix_pre"][l], 8)
        vec[:, l, 8:16] = _pl(inp["norm_mix_post"][l], 8)
        vec[:, l, 16:24] = _pl(inp["norm_ffn_pre"][l], 8)
        vec[:, l, 24:32] = _pl(inp["norm_ffn_post"][l], 8)
        vec[:, l, 32:80] = _pl(inp["b_mod"][l], 48)
        vec[:, l, 80] = np.tile(inp["attn_q_norm"][l], 2)
        vec[:, l, 81] = np.tile(inp["attn_k_norm"][l], 2)
        vec[:, l, 82:84] = _pl(inp["mla_q_norm"][l], 2)
        vec[:, l, 84] = inp["mla_kv_norm"][l]
        vec[:, l, 85:110] = inp["ssm_conv_w"][l].reshape(5, 128, 5).transpose(1, 0, 2).reshape(128, 25)
        vec[:, l, 110:115] = _pl(inp["ssm_conv_b"][l], 5)
        row = np.concatenate([inp["ssm_dt_bias"][l].reshape(12), inp["ssm_a_log"][l].reshape(12),
                              np.repeat(inp["ssm_d"][l], 64), inp["ssm_norm"][l]]).astype(np.float32)
        bc[l] = row[None, :]
    vec = f(vec.reshape(128, DEPTH * NV))
    shared = dict(vec=vec, bc=f(bc), cst=cst, w_in=f(inp["w_in"][:RUN_DEPTH]), w_qb=f(inp["mla_w_qb"][:RUN_DEPTH]),
                  w_kvb=f(inp["mla_w_kvb"][:RUN_DEPTH]), w_out=f(inp["w_out"][:RUN_DEPTH]), w_ffn1=f(inp["w_ffn1"][:RUN_DEPTH]),
                  w_ffn2=f(inp["w_ffn2"][:RUN_DEPTH]))
    in_maps = []
    for r in range(8):
        s, q = r // 4, r % 4
        xT = np.concatenate([inp["x_prompt"][2 * r], inp["x_prompt"][2 * r + 1],
                             inp["x_sample"][s, 1024 * q:1024 * q + 1024]], axis=0).T
        cT = np.zeros((128, 8, 4), np.float32)
        cT[:, :, 0] = _pl(inp["c_ctx"], 8)
        cT[:, :, 1] = _pl(inp["c"][0], 8)
        cT[:, :, 2] = _pl(inp["c"][1], 8)
        cT[:, :, 3] = _pl(inp["c_ctx"], 8)
        msk = np.zeros((128, 18), np.float32)
        msk[:, 16 + s] = 1.0
        for j in range(4):
            msk[:, j] = 1.0 if j < q else 0.0
            msk[:, 4 + j] = 1.0 if j > q else 0.0
            msk[:, 8 + j] = 1.0 if j == q - 1 else 0.0
            msk[:, 12 + j] = 1.0 if j == q + 1 else 0.0
        ck = inp["cache_attn_k"][s].reshape(DEPTH, 256, 128).transpose(0, 2, 1)
        cv = inp["cache_attn_v"][s].reshape(DEPTH, 256, 128)
        cc = inp["cache_mla_ckv"][s].transpose(0, 2, 1)
        cp = np.tile(inp["cache_mla_kpe"][s].transpose(0, 2, 1), (1, 4, 1))
        h0 = inp["state_ssm"][s].reshape(DEPTH, 2, 2, 3, 64, 64).transpose(0, 1, 2, 5, 3, 4).reshape(DEPTH, 2, 128, 192)
        m = dict(shared)
        m.update(xT=f(xT), cT=f(cT.reshape(128, 32)), w_mod=f(inp["w_mod"][:RUN_DEPTH, :, 1536 * q:1536 * q + 1536]), rope=_rope_tables(q), msk=f(msk), ctxK=f(ck), ctxV=f(cv),
                 ctxC=f(cc), ctxP=f(cp), h0=f(h0))
        in_maps.append(m)
    res = run_bass_kernel_spmd(nc, in_maps, core_ids=list(range(8)))
    R = res.results
    yp = np.zeros((16, 256, D), np.float32)
    ys = np.zeros((2, 4096, D), np.float32)
    nk = np.zeros((16, DEPTH, 256, 2, 64), np.float32)
    nv = np.zeros((16, DEPTH, 256, 2, 64), np.float32)
    nckv = np.zeros((16, DEPTH, 256, 128), np.float32)
    nkpe = np.zeros((16, DEPTH, 256, 32), np.float32)
    nssm = np.zeros((16, DEPTH, 2, 6, 64, 64), np.float32)
    for r in range(8):
        s, q = r // 4, r % 4
        yT = np.asarray(R[r]["yT"])
        for j in range(2):
            b = 2 * r + j
            yp[b] = yT[:, 256 * j:256 * j + 256].T
            for l in range(DEPTH):
                nk[b, l] = np.asarray(R[r]["nkT"])[l][:, 256 * j:256 * j + 256].T.reshape(256, 2, 64)
                nv[b, l] = np.asarray(R[r]["nv"])[l][256 * j:256 * j + 256, :].reshape(256, 2, 64)
                nckv[b, l] = np.asarray(R[r]["nckvT"])[l][:, 256 * j:256 * j + 256].T
                nkpe[b, l] = np.asarray(R[r]["nkpeT"])[l][:, 256 * j:256 * j + 256].T
                for d in range(2):
                    a = np.asarray(R[r]["nssm"])[l, j, d]
                    nssm[b, l, d] = a.reshape(2, 64, 3, 64).transpose(0, 2, 3, 1).reshape(6, 64, 64)
        ys[s, 1024 * q:1024 * q + 1024] = yT[:, 512:].T
    return yp, ys, nk, nv, nckv, nkpe, nssm
```
